# Optimizing a Trainium2 kernel written in Bass

```python
import jax, jax.numpy as jnp
from jax import lax
import numpy as np

D_MODEL = 1024
BATCH = 2
SEQ = 8192
DEPTH = 1
DEC_BATCH = 16
DEC_SEQ = 2048
PAST_LEN = 128

HEAD_DIM = 64
RWKV_HEADS = 8
RWKV_W = RWKV_HEADS * HEAD_DIM
ATT_Q_HEADS = 8
ATT_KV_HEADS = 2
ATT_GROUP = ATT_Q_HEADS // ATT_KV_HEADS
ATT_W = ATT_Q_HEADS * HEAD_DIM
ATT_KV_W = ATT_KV_HEADS * HEAD_DIM
N_DIRS = 2
D_DECAY_LORA = 64
D_AAA_LORA = 64
D_GATE_LORA = 128
D_FF = 2816
GRID_W = 64
ROPE_THETA = 10000.0
ROPE_AXIS_DIM = HEAD_DIM // 2
Q_BLOCK = 128
NORM_EPS = 1e-6
LNX_EPS = 64e-5
L2_EPS = 1e-12

RWKV_SIZES = [RWKV_W, RWKV_W, RWKV_W, N_DIRS * D_DECAY_LORA, N_DIRS * D_AAA_LORA, D_GATE_LORA]
RWKV_COLS = sum(RWKV_SIZES)
ATT_SIZES = [ATT_W, ATT_KV_W, ATT_KV_W]
ATT_COLS = sum(ATT_SIZES)
GATE_COLS = 2 * D_MODEL
N_IN_COLS = RWKV_COLS + ATT_COLS + GATE_COLS

kernel_name = "hybrid_rwkv7_axial_gqa_encoder"


def _split(z, sizes):
    idx = np.cumsum(sizes)[:-1].tolist()
    return jnp.split(z, idx, axis=-1)


def rms_norm(x, g, eps=NORM_EPS):
    xf = x.astype(jnp.float32)
    y = xf * lax.rsqrt(jnp.mean(xf * xf, axis=-1, keepdims=True) + eps)
    return (y * g.astype(jnp.float32)).astype(x.dtype)


def swiglu(h, w_gate, w_up, w_down):
    return (jax.nn.silu(h @ w_gate) * (h @ w_up)) @ w_down


def centred_shift(p):
    prev = jnp.pad(p[:, :-1], ((0, 0), (1, 0), (0, 0)))
    nxt = jnp.pad(p[:, 1:], ((0, 0), (0, 1), (0, 0)))
    return 0.5 * (prev + nxt)


def _dir_major(z):
    B, T = z.shape[0], z.shape[1]
    z = jnp.stack([z[:, :, 0], z[:, ::-1, 1]], axis=0)
    z = z.reshape(N_DIRS, B, T, RWKV_HEADS, HEAD_DIM)
    return jnp.transpose(z, (2, 0, 1, 3, 4))


def _rwkv7_step(S, inp):
    r, w, k, v, kk, b = inp
    sa = jnp.einsum('dbhij,dbhj->dbhi', S, -kk)
    S = S * w[..., None, :] + sa[..., :, None] * b[..., None, :] + v[..., :, None] * k[..., None, :]
    y = jnp.einsum('dbhij,dbhj->dbhi', S, r)
    return S, y


def rwkv7_bidir(p_rw, mu, w0, w_up, a0, a_up, g_up, k_k, k_a, r_k, lnx_w, lnx_b):
    f32 = jnp.float32
    B, T, _ = p_rw.shape
    xs = p_rw + (centred_shift(p_rw) - p_rw) * mu
    r, k, v, wd, ad, gd = _split(xs, RWKV_SIZES)
    wd = wd.reshape(B, T, N_DIRS, D_DECAY_LORA)
    ad = ad.reshape(B, T, N_DIRS, D_AAA_LORA)
    w_log = -jax.nn.softplus(-(w0 + jnp.einsum('btdr,drc->btdc', jnp.tanh(wd), w_up)).astype(f32)) - 0.5
    decay = jnp.exp(-jnp.exp(w_log))
    a = jax.nn.sigmoid((a0 + jnp.einsum('btdr,drc->btdc', ad, a_up)).astype(f32))
    g = (jax.nn.sigmoid(gd) @ g_up).astype(f32)
    kk = (k * k_k).astype(f32).reshape(B, T, RWKV_HEADS, HEAD_DIM)
    kk = kk / jnp.maximum(jnp.linalg.norm(kk, axis=-1, keepdims=True), L2_EPS)
    kk = kk.reshape(B, T, 1, RWKV_W)
    r32 = r.astype(f32)[:, :, None]
    v32 = v.astype(f32)[:, :, None]
    k_dir = k.astype(f32)[:, :, None] * (1.0 + (a - 1.0) * k_a.astype(f32))
    b_dir = kk * a
    ones = jnp.ones((1, 1, N_DIRS, 1), f32)
    seq_in = (_dir_major(r32 * ones), _dir_major(decay), _dir_major(k_dir),
              _dir_major(v32 * ones), _dir_major(kk * ones), _dir_major(b_dir))
    S0 = jnp.zeros((N_DIRS, B, RWKV_HEADS, HEAD_DIM, HEAD_DIM), f32)
    _, y = lax.scan(_rwkv7_step, S0, seq_in)
    y = jnp.transpose(y, (1, 2, 0, 3, 4))
    y = y[0] + y[1][:, ::-1]
    mean = jnp.mean(y, axis=-1, keepdims=True)
    var = jnp.mean(jnp.square(y - mean), axis=-1, keepdims=True)
    y = ((y - mean) * lax.rsqrt(var + LNX_EPS)).reshape(B, T, RWKV_W)
    y = y * lnx_w.astype(f32) + lnx_b.astype(f32)
    coef = jnp.sum((r32 * k_dir).reshape(B, T, N_DIRS, RWKV_HEADS, HEAD_DIM) * r_k.astype(f32), axis=(2, 4))
    bonus = (coef[..., None] * v.astype(f32).reshape(B, T, RWKV_HEADS, HEAD_DIM)).reshape(B, T, RWKV_W)
    return ((y + bonus) * g).astype(p_rw.dtype)


def axial_rope_tables(T):
    rows = T // GRID_W
    r_idx, c_idx = jnp.meshgrid(jnp.arange(rows, dtype=jnp.float32),
                                jnp.arange(GRID_W, dtype=jnp.float32), indexing='ij')
    r_idx = r_idx.reshape(T)
    c_idx = c_idx.reshape(T)
    inv_freq = ROPE_THETA ** (-jnp.arange(0, ROPE_AXIS_DIM, 2, dtype=jnp.float32) / ROPE_AXIS_DIM)
    ang = jnp.concatenate([r_idx[:, None] * inv_freq, c_idx[:, None] * inv_freq], axis=-1)
    return jnp.cos(ang), jnp.sin(ang)


def apply_rope(x, cos, sin):
    B, T, H, N = x.shape
    xf = x.astype(jnp.float32).reshape(B, T, H, N // 2, 2)
    x0, x1 = xf[..., 0], xf[..., 1]
    c = cos[None, :, None, :]
    s = sin[None, :, None, :]
    out = jnp.stack([x0 * c - x1 * s, x0 * s + x1 * c], axis=-1)
    return out.reshape(B, T, H, N).astype(x.dtype)


def axial_gqa(p_att, qk_g):
    B, T, _ = p_att.shape
    q, k, v = _split(p_att, ATT_SIZES)
    cos, sin = axial_rope_tables(T)
    q = apply_rope(rms_norm(q.reshape(B, T, ATT_Q_HEADS, HEAD_DIM), qk_g[0]), cos, sin)
    k = apply_rope(rms_norm(k.reshape(B, T, ATT_KV_HEADS, HEAD_DIM), qk_g[1]), cos, sin)
    v = v.reshape(B, T, ATT_KV_HEADS, HEAD_DIM)
    n_blk = T // Q_BLOCK
    qb = q.reshape(B, n_blk, Q_BLOCK, ATT_KV_HEADS, ATT_GROUP, HEAD_DIM)
    qb = jnp.transpose(qb, (1, 0, 3, 4, 2, 5))
    kt = jnp.transpose(k, (0, 2, 1, 3))
    vt = jnp.transpose(v, (0, 2, 1, 3))
    scale = HEAD_DIM ** -0.5

    def block(q_blk):
        s = jnp.einsum('bhgqd,bhkd->bhgqk', q_blk, kt).astype(jnp.float32) * scale
        p = jax.nn.softmax(s, axis=-1)
        return jnp.einsum('bhgqk,bhkd->bhgqd', p.astype(vt.dtype), vt)

    o = lax.map(block, qb)
    o = jnp.transpose(o, (1, 0, 4, 2, 3, 5))
    return o.reshape(B, T, ATT_W)


def encoder_layer(x, norm_g, ffn_w_gate, ffn_w_up, ffn_w_down, w_in, mu_shift, w0, w_lora_up,
                  a0, a_lora_up, g_lora_up, k_k, k_a, r_k, lnx_w, lnx_b, qk_norm_g,
                  w_branch_a, w_branch_b, b_gate, w_out):
    h = rms_norm(x, norm_g[0])
    x = x + 0.5 * rms_norm(swiglu(h, ffn_w_gate[0], ffn_w_up[0], ffn_w_down[0]), norm_g[1])
    h = rms_norm(x, norm_g[2])
    p = h @ w_in
    p_rw, p_att, p_gate = _split(p, [RWKV_COLS, ATT_COLS, GATE_COLS])
    y_a = rwkv7_bidir(p_rw, mu_shift, w0, w_lora_up, a0, a_lora_up, g_lora_up,
                      k_k, k_a, r_k, lnx_w, lnx_b)
    y_b = axial_gqa(p_att, qk_norm_g)
    g_a, g_b = _split(jax.nn.sigmoid(p_gate + b_gate.reshape(GATE_COLS)), [D_MODEL, D_MODEL])
    merged = g_a * (y_a @ w_branch_a) + g_b * (y_b @ w_branch_b)
    x = x + rms_norm(merged @ w_out, norm_g[3])
    h = rms_norm(x, norm_g[4])
    x = x + 0.5 * rms_norm(swiglu(h, ffn_w_gate[1], ffn_w_up[1], ffn_w_down[1]), norm_g[5])
    return x


def setup_inputs(seed: int = 0) -> dict:
    key = jax.random.key(seed)
    ks = jax.random.split(key, 24)
    f32 = jnp.float32
    L = DEPTH
    nrm = lambda k, shape, s: jax.random.normal(k, shape, f32) * s
    return {
        "x_prompt": jax.random.normal(ks[0], (BATCH, SEQ, D_MODEL), f32),
        "x_sample": jax.random.normal(ks[1], (DEC_BATCH, DEC_SEQ, D_MODEL), f32),
        "norm_g": 1.0 + nrm(ks[2], (L, 6, D_MODEL), 0.05),
        "ffn_w_gate": nrm(ks[3], (L, 2, D_MODEL, D_FF), D_MODEL ** -0.5),
        "ffn_w_up": nrm(ks[4], (L, 2, D_MODEL, D_FF), D_MODEL ** -0.5),
        "ffn_w_down": nrm(ks[5], (L, 2, D_FF, D_MODEL), D_FF ** -0.5),
        "w_in": nrm(ks[6], (L, D_MODEL, N_IN_COLS), D_MODEL ** -0.5),
        "mu_shift": jax.random.uniform(ks[7], (L, RWKV_COLS), f32, 0.1, 0.9),
        "w0": nrm(ks[8], (L, N_DIRS, RWKV_W), 0.5),
        "w_lora_up": nrm(ks[9], (L, N_DIRS, D_DECAY_LORA, RWKV_W), 0.1 * D_DECAY_LORA ** -0.5),
        "a0": nrm(ks[10], (L, N_DIRS, RWKV_W), 0.5),
        "a_lora_up": nrm(ks[11], (L, N_DIRS, D_AAA_LORA, RWKV_W), 0.1 * D_AAA_LORA ** -0.5),
        "g_lora_up": nrm(ks[12], (L, D_GATE_LORA, RWKV_W), D_GATE_LORA ** -0.5),
        "k_k": 0.85 + nrm(ks[13], (L, RWKV_W), 0.05),
        "k_a": 1.0 + nrm(ks[14], (L, RWKV_W), 0.05),
        "r_k": nrm(ks[15], (L, RWKV_HEADS, HEAD_DIM), 0.1),
        "lnx_w": 1.0 + nrm(ks[16], (L, RWKV_W), 0.05),
        "lnx_b": nrm(ks[17], (L, RWKV_W), 0.01),
        "qk_norm_g": 1.0 + nrm(ks[18], (L, 2, HEAD_DIM), 0.05),
        "w_branch_a": nrm(ks[19], (L, RWKV_W, D_MODEL), RWKV_W ** -0.5),
        "w_branch_b": nrm(ks[20], (L, ATT_W, D_MODEL), ATT_W ** -0.5),
        "b_gate": nrm(ks[21], (L, 2, D_MODEL), 0.01),
        "w_out": nrm(ks[22], (L, D_MODEL, D_MODEL), D_MODEL ** -0.5),
    }


def reference(x_prompt, x_sample, norm_g, ffn_w_gate, ffn_w_up, ffn_w_down, w_in, mu_shift,
              w0, w_lora_up, a0, a_lora_up, g_lora_up, k_k, k_a, r_k, lnx_w, lnx_b,
              qk_norm_g, w_branch_a, w_branch_b, b_gate, w_out):
    def trunk(x):
        for l in range(DEPTH):
            x = encoder_layer(x, norm_g[l], ffn_w_gate[l], ffn_w_up[l], ffn_w_down[l], w_in[l],
                              mu_shift[l], w0[l], w_lora_up[l], a0[l], a_lora_up[l], g_lora_up[l],
                              k_k[l], k_a[l], r_k[l], lnx_w[l], lnx_b[l], qk_norm_g[l],
                              w_branch_a[l], w_branch_b[l], b_gate[l], w_out[l])
        return x

    y_prompt = trunk(x_prompt)
    y_sample = trunk(x_sample)
    return (y_prompt, y_sample)
```

```python
import contextlib
import numpy as np
import ml_dtypes
import concourse.bass as bass
import concourse.mybir as mybir
from concourse.bass_utils import run_bass_kernel_spmd

F32 = mybir.dt.float32
BF16 = mybir.dt.bfloat16
AF = mybir.ActivationFunctionType
ALU = mybir.AluOpType
AX = mybir.AxisListType

D = 1024
KC = 8
FF = 2816
FC = 22
NIN = 4736
RW = 512
NRW = 1920
EPS = 1e-6
LNX_EPS = 64e-5
CH = 64
NS_CAP = 1024
NS_CAP_B = 512


class Sched:
    ENGS = ('pe', 'act', 'dve', 'pool', 'sp')
    NLANES = 8

    def __init__(self, nc, st):
        self.nc = nc
        self.sems = {}
        self.st = st
        self.cnt = {e: 0 for e in self.ENGS}
        self.lane_cnt = {}
        self.lane_rr = {e: 0 for e in self.ENGS}
        self.nflush = 0
        self._reset()

    def _reset(self):
        self.q = {e: [] for e in self.ENGS}
        self.lw = {}
        self.rd = {}
        self.seen = {e: {} for e in self.ENGS}
        self.signaled = {e: set() for e in self.ENGS}

    def _sem(self, name):
        if name not in self.sems:
            self.sems[name] = self.st.enter_context(self.nc.semaphore(name))
        return self.sems[name]

    def _deps(self, reads, writes):
        toks = set()
        for k in reads:
            t = self.lw.get(k)
            if t is not None:
                toks.add(t)
        for k in writes:
            t = self.lw.get(k)
            if t is not None:
                toks.add(t)
            for r in self.rd.get(k, {}).values():
                toks.add(r)
        return toks

    def _commit(self, tok, reads, writes):
        src = tok[1]
        for k in reads:
            self.rd.setdefault(k, {})[src] = tok
        for k in writes:
            self.lw[k] = tok
            self.rd[k] = {}

    def _filter(self, eng, toks):
        best = {}
        for t in toks:
            kind, src, n = t
            if kind == 'c' and src == 'pe' and eng == 'pe':
                continue
            if self.seen[eng].get(src, 0) >= n:
                continue
            if src not in best or best[src][2] < n:
                best[src] = t
        for t in best.values():
            self.seen[eng][t[1]] = t[2]
            if t[0] == 'c':
                self.signaled[t[1]].add(t[2])
        return list(best.values())

    def op(self, eng, fn, reads=(), writes=()):
        toks = self._deps(reads, writes)
        waits = self._filter(eng, toks)
        idx = len(self.q[eng]) + 1
        self.q[eng].append(dict(fn=fn, waits=waits, idx=idx, dma=None))
        self._commit(('c', eng, idx), reads, writes)

    def dma(self, out, in_, reads=(), writes=(), queue='sp', **kw):
        toks = self._deps(reads, writes)
        li = self.lane_rr[queue]
        self.lane_rr[queue] = (li + 1) % self.NLANES
        lane = 'L_%s_%d' % (queue, li)
        c = self.lane_cnt.get(lane, 0)
        if c > 0:
            toks.add(('d', lane, 16 * c))
        self.lane_cnt[lane] = c + 1
        waits = self._filter(queue, toks)
        idx = len(self.q[queue]) + 1
        self.q[queue].append(dict(fn=None, waits=waits, idx=idx, dma=(out, in_, lane, kw)))
        self._commit(('d', lane, 16 * (c + 1)), reads, writes)

    def coll(self, kind, ins, outs, groups, reads=(), writes=()):
        queue = 'pool'
        toks = self._deps(reads, writes)
        lane = 'L_coll'
        c = self.lane_cnt.get(lane, 0)
        if c > 0:
            toks.add(('d', lane, 16 * c))
        self.lane_cnt[lane] = c + 1
        waits = self._filter(queue, toks)
        idx = len(self.q[queue]) + 1
        fn = lambda e: e.collective_compute(kind, ALU.bypass, replica_groups=groups, ins=ins, outs=outs)
        self.q[queue].append(dict(fn=None, waits=waits, idx=idx, dma=(fn, None, lane, None)))
        self._commit(('d', lane, 16 * (c + 1)), reads, writes)

    def flush(self):
        nc = self.nc
        toks = set(('d', lane, 16 * c) for lane, c in self.lane_cnt.items())
        waits = self._filter('sp', toks)
        self.q['sp'].append(dict(fn=None, waits=waits, idx=len(self.q['sp']) + 1, dma=None))
        cmap = {}
        for e in self.ENGS:
            m = {}
            c = self.cnt[e]
            for i in sorted(self.signaled[e]):
                c += 1
                m[i] = c
            cmap[e] = m
            self.cnt[e] = c
            self._sem('s_' + e)
        for lane in self.lane_cnt:
            self._sem(lane)
        sems = self.sems
        engobj = {'pe': 'tensor', 'act': 'scalar', 'dve': 'vector', 'pool': 'gpsimd', 'sp': 'sync'}
        q = self.q

        def make(e):
            def body(eng):
                for o in q[e]:
                    for (kind, src, n) in o['waits']:
                        if kind == 'c':
                            eng.wait_ge(sems['s_' + src], cmap[src][n])
                        else:
                            eng.wait_ge(sems[src], n)
                    if o['dma'] is not None:
                        out, in_, lane, kw = o['dma']
                        if kw is None:
                            out(eng).then_inc(sems[lane], 16)
                        else:
                            if callable(out):
                                out = out(eng)
                            if callable(in_):
                                in_ = in_(eng)
                            try:
                                eng.dma_start(out=out, in_=in_, **kw).then_inc(sems[lane], 16)
                            except Exception:
                                print("DMA FAIL", e, lane, getattr(out, 'shape', None), getattr(in_, 'shape', None), out, in_)
                                raise
                    elif o['fn'] is not None:
                        ins = o['fn'](eng)
                        if o['idx'] in cmap[e]:
                            ins.then_inc(sems['s_' + e], 1)
            return body

        with nc.Block() as block:
            for e in self.ENGS:
                if q[e]:
                    getattr(block, engobj[e])(make(e))
        self.nflush += 1
        self._reset()


def _consts():
    c = {}
    c['ident'] = np.eye(128, dtype=np.float32)
    s = np.arange(64)[:, None]
    t = np.arange(64)[None, :]
    su = (s < t).astype(np.float32)
    iu = (s <= t).astype(np.float32)
    mf = np.block([[su, iu], [su, iu]])
    sl = (s > t).astype(np.float32)
    il = (s >= t).astype(np.float32)
    mb = np.block([[sl, il], [sl, il]])
    c['maskf'] = mf
    c['maskb'] = mb
    nt = np.zeros((2, 128, 64), np.float32)
    nt[0, :64] = (t.T > s.T).astype(np.float32)
    nt[1, :64] = (t.T < s.T).astype(np.float32)
    c['masknt'] = nt
    bo = np.zeros((128, 128), np.float32)
    bo[:64, :64] = 1.0
    bo[64:, 64:] = 1.0
    c['blockones'] = bo
    hs = np.zeros((128, 2), np.float32)
    hs[:64, 0] = 1.0
    hs[64:, 1] = 1.0
    c['headsel'] = hs
    return c


def _rope_tables(T):
    rows = T // 64
    r_idx = np.repeat(np.arange(rows, dtype=np.float32), 64)
    c_idx = np.tile(np.arange(64, dtype=np.float32), rows)
    inv = (10000.0 ** (-np.arange(0, 32, 2, dtype=np.float32) / 32)).astype(np.float32)
    ang = np.concatenate([r_idx[:, None] * inv, c_idx[:, None] * inv], axis=-1)
    return np.cos(ang).astype(np.float32), np.sin(ang).astype(np.float32)


def build_nc(seq_lens, debug=None, quarter=()):
    nc = bass.Bass("TRN2", target_bir_lowering=False)
    TMAX = max(seq_lens)
    NSEQ = len(seq_lens)
    TT = sum(seq_lens)

    def din(name, shape, dt=F32):
        return nc.dram_tensor(name, list(shape), dt, kind="ExternalInput").ap()

    def dscr(name, shape, dt):
        return nc.dram_tensor(name, list(shape), dt, kind="Internal").ap()

    xs_in = [din("x%d" % i, [T, D]) for i, T in enumerate(seq_lens)]
    ys_out = [nc.dram_tensor("y%d" % i, [T // 4 if i in quarter else T, D], F32, kind="ExternalOutput").ap()
              for i, T in enumerate(seq_lens)]
    cos_in = {T: din("cos%d" % T, [T, 32]) for T in sorted(set(seq_lens))}
    sin_in = {T: din("sin%d" % T, [T, 32]) for T in sorted(set(seq_lens))}
    norm_g = din("norm_g", [6, D])
    w_gate = din("ffn_w_gate", [2, D, FF])
    w_up = din("ffn_w_up", [2, D, FF])
    w_down = din("ffn_w_down", [2, FF, D])
    w_in = din("w_in", [D, NIN])
    mu = din("mu_shift", [NRW])
    w0 = din("w0", [2, RW])
    wl_up = din("w_lora_up", [2, 64, RW])
    a0 = din("a0", [2, RW])
    al_up = din("a_lora_up", [2, 64, RW])
    gl_up = din("g_lora_up", [128, RW])
    k_k = din("k_k", [RW])
    k_a = din("k_a", [RW])
    r_k = din("r_k", [RW])
    lnx_w = din("lnx_w", [RW])
    lnx_b = din("lnx_b", [RW])
    qk_g = din("qk_norm_g", [2, 64])
    w_ba = din("w_branch_a", [RW, D])
    w_bb = din("w_branch_b", [RW, D])
    b_gate = din("b_gate", [2 * D])
    w_out = din("w_out", [D, D])
    C = _consts()
    cin = {k: din("c_" + k, v.shape) for k, v in C.items()}

    dbg = {}
    if debug:
        for name, shape, dt in debug:
            dbg[name] = nc.dram_tensor("dbg_" + name, list(shape), dt, kind="ExternalOutput").ap()

    WG = [dscr("WG%d" % l, [FC, 128, KC, 128], BF16) for l in range(2)]
    WU = [dscr("WU%d" % l, [FC, 128, KC, 128], BF16) for l in range(2)]
    WD = [dscr("WD%d" % l, [FC, 128, D], BF16) for l in range(2)]
    WIN1 = dscr("WIN1", [KC, 128, NIN], BF16)
    WIN2 = dscr("WIN2", [KC, 128, NRW], BF16)
    WBA = dscr("WBA", [4, 128, D], BF16)
    WBB = dscr("WBB", [4, 128, D], BF16)
    WOUT = dscr("WOUT", [KC, 128, D], BF16)
    X1 = dscr("X1", [TMAX, D], F32)
    H2 = dscr("H2", [128, KC, TMAX + 2], BF16)
    R_T = dscr("R_T", [4, 128, TMAX], F32)
    K_T = dscr("K_T", [4, 128, TMAX], F32)
    LW = dscr("LW", [2, 4, 128, TMAX], F32)
    AA = dscr("AA", [2, 4, 128, TMAX], F32)
    V_TM = dscr("V_TM", [TMAX, RW], BF16)
    VF_TM = dscr("VF_TM", [TMAX, RW], F32)
    G_TM = dscr("G_TM", [TMAX, RW], F32)
    QT = dscr("QT", [4, 128, TMAX], BF16)
    KTA = dscr("KTA", [128, TMAX], BF16)
    VA = dscr("VA", [TMAX, 128], BF16)
    YD = [dscr("YD%d" % d, [TMAX, RW], F32) for d in range(2)]
    COEF = [dscr("COEF%d" % d, [TMAX, 8], F32) for d in range(2)]
    YB = dscr("YB", [TMAX, RW], BF16)
    COEFS16 = dscr("COEFS16", [TMAX, 16], F32)
    TQ = max([seq_lens[i] // 4 for i in quarter] + [128])
    QTO = dscr("QTO", [4, 128, TQ], BF16)
    YBO = dscr("YBO", [TQ, RW], BF16)
    X1O = dscr("X1O", [TQ, D], F32)
    H2O = dscr("H2O", [128, KC, TQ], BF16)
    YDO = [dscr("YDO%d" % d, [TQ, RW], F32) for d in range(2)]
    GO = dscr("GO", [TQ, RW], F32)
    VO = dscr("VO", [TQ, RW], BF16)
    CFO = dscr("CFO", [TQ, 16], F32)

    SFX = [""]
    PIDC = {}

    def SBT(name, shape, dt):
        return nc.sbuf_tensor(name + SFX[0], shape, dt)

    def PST(name, shape, dt):
        return nc.psum_tensor(name + SFX[0], shape, dt)

    gst = contextlib.ExitStack()
    with gst:
        def GT(name, shape, dt):
            return gst.enter_context(SBT(name, list(shape), dt))

        S = Sched(nc, gst)
        ident_f = GT("ident_f", [128, 128], F32)
        ident = GT("ident", [128, 128], BF16)
        mhalf = GT("mhalf", [128, 16], F32)
        S.dma(ident_f[:], cin['ident'], writes=['ident_f'])
        S.op('dve', lambda e: e.tensor_copy(out=ident[:], in_=ident_f[:]), reads=['ident_f'], writes=['ident'])
        S.op('pool', lambda e: e.memset(mhalf[:], -0.5), writes=['mhalf'])
        gpost = [GT("gpost%d" % i, [128, D], F32) for i in range(3)]
        for i, (gi, sc) in enumerate([(1, 0.5), (3, 1.0), (5, 0.5)]):
            S.dma(gpost[i][:], norm_g[gi].partition_broadcast(128), writes=['gpost%d' % i])
            S.op('dve', lambda e, i=i, sc=sc: e.tensor_scalar(out=gpost[i][:], in0=gpost[i][:], scalar1=sc, scalar2=None, op0=ALU.mult),
                 reads=['gpost%d' % i], writes=['gpost%d' % i])
        S.flush()

        with contextlib.ExitStack() as st:
            def T_(name, shape, dt):
                return st.enter_context(SBT(name, list(shape), dt))
            gT = T_("gT", [128, 3, KC], F32)
            for i, gi in enumerate([0, 2, 4]):
                S.dma(gT[:, i, :], norm_g[gi].rearrange("(kc p) -> p kc", p=128), writes=['gT'],
                      allow_slow_non_contiguous=True)
            omm = T_("omm", [128, NRW], F32)
            hmu = T_("hmu", [128, NRW], F32)
            S.dma(omm[:], mu.partition_broadcast(128), writes=['omm'])
            S.op('dve', lambda e: e.tensor_scalar(out=hmu[:], in0=omm[:], scalar1=0.5, scalar2=None, op0=ALU.mult), reads=['omm'], writes=['hmu'])
            S.op('dve', lambda e: e.tensor_scalar(out=omm[:], in0=omm[:], scalar1=-1.0, scalar2=1.0, op0=ALU.mult, op1=ALU.add), reads=['omm', 'hmu'], writes=['omm'])
            stg = [T_("stg%d" % i, [128, NIN], F32) for i in range(2)]
            stb = [T_("stb%d" % i, [128, NIN], BF16) for i in range(2)]
            stb2 = [T_("stb2%d" % i, [128, NRW], BF16) for i in range(2)]
            cnt = [0]

            def cast_job(src_ap, ncols, dst_ap, gcol=None, rw=None, dst2_ap=None, fin=None, fout=None):
                i = cnt[0] % 2
                cnt[0] += 1
                sk, bk, b2k = 'stg%d' % i, 'stb%d' % i, 'stb2%d' % i
                sv = stg[i][:, 0:ncols]
                if fin:
                    sv = sv.rearrange("p (f c) -> p f c", c=fin)
                S.dma(sv, src_ap, writes=[sk])
                eng = 'act' if (cnt[0] % 2) else 'dve'
                if rw is None:
                    if gcol is None:
                        if eng == 'act':
                            S.op('act', lambda e: e.activation(out=stb[i][:, 0:ncols], in_=stg[i][:, 0:ncols], func=AF.Copy), reads=[sk], writes=[bk])
                        else:
                            S.op('dve', lambda e: e.tensor_copy(out=stb[i][:, 0:ncols], in_=stg[i][:, 0:ncols]), reads=[sk], writes=[bk])
                    else:
                        if eng == 'act':
                            S.op('act', lambda e: e.activation(out=stb[i][:, 0:ncols], in_=stg[i][:, 0:ncols], func=AF.Copy, scale=gcol), reads=[sk, 'gT'], writes=[bk])
                        else:
                            S.op('dve', lambda e: e.tensor_scalar(out=stb[i][:, 0:ncols], in0=stg[i][:, 0:ncols], scalar1=gcol, scalar2=None, op0=ALU.mult), reads=[sk, 'gT'], writes=[bk])
                else:
                    S.op('act', lambda e: e.activation(out=stb[i][:, NRW:ncols], in_=stg[i][:, NRW:ncols], func=AF.Copy, scale=gcol), reads=[sk, 'gT'], writes=[bk])
                    S.op('act', lambda e: e.activation(out=stg[i][:, 0:NRW], in_=stg[i][:, 0:NRW], func=AF.Copy, scale=gcol), reads=[sk, 'gT'], writes=[sk])
                    S.op('dve', lambda e: e.tensor_tensor(out=stb2[i][:], in0=stg[i][:, 0:NRW], in1=hmu[:], op=ALU.mult), reads=[sk, 'hmu'], writes=[b2k])
                    S.op('dve', lambda e: e.tensor_tensor(out=stb[i][:, 0:NRW], in0=stg[i][:, 0:NRW], in1=omm[:], op=ALU.mult), reads=[sk, 'omm'], writes=[bk])
                    S.dma(dst2_ap, stb2[i][:], reads=[b2k], writes=['wscr'], queue='pool')
                bv = stb[i][:, 0:ncols]
                if fout:
                    bv = bv.rearrange("p (f c) -> p f c", c=fout)
                S.dma(dst_ap, bv, reads=[bk], writes=['wscr'], queue='pool')

            for l in range(2):
                gi = 0 if l == 0 else 2
                for kc in range(KC):
                    rows = slice(kc * 128, (kc + 1) * 128)
                    for (Wsrc, Wdst) in ((w_gate, WG), (w_up, WU)):
                        cast_job(Wsrc[l, rows, :], FF,
                                 Wdst[l][:, :, kc, :].rearrange("f p c -> p f c"),
                                 gcol=gT[:, gi, kc:kc + 1], fout=128)
                for fc in range(0, FC, 4):
                    n = min(4, FC - fc)
                    cast_job(w_down[l, fc * 128:(fc + n) * 128, :].rearrange("(f p) c -> p f c", p=128), n * D,
                             WD[l][fc:fc + n].rearrange("f p c -> p f c"), fin=D, fout=D)
            for kc in range(KC):
                rows = slice(kc * 128, (kc + 1) * 128)
                cast_job(w_in[rows, :], NIN, WIN1[kc], gcol=gT[:, 1, kc:kc + 1], rw=True, dst2_ap=WIN2[kc])
            cast_job(w_ba.rearrange("(f p) c -> p f c", p=128), 4 * D, WBA.rearrange("f p c -> p f c"), fin=D, fout=D)
            cast_job(w_bb.rearrange("(f p) c -> p f c", p=128), 4 * D, WBB.rearrange("f p c -> p f c"), fin=D, fout=D)
            for h in range(2):
                cast_job(w_out[h * 512:(h + 1) * 512, :].rearrange("(f p) c -> p f c", p=128), 4 * D,
                         WOUT[h * 4:(h + 1) * 4].rearrange("f p c -> p f c"), fin=D, fout=D)
            S.flush()

        def norm_T(S, P, xin, xin_key, sub, hT, hT_key):
            S.op('act', lambda e: e.activation(out=P['junk'][:], in_=xin[:, sub, :], func=AF.Square, accum_out=P['ss'][:, 0:1]),
                 reads=[xin_key], writes=['junk', 'ss'])
            S.op('dve', lambda e: e.tensor_scalar(out=P['ss'][:, 1:2], in0=P['ss'][:, 0:1], scalar1=1.0 / D, scalar2=EPS, op0=ALU.mult, op1=ALU.add),
                 reads=['ss'], writes=['ss1'])
            S.op('pool', lambda e: e.tensor_tensor(out=P['ss'][:, 2:3], in0=P['ss'][:, 1:2], in1=mhalf[:, 0:1], op=ALU.pow),
                 reads=['ss1', 'mhalf'], writes=['ss2'])
            S.op('act', lambda e: e.activation(out=P['xnb'][:], in_=xin[:, sub, :], func=AF.Copy, scale=P['ss'][:, 2:3]),
                 reads=[xin_key, 'ss2'], writes=['xnb'])
            for kc in range(KC):
                S.op('pe', lambda e, kc=kc: e.transpose(out=P['pst'][:, kc, :], in_=P['xnb'][:, kc * 128:(kc + 1) * 128], identity=ident[:]),
                     reads=['xnb', 'ident'], writes=['pst'])
            S.op('dve', lambda e: e.tensor_copy(out=hT[:, :, sub * 128:(sub + 1) * 128], in_=P['pst'][:]),
                 reads=['pst'], writes=[hT_key])

        def ffn(S, P, l, hT, hT_key, xres, xres_key, gp, xout, xout_key):
            S.dma(P['wd'][:], WD[l].rearrange("f p c -> p f c"), writes=['wd'])
            for fc in range(FC):
                sl = fc % 3
                S.dma(P['wg'][sl][:], WG[l][fc], writes=['wg%d' % sl])
                S.dma(P['wu'][sl][:], WU[l][fc], writes=['wu%d' % sl])
                b = fc % 2
                for kc in range(KC):
                    S.op('pe', lambda e, kc=kc, sl=sl, b=b: e.matmul(out=P['psg'][b][:], lhsT=P['wg'][sl][:, kc, :], rhs=hT[:, kc, :], start=(kc == 0), stop=(kc == KC - 1)),
                         reads=['wg%d' % sl, hT_key], writes=['psg%d' % b])
                for kc in range(KC):
                    S.op('pe', lambda e, kc=kc, sl=sl, b=b: e.matmul(out=P['psu'][b][:], lhsT=P['wu'][sl][:, kc, :], rhs=hT[:, kc, :], start=(kc == 0), stop=(kc == KC - 1)),
                         reads=['wu%d' % sl, hT_key], writes=['psu%d' % b])
                S.op('act', lambda e, b=b: e.activation(out=P['sg'][b][:], in_=P['psg'][b][:], func=AF.Silu),
                     reads=['psg%d' % b], writes=['sg%d' % b])
                S.op('dve', lambda e, b=b, fc=fc: e.tensor_tensor(out=P['actT'][:, fc, :], in0=P['psu'][b][:], in1=P['sg'][b][:], op=ALU.mult),
                     reads=['psu%d' % b, 'sg%d' % b], writes=['actT'])
            for sub in range(4):
                bks = [(2 * sub + hf) % 3 for hf in range(2)]
                for hf in range(2):
                    bk = bks[hf]
                    for fc in range(FC):
                        S.op('pe', lambda e, fc=fc, sub=sub, hf=hf, bk=bk: e.matmul(out=P['psd'][bk][:], lhsT=P['actT'][:, fc, sub * 128:(sub + 1) * 128], rhs=P['wd'][:, fc, hf * 512:(hf + 1) * 512], start=(fc == 0), stop=(fc == FC - 1)),
                             reads=['actT', 'wd'], writes=['psd%d' % bk])
                    S.op('act', lambda e, hf=hf, bk=bk: e.activation(out=P['junk'][:, 0:512], in_=P['psd'][bk][:], func=AF.Square, accum_out=P['ssd'][:, hf:hf + 1]),
                         reads=['psd%d' % bk], writes=['junk', 'ssd'])
                post_norm_res(S, P, [P['psd'][bks[0]], P['psd'][bks[1]]], ['psd%d' % bks[0], 'psd%d' % bks[1]], gp, xres, xres_key, xout, xout_key, sub)

        def post_norm_res(S, P, ps, ps_keys, gp, xres, xres_key, xout, xout_key, sub):
            S.op('dve', lambda e: e.tensor_tensor(out=P['ssd'][:, 2:3], in0=P['ssd'][:, 0:1], in1=P['ssd'][:, 1:2], op=ALU.add),
                 reads=['ssd'], writes=['ssd2'])
            S.op('dve', lambda e: e.tensor_scalar(out=P['ssd'][:, 3:4], in0=P['ssd'][:, 2:3], scalar1=1.0 / D, scalar2=EPS, op0=ALU.mult, op1=ALU.add),
                 reads=['ssd2'], writes=['ssd3'])
            S.op('pool', lambda e: e.tensor_tensor(out=P['ssd'][:, 4:5], in0=P['ssd'][:, 3:4], in1=mhalf[:, 0:1], op=ALU.pow),
                 reads=['ssd3', 'mhalf'], writes=['ssd4'])
            for hf in range(2):
                cs = slice(hf * 512, (hf + 1) * 512)
                S.op('act', lambda e, hf=hf, cs=cs: e.activation(out=P['tmpn'][:, cs], in_=ps[hf][:], func=AF.Copy, scale=P['ssd'][:, 4:5]),
                     reads=[ps_keys[hf], 'ssd4'], writes=['tmpn%d' % hf])
                S.op('dve', lambda e, hf=hf, cs=cs: e.tensor_tensor(out=P['tmpn'][:, cs], in0=P['tmpn'][:, cs], in1=gp[:, cs], op=ALU.mult),
                     reads=['tmpn%d' % hf], writes=['tmpn%d' % hf])
                S.op('pool', lambda e, hf=hf, cs=cs: e.tensor_tensor(out=xout[:, sub, cs], in0=P['tmpn'][:, cs], in1=xres[:, sub, cs], op=ALU.add),
                     reads=['tmpn%d' % hf, xres_key], writes=[xout_key])

        def alloc_ffn(st, pfx):
            def T_(name, shape, dt):
                return st.enter_context(SBT(pfx + name, list(shape), dt))

            def PS(name, shape, dt):
                return st.enter_context(PST(pfx + name, list(shape), dt))
            P = {}
            P['junk'] = T_("junk", [128, D], F32)
            P['ss'] = T_("ss", [128, 4], F32)
            P['ssd'] = T_("ssd", [128, 8], F32)
            P['xnb'] = T_("xnb", [128, D], BF16)
            P['tmpn'] = T_("tmpn", [128, D], F32)
            P['wd'] = T_("wd", [128, FC, D], BF16)
            P['wg'] = [T_("wg%d" % i, [128, KC, 128], BF16) for i in range(3)]
            P['wu'] = [T_("wu%d" % i, [128, KC, 128], BF16) for i in range(3)]
            P['sg'] = [T_("sg%d" % i, [128, 512], F32) for i in range(2)]
            P['actT'] = T_("actT", [128, FC, 512], BF16)
            P['pst'] = PS("pst", [128, KC, 128], BF16)
            P['psg'] = [PS("psg%d" % i, [128, 512], F32) for i in range(2)]
            P['psu'] = [PS("psu%d" % i, [128, 512], F32) for i in range(2)]
            P['psd'] = [PS("psd%d" % i, [128, 512], F32) for i in range(3)]
            return P

        for si, T in enumerate(seq_lens):
            NT = T // 512
            SFX[0] = "_q%d" % si
            QUART = si in quarter
            NQ = T // 4 if QUART else T

            def qoff(eng, NQ=NQ):
                key = (str(eng.engine), S.nflush)
                if key not in PIDC:
                    PIDC[key] = eng.snap((eng.partition_id() % 4) * NQ)
                return PIDC[key]

            def rowsl(base, size):
                if not QUART:
                    return lambda eng: slice(base, base + size)
                return lambda eng: bass.ds(qoff(eng) + base, size)
            xin = xs_in[si]
            yout = ys_out[si]
            with contextlib.ExitStack() as st:
                def T_(name, shape, dt):
                    return st.enter_context(SBT(name, list(shape), dt))
                P = alloc_ffn(st, "A_")
                xres = [T_("A_xres%d" % i, [128, 4, D], F32) for i in range(2)]
                x1 = [T_("A_x1%d" % i, [128, 4, D], F32) for i in range(2)]
                hT = T_("A_hT", [128, KC, 512], BF16)
                h2T = [T_("A_h2T%d" % i, [128, KC, 512], BF16) for i in range(2)]
                zc = T_("A_zc", [128, KC, 1], BF16)
                S.op('pool', lambda e: e.memset(zc[:], 0.0), writes=['zc'])
                S.dma(H2[:, :, 0:1], zc[:], reads=['zc'], writes=['H2pad'], allow_slow_non_contiguous=True)
                S.dma(H2[:, :, T + 1:T + 2], zc[:], reads=['zc'], writes=['H2pad'], allow_slow_non_contiguous=True)
                for ti in range(NT):
                    b = ti % 2
                    xk, x1k, h2k = 'xres%d' % b, 'x1%d' % b, 'h2T%d' % b
                    S.dma(xres[b][:], xin[ti * 512:(ti + 1) * 512, :].rearrange("(s p) d -> p s d", p=128), writes=[xk])
                    for sub in range(4):
                        norm_T(S, P, xres[b], xk, sub, hT, 'hT')
                    ffn(S, P, 0, hT, 'hT', xres[b], xk, gpost[0], x1[b], x1k)
                    S.dma(X1[ti * 512:(ti + 1) * 512, :].rearrange("(s p) d -> p s d", p=128), x1[b][:], reads=[x1k], writes=['X1'], queue='pool')
                    for sub in range(4):
                        norm_T(S, P, x1[b], x1k, sub, h2T[b], h2k)
                    S.dma(H2[:, :, 1 + ti * 512:1 + (ti + 1) * 512], h2T[b][:], reads=[h2k], writes=['H2'], queue='pool')
                S.flush()
            if debug and 'X1' in dbg:
                with contextlib.ExitStack() as st:
                    t1 = st.enter_context(SBT("dbg_t1", [128, T // 128, D], F32))
                    S.dma(t1[:], X1[0:T, :].rearrange("(s p) d -> p s d", p=128), writes=['t1'])
                    S.dma(dbg['X1'].rearrange("(s p) d -> p s d", p=128), t1[:], reads=['t1'], writes=['o'])
                    t2 = st.enter_context(SBT("dbg_t2", [128, KC, T + 2], BF16))
                    S.dma(t2[:], H2[:, :, 0:T + 2], writes=['t2'])
                    S.dma(dbg['H2'], t2[:], reads=['t2'], writes=['o2'])
                    S.flush()
                continue

            with contextlib.ExitStack() as st:
                def T_(name, shape, dt):
                    return st.enter_context(SBT(name, list(shape), dt))

                def PS(name, shape, dt):
                    return st.enter_context(PST(name, list(shape), dt))
                win1 = T_("B_win1", [128, KC, NIN], BF16)
                win2 = T_("B_win2", [128, KC, NRW], BF16)
                for kc in range(KC):
                    S.dma(win1[:, kc, :], WIN1[kc], writes=['win1'])
                    S.dma(win2[:, kc, :], WIN2[kc], writes=['win2'])
                lstg = T_("B_lstg", [128, 3, RW], F32)
                lw_b = T_("B_lw_b", [128, 3, RW], BF16)
                S.dma(lstg[:, 0, :], wl_up.rearrange("d r c -> (d r) c"), writes=['lstg'])
                S.dma(lstg[:, 1, :], al_up.rearrange("d r c -> (d r) c"), writes=['lstg'])
                S.dma(lstg[:, 2, :], gl_up, writes=['lstg'])
                S.op('dve', lambda e: e.tensor_copy(out=lw_b[:], in_=lstg[:]), reads=['lstg'], writes=['lw_b'])
                w0T = T_("B_w0T", [128, 2, 4], F32)
                a0T = T_("B_a0T", [128, 2, 4], F32)
                S.dma(w0T[:], w0.rearrange("d (h p) -> p d h", p=128), writes=['w0T'], allow_slow_non_contiguous=True)
                S.dma(a0T[:], a0.rearrange("d (h p) -> p d h", p=128), writes=['a0T'], allow_slow_non_contiguous=True)
                g64 = T_("B_g64", [128, 2, 64], F32)
                S.dma(g64[:, 0, :], qk_g[0].partition_broadcast(128), writes=['g64'])
                S.dma(g64[:, 1, :], qk_g[1].partition_broadcast(128), writes=['g64'])
                gq = T_("B_gq", [128, 8, 64], F32)
                gk = T_("B_gk", [128, 2, 64], F32)
                S.op('dve', lambda e: e.tensor_copy(out=gq[:], in_=g64[:, 0:1, :].broadcast_to([128, 8, 64])), reads=['g64'], writes=['gq'])
                S.op('dve', lambda e: e.tensor_copy(out=gk[:], in_=g64[:, 1:2, :].broadcast_to([128, 2, 64])), reads=['g64'], writes=['gk'])
                hext = [T_("B_hext%d" % i, [128, KC, 514], BF16) for i in range(2)]
                hs = T_("B_hs", [128, KC, 512], BF16)
                fst = [T_("B_fst%d" % i, [128, 512], F32) for i in range(3)]
                tw = T_("B_tw", [128, 512], BF16)
                ta = T_("B_ta", [128, 512], BF16)
                tg = T_("B_tg", [128, 512], BF16)
                vst = [T_("B_vst%d" % i, [128, 512], BF16) for i in range(2)]
                cs = [T_("B_cs%d" % i, [128, 2, 32], F32) for i in range(2)]
                nq = T_("B_nq", [128, 8, 64], F32)
                nq2 = T_("B_nq2", [128, 8, 64], F32)
                rt = [T_("B_rt%d" % i, [128, 8, 32], F32) for i in range(4)]
                qr = T_("B_qr", [128, 8, 64], BF16)
                kr = T_("B_kr", [128, 2, 64], BF16)
                nss = T_("B_nss", [128, 3, 8], F32)
                qTs = T_("B_qTs", [128, 4, 128], BF16)
                kTs = T_("B_kTs", [128, 128], BF16)
                vas = T_("B_vas", [128, 128], BF16)
                psF = [PS("B_psF%d" % i, [128, 512], F32) for i in range(2)]
                psT = [PS("B_psT%d" % i, [128, 512], F32) for i in range(2)]
                psl = [PS("B_psl%d" % i, [128, 512], F32) for i in range(2)]
                pstr = PS("B_pstr", [128, 8, 128], BF16)
                fcnt = [0]
                lcnt = [0]
                tcnt = [0]

                def fm_proj(hb, co):
                    b = fcnt[0] % 2
                    fcnt[0] += 1
                    for kc in range(KC):
                        S.op('pe', lambda e, kc=kc, b=b: e.matmul(out=psF[b][:], lhsT=win1[:, kc, co:co + 128], rhs=hext[hb][:, kc, 1:513], start=(kc == 0), stop=False),
                             reads=['win1', 'hext%d' % hb], writes=['psF%d' % b])
                    for kc in range(KC):
                        S.op('pe', lambda e, kc=kc, b=b: e.matmul(out=psF[b][:], lhsT=win2[:, kc, co:co + 128], rhs=hs[:, kc, :], start=False, stop=(kc == KC - 1)),
                             reads=['win2', 'hs'], writes=['psF%d' % b])
                    return b

                def norm_rope(ps, ps_key, nh, gain, gain_key, cb, outb, out_key):
                    pv = ps.rearrange("p (h j) -> p h j", j=64)
                    S.op('act', lambda e: e.activation(out=nq[:, 0:nh, :], in_=pv, func=AF.Square), reads=[ps_key], writes=['nq'])
                    S.op('dve', lambda e: e.tensor_reduce(out=nss[:, 0, 0:nh], in_=nq[:, 0:nh, :], axis=AX.X, op=ALU.add), reads=['nq'], writes=['nss0'])
                    S.op('dve', lambda e: e.tensor_scalar(out=nss[:, 1, 0:nh], in0=nss[:, 0, 0:nh], scalar1=1.0 / 64, scalar2=EPS, op0=ALU.mult, op1=ALU.add), reads=['nss0'], writes=['nss1'])
                    S.op('pool', lambda e: e.tensor_tensor(out=nss[:, 2, 0:nh], in0=nss[:, 1, 0:nh], in1=mhalf[:, 0:nh], op=ALU.pow), reads=['nss1', 'mhalf'], writes=['nss2'])
                    S.op('dve', lambda e: e.tensor_tensor(out=nq2[:, 0:nh, :], in0=pv, in1=nss[:, 2, 0:nh].unsqueeze(2).broadcast_to([128, nh, 64]), op=ALU.mult), reads=[ps_key, 'nss2'], writes=['nq2'])
                    S.op('dve', lambda e: e.tensor_tensor(out=nq[:, 0:nh, :], in0=nq2[:, 0:nh, :], in1=gain[:, 0:nh, :], op=ALU.mult), reads=['nq2', gain_key], writes=['nq'])
                    x0 = nq[:, 0:nh, 0:64:2]
                    x1 = nq[:, 0:nh, 1:64:2]
                    cc = cs[cb][:, 0:1, :].broadcast_to([128, nh, 32])
                    sn = cs[cb][:, 1:2, :].broadcast_to([128, nh, 32])
                    ck = 'cs%d' % cb
                    S.op('dve', lambda e: e.tensor_tensor(out=rt[0][:, 0:nh, :], in0=x0, in1=cc, op=ALU.mult), reads=['nq', ck], writes=['rt0'])
                    S.op('dve', lambda e: e.tensor_tensor(out=rt[1][:, 0:nh, :], in0=x1, in1=sn, op=ALU.mult), reads=['nq', ck], writes=['rt1'])
                    S.op('dve', lambda e: e.tensor_tensor(out=rt[2][:, 0:nh, :], in0=x0, in1=sn, op=ALU.mult), reads=['nq', ck], writes=['rt2'])
                    S.op('dve', lambda e: e.tensor_tensor(out=rt[3][:, 0:nh, :], in0=x1, in1=cc, op=ALU.mult), reads=['nq', ck], writes=['rt3'])
                    S.op('dve', lambda e: e.tensor_tensor(out=outb[:, 0:nh, 0:32], in0=rt[0][:, 0:nh, :], in1=rt[1][:, 0:nh, :], op=ALU.subtract), reads=['rt0', 'rt1'], writes=[out_key])
                    S.op('dve', lambda e: e.tensor_tensor(out=outb[:, 0:nh, 32:64], in0=rt[2][:, 0:nh, :], in1=rt[3][:, 0:nh, :], op=ALU.add), reads=['rt2', 'rt3'], writes=[out_key])

                for ti in range(NT):
                    hb = ti % 2
                    tok = slice(ti * 512, (ti + 1) * 512)
                    S.dma(hext[hb][:], H2[:, :, ti * 512:ti * 512 + 514], writes=['hext%d' % hb])
                    S.op('pool', lambda e, hb=hb: e.tensor_tensor(out=hs[:], in0=hext[hb][:, :, 0:512], in1=hext[hb][:, :, 2:514], op=ALU.add),
                         reads=['hext%d' % hb], writes=['hs'])
                    for (co0, DST) in ((0, R_T), (512, K_T)):
                        for hp in range(4):
                            b = fm_proj(hb, co0 + hp * 128)
                            f = fcnt[0] % 3
                            S.op('act', lambda e, b=b, f=f: e.activation(out=fst[f][:], in_=psF[b][:], func=AF.Copy), reads=['psF%d' % b], writes=['fst%d' % f])
                            S.dma(DST[hp][:, tok], fst[f][:], reads=['fst%d' % f], writes=['rk_scr'], queue='pool')
                    b = fm_proj(hb, 1536)
                    S.op('act', lambda e, b=b: e.activation(out=tw[:], in_=psF[b][:], func=AF.Tanh), reads=['psF%d' % b], writes=['tw'])
                    b = fm_proj(hb, 1664)
                    S.op('act', lambda e, b=b: e.activation(out=ta[:], in_=psF[b][:], func=AF.Copy), reads=['psF%d' % b], writes=['ta'])
                    b = fm_proj(hb, 1792)
                    S.op('act', lambda e, b=b: e.activation(out=tg[:], in_=psF[b][:], func=AF.Sigmoid), reads=['psF%d' % b], writes=['tg'])
                    for (wi, src, src_key, biasT, bias_key, DST, scl) in ((0, tw, 'tw', w0T, 'w0T', LW, -0.6065306597126334), (1, ta, 'ta', a0T, 'a0T', AA, None)):
                        for d in range(2):
                            for hp in range(4):
                                lb = lcnt[0] % 2
                                lcnt[0] += 1
                                S.op('pe', lambda e, lb=lb, wi=wi, d=d, hp=hp, src=src: e.matmul(out=psl[lb][:], lhsT=lw_b[d * 64:(d + 1) * 64, wi, hp * 128:(hp + 1) * 128], rhs=src[d * 64:(d + 1) * 64, :], start=True, stop=True),
                                     reads=['lw_b', src_key], writes=['psl%d' % lb])
                                f = lcnt[0] % 3
                                S.op('act', lambda e, lb=lb, f=f, d=d, hp=hp, biasT=biasT: e.activation(out=fst[f][:], in_=psl[lb][:], func=AF.Sigmoid, bias=biasT[:, d, hp:hp + 1]),
                                     reads=['psl%d' % lb, bias_key], writes=['fst%d' % f])
                                if scl is not None:
                                    S.op('dve', lambda e, f=f, scl=scl: e.tensor_scalar(out=fst[f][:], in0=fst[f][:], scalar1=scl, scalar2=None, op0=ALU.mult), reads=['fst%d' % f], writes=['fst%d' % f])
                                S.dma(DST[d, hp][:, tok], fst[f][:], reads=['fst%d' % f], writes=['la_scr'], queue='pool')
                    for sub in range(4):
                        rows = slice(ti * 512 + sub * 128, ti * 512 + (sub + 1) * 128)
                        scol = slice(sub * 128, (sub + 1) * 128)
                        tb = tcnt[0] % 2
                        tcnt[0] += 1
                        S.op('pe', lambda e, tb=tb, scol=scol: e.matmul(out=psT[tb][:], lhsT=tg[:, scol], rhs=lw_b[:, 2, :], start=True, stop=True),
                             reads=['tg', 'lw_b'], writes=['psT%d' % tb])
                        f = tcnt[0] % 3
                        S.op('act', lambda e, tb=tb, f=f: e.activation(out=fst[f][:], in_=psT[tb][:], func=AF.Copy), reads=['psT%d' % tb], writes=['fst%d' % f])
                        S.dma(G_TM[rows, :], fst[f][:], reads=['fst%d' % f], writes=['g_scr'], queue='pool')
                        tb = tcnt[0] % 2
                        tcnt[0] += 1
                        for kc in range(KC):
                            S.op('pe', lambda e, kc=kc, tb=tb, sub=sub, hb=hb: e.matmul(out=psT[tb][:], lhsT=hext[hb][:, kc, 1 + sub * 128:1 + (sub + 1) * 128], rhs=win1[:, kc, 1024:1536], start=(kc == 0), stop=False),
                                 reads=['win1', 'hext%d' % hb], writes=['psT%d' % tb])
                        for kc in range(KC):
                            S.op('pe', lambda e, kc=kc, tb=tb, scol=scol: e.matmul(out=psT[tb][:], lhsT=hs[:, kc, scol], rhs=win2[:, kc, 1024:1536], start=False, stop=(kc == KC - 1)),
                                 reads=['win2', 'hs'], writes=['psT%d' % tb])
                        vb = tcnt[0] % 2
                        S.op('act', lambda e, tb=tb, vb=vb: e.activation(out=vst[vb][:], in_=psT[tb][:], func=AF.Copy), reads=['psT%d' % tb], writes=['vst%d' % vb])
                        S.dma(V_TM[rows, :], vst[vb][:], reads=['vst%d' % vb], writes=['v_scr'], queue='pool')
                        cb = sub % 2
                        S.dma(cs[cb][:, 0, :], cos_in[T][rows, :], writes=['cs%d' % cb])
                        S.dma(cs[cb][:, 1, :], sin_in[T][rows, :], writes=['cs%d' % cb])
                        tb = tcnt[0] % 2
                        tcnt[0] += 1
                        for kc in range(KC):
                            S.op('pe', lambda e, kc=kc, tb=tb, sub=sub, hb=hb: e.matmul(out=psT[tb][:], lhsT=hext[hb][:, kc, 1 + sub * 128:1 + (sub + 1) * 128], rhs=win1[:, kc, 1920:2432], start=(kc == 0), stop=(kc == KC - 1)),
                                 reads=['win1', 'hext%d' % hb], writes=['psT%d' % tb])
                        norm_rope(psT[tb][:], 'psT%d' % tb, 8, gq, 'gq', cb, qr, 'qr')
                        for hp in range(4):
                            S.op('pe', lambda e, hp=hp: e.transpose(out=pstr[:, hp, :], in_=qr[:, 2 * hp:2 * hp + 2, :].rearrange("p a b -> p (a b)"), identity=ident[:]),
                                 reads=['qr', 'ident'], writes=['pstr'])
                        S.op('dve', lambda e: e.tensor_copy(out=qTs[:], in_=pstr[:, 0:4, :]), reads=['pstr'], writes=['qTs'])
                        S.dma(QT[:, :, rows].rearrange("h p t -> p h t"), qTs[:], reads=['qTs'], writes=['q_scr'], queue='pool')
                        tb = tcnt[0] % 2
                        tcnt[0] += 1
                        for kc in range(KC):
                            S.op('pe', lambda e, kc=kc, tb=tb, sub=sub, hb=hb: e.matmul(out=psT[tb][:, 0:256], lhsT=hext[hb][:, kc, 1 + sub * 128:1 + (sub + 1) * 128], rhs=win1[:, kc, 2432:2688], start=(kc == 0), stop=(kc == KC - 1)),
                                 reads=['win1', 'hext%d' % hb], writes=['psT%d' % tb])
                        S.op('act', lambda e, tb=tb: e.activation(out=vas[:], in_=psT[tb][:, 128:256], func=AF.Copy), reads=['psT%d' % tb], writes=['vas'])
                        S.dma(VA[rows, :], vas[:], reads=['vas'], writes=['va_scr'], queue='pool')
                        norm_rope(psT[tb][:, 0:128], 'psT%d' % tb, 2, gk, 'gk', cb, kr, 'kr')
                        S.op('pe', lambda e: e.transpose(out=pstr[:, 4, :], in_=kr[:].rearrange("p a b -> p (a b)"), identity=ident[:]),
                             reads=['kr', 'ident'], writes=['pstr'])
                        S.op('dve', lambda e: e.tensor_copy(out=kTs[:], in_=pstr[:, 4, :]), reads=['pstr'], writes=['kTs'])
                        S.dma(KTA[:, rows], kTs[:], reads=['kTs'], writes=['k_scr'], queue='pool')
                S.flush()
            if debug and 'RT' in dbg:
                with contextlib.ExitStack() as st:
                    def dump(name, src, shape, dt):
                        t = st.enter_context(SBT("dbgt_" + name, shape, dt))
                        S.dma(t[:], src, writes=[name])
                        S.dma(dbg[name], t[:], reads=[name], writes=[name + 'o'])
                    dump('RT', R_T[0][:, 0:T], [128, T], F32)
                    dump('KT', K_T[1][:, 0:T], [128, T], F32)
                    dump('LW', LW[1, 2][:, 0:T], [128, T], F32)
                    dump('AA', AA[0, 3][:, 0:T], [128, T], F32)
                    dump('V', V_TM[0:128, :], [128, RW], BF16)
                    dump('G', G_TM[128:256, :], [128, RW], F32)
                    dump('QT', QT[1][:, 0:T], [128, T], BF16)
                    dump('KTA', KTA[:, 0:T], [128, T], BF16)
                    dump('VA', VA[0:128, :], [128, 128], BF16)
                    S.flush()
                continue

            NS = min(T, NS_CAP_B)
            NSC = T // NS
            NCH = NS // CH
            with contextlib.ExitStack() as st:
                def T_(name, shape, dt):
                    return st.enter_context(SBT(name, list(shape), dt))
                pb_ = [st.enter_context(PST("R_pb%d" % i, [128, 512], F32)) for i in range(8)]
                pk = ['pb%d' % i for i in range(8)]
                cstg = T_("R_cstg", [128, 5, 128], F32)
                S.dma(cstg[:, 0, :], cin['maskf'], writes=['cstg'])
                S.dma(cstg[:, 1, :], cin['maskb'], writes=['cstg'])
                S.dma(cstg[:, 2, :], cin['blockones'], writes=['cstg'])
                S.dma(cstg[:, 3, 0:64], cin['masknt'][0], writes=['cstg'])
                S.dma(cstg[:, 3, 64:128], cin['masknt'][1], writes=['cstg'])
                S.dma(cstg[:, 4, 0:2], cin['headsel'], writes=['cstg'])
                bones = T_("R_bones", [128, 128], BF16)
                hsel = T_("R_hsel", [128, 2], BF16)
                hself = T_("R_hself", [128, 2], F32)
                S.op('dve', lambda e: e.tensor_copy(out=bones[:], in_=cstg[:, 2, :]), reads=['cstg'], writes=['bones'])
                S.op('dve', lambda e: e.tensor_copy(out=hsel[:], in_=cstg[:, 4, 0:2]), reads=['cstg'], writes=['hsel'])
                S.op('dve', lambda e: e.tensor_copy(out=hself[:], in_=cstg[:, 4, 0:2]), reads=['cstg'], writes=['hself'])
                pvec = T_("R_pvec", [128, 4, 4], F32)
                for i, v in enumerate((k_k, k_a, k_a, r_k)):
                    S.dma(pvec[:, i, :], v.rearrange("(h p) -> p h", p=128), writes=['pvec'], allow_slow_non_contiguous=True)
                S.op('dve', lambda e: e.tensor_scalar(out=pvec[:, 2, :], in0=pvec[:, 2, :], scalar1=-1.0, scalar2=1.0, op0=ALU.mult, op1=ALU.add), reads=['pvec'], writes=['pvec'])
                rmask = T_("R_rmask", [128, NCH, CH], F32)
                S.op('pool', lambda e: e.memset(rmask[:], 1.0), writes=['rmask'])
                S.op('pool', lambda e: e.memset(rmask[:, :, 0:1], 0.0), reads=[], writes=['rmask'])
                ft = [T_("R_ft%d" % i, [128, NS], F32) for i in range(10)]
                fk = ['ft%d' % i for i in range(10)]
                sqb = T_("R_sqb", [128, NS], BF16)
                prodT = T_("R_prodT", [128, NS], BF16)
                QTt = [[T_("R_QTt%d_%d" % (z, d), [128, NCH, 2, CH], BF16) for d in range(2)] for z in range(2)]
                KTt = [[T_("R_KTt%d_%d" % (z, d), [128, NCH, 2, CH], BF16) for d in range(2)] for z in range(2)]
                Vm = [[T_("R_Vm%d_%d" % (z, d), [128, NCH, 2, CH], BF16) for d in range(2)] for z in range(2)]
                ATt = [[T_("R_AT%d_%d" % (z, d), [128, NCH, 2, 128], BF16) for d in range(2)] for z in range(2)]
                Km = [[T_("R_Km%d_%d" % (z, d), [128, NCH, 2, CH], BF16) for d in range(2)] for z in range(2)]
                KTm = [[T_("R_KTm%d_%d" % (z, d), [128, NCH, 2, 2, CH], BF16) for d in range(2)] for z in range(2)]
                Ttt = [[T_("R_Tt%d_%d" % (z, d), [64, NCH, 2, CH], BF16) for d in range(2)] for z in range(2)]
                Yt = [[T_("R_Y%d_%d" % (z, d), [64, NCH, 2, CH], F32) for d in range(2)] for z in range(2)]
                gam = [[T_("R_gam%d_%d" % (z, d), [128, NCH], F32) for d in range(2)] for z in range(2)]
                Xb = [[T_("R_Xb%d_%d" % (u, i), [64, 8, CH], BF16) for i in range(2)] for u in range(2)]
                XTb = [[T_("R_XTb%d_%d" % (u, i), [64, 8, CH], BF16) for i in range(2)] for u in range(2)]
                Ttf = [T_("R_Ttf%d" % u, [64, 8, CH], F32) for u in range(2)]
                Ttb = [T_("R_Ttb%d" % u, [64, 8, CH], BF16) for u in range(2)]
                Hf = T_("R_Hf", [128, 2, CH], F32)
                Hg = T_("R_Hg", [128, 2, CH], F32)
                Hb = T_("R_Hb", [128, 4, CH], BF16)
                Xs = T_("R_Xs", [64, 4, CH], BF16)
                coefT = T_("R_coefT", [128, T // 128, 16], F32)
                S.flush()
                INV_BANKS = ((5, 6, 0), (2, 3, 4))

                def precompute(hp, d, sc, z):
                    tok0 = sc * NS
                    cols = slice(tok0, tok0 + NS)
                    D_ = str(d) + '_' + str(z)
                    kQ, kK, kKm_, kAT, kKmm, kTt, kVV, kVU, kG = 'QTt' + D_, 'KTt' + D_, 'KTm' + D_, 'AT' + D_, 'Km' + D_, 'Tt' + D_, 'VmV' + D_, 'VmU' + D_, 'gam' + D_
                    r_s, k_s, lw_s, a_s = ft[0], ft[1], ft[2], ft[3]
                    S.dma(r_s[:], R_T[hp][:, cols], writes=[fk[0]])
                    S.dma(k_s[:], K_T[hp][:, cols], writes=[fk[1]])
                    S.dma(lw_s[:], LW[d, hp][:, cols], writes=[fk[2]])
                    S.dma(a_s[:], AA[d, hp][:, cols], writes=[fk[3]])
                    for hh in range(2):
                        S.dma(Vm[z][d][64:128, :, hh, :], V_TM[tok0:tok0 + NS, (hp * 2 + hh) * 64:(hp * 2 + hh + 1) * 64].rearrange("(c p) j -> p c j", p=64), writes=[kVV])
                    S.op('pool', lambda e: e.memset(Vm[z][d][0:64, :, :, :], 0.0), writes=[kVU])
                    yield
                    S.op('dve', lambda e: e.tensor_scalar(out=ft[4][:], in0=k_s[:], scalar1=pvec[:, 0, hp:hp + 1], scalar2=None, op0=ALU.mult), reads=[fk[1], 'pvec'], writes=[fk[4]])
                    S.op('act', lambda e: e.activation(out=sqb[:], in_=ft[4][:], func=AF.Square), reads=[fk[4]], writes=['sqb'])
                    for q in range(NS // 512):
                        qs = slice(q * 512, (q + 1) * 512)
                        S.op('pe', lambda e, q=q, qs=qs: e.matmul(out=pb_[0][:], lhsT=bones[:], rhs=sqb[:, qs], start=True, stop=True), reads=['bones', 'sqb'], writes=[pk[0]])
                        S.op('dve', lambda e, q=q, qs=qs: e.tensor_scalar(out=ft[5][:, qs], in0=pb_[0][:], scalar1=1e-24, scalar2=None, op0=ALU.max), reads=[pk[0]], writes=[fk[5]])
                    S.op('act', lambda e: e.activation(out=ft[5][:], in_=ft[5][:], func=AF.Ln), reads=[fk[5]], writes=[fk[5]])
                    S.op('act', lambda e: e.activation(out=ft[5][:], in_=ft[5][:], func=AF.Exp, scale=-0.5), reads=[fk[5]], writes=[fk[5]])
                    S.op('dve', lambda e: e.tensor_tensor(out=ft[4][:], in0=ft[4][:], in1=ft[5][:], op=ALU.mult), reads=[fk[4], fk[5]], writes=[fk[4]])
                    yield
                    S.op('dve', lambda e: e.tensor_scalar(out=ft[6][:], in0=a_s[:], scalar1=pvec[:, 1, hp:hp + 1], scalar2=pvec[:, 2, hp:hp + 1], op0=ALU.mult, op1=ALU.add), reads=[fk[3], 'pvec'], writes=[fk[6]])
                    S.op('pool', lambda e: e.tensor_tensor(out=ft[6][:], in0=ft[6][:], in1=k_s[:], op=ALU.mult), reads=[fk[6], fk[1]], writes=[fk[6]])
                    S.op('pool', lambda e: e.tensor_tensor(out=ft[7][:], in0=ft[4][:], in1=a_s[:], op=ALU.mult), reads=[fk[4], fk[3]], writes=[fk[7]])
                    yield
                    S.op('dve', lambda e: e.tensor_tensor_scan(out=ft[8][:], data0=rmask[:].rearrange("p c j -> p (c j)"), data1=lw_s[:], initial=0.0, op0=ALU.mult, op1=ALU.add), reads=['rmask', fk[2]], writes=[fk[8]])
                    L3 = ft[8][:].rearrange("p (c j) -> p c j", j=CH)
                    if d == 1:
                        S.op('dve', lambda e: e.tensor_tensor(out=ft[9][:], in0=lw_s[:], in1=ft[8][:], op=ALU.subtract), reads=[fk[2], fk[8]], writes=[fk[9]])
                        S.op('dve', lambda e: e.tensor_tensor(out=ft[9][:].rearrange("p (c j) -> p c j", j=CH), in0=ft[9][:].rearrange("p (c j) -> p c j", j=CH), in1=L3[:, :, CH - 1:CH].broadcast_to([128, NCH, CH]), op=ALU.add), reads=[fk[9], fk[8]], writes=[fk[9]])
                        Lt, Lk, Et, Ek = ft[9], fk[9], ft[8], fk[8]
                    else:
                        Lt, Lk, Et, Ek = ft[8], fk[8], ft[9], fk[9]
                    Et3 = Et[:].rearrange("p (c j) -> p c j", j=CH)
                    S.op('act', lambda e: e.activation(out=Et[:], in_=Lt[:], func=AF.Exp), reads=[Lk], writes=[Ek])
                    gi = CH - 1 if d == 0 else 0
                    S.op('dve', lambda e: e.tensor_copy(out=gam[z][d][:], in_=Et3[:, :, gi]), reads=[Ek], writes=[kG])
                    S.op('dve', lambda e: e.tensor_tensor(out=QTt[z][d][:, :, 1, :], in0=r_s[:].rearrange("p (c j) -> p c j", j=CH), in1=Et3, op=ALU.mult), reads=[fk[0], Ek], writes=[kQ])
                    S.op('dve', lambda e: e.tensor_tensor(out=ft[5][:], in0=r_s[:], in1=ft[6][:], op=ALU.mult), reads=[fk[0], fk[6]], writes=[fk[5]])
                    S.op('act', lambda e: e.activation(out=prodT[:], in_=ft[5][:], func=AF.Copy, scale=pvec[:, 3, hp:hp + 1]), reads=[fk[5], 'pvec'], writes=['prodT'])
                    yield
                    S.op('act', lambda e: e.activation(out=Et[:], in_=Lt[:], func=AF.Exp, scale=-1.0), reads=[Lk, kQ, kG], writes=[Ek])
                    S.op('dve', lambda e: e.tensor_tensor(out=KTt[z][d][:, :, 0, :], in0=ft[7][:].rearrange("p (c j) -> p c j", j=CH), in1=Et3, op=ALU.mult), reads=[fk[7], Ek], writes=[kK])
                    S.op('dve', lambda e: e.tensor_tensor(out=KTt[z][d][:, :, 1, :], in0=ft[6][:].rearrange("p (c j) -> p c j", j=CH), in1=Et3, op=ALU.mult), reads=[fk[6], Ek], writes=[kK])
                    yield
                    for hh in range(2):
                        S.op('act', lambda e, hh=hh: e.activation(out=KTm[z][d][:, :, hh, :, :].rearrange("p c a b -> p c (a b)"), in_=KTt[z][d][:].rearrange("p c a b -> p c (a b)"), func=AF.Copy, scale=hself[:, hh:hh + 1]), reads=[kK, 'hself'], writes=[kKm_])
                    S.op('dve', lambda e: e.tensor_tensor(out=Et[:], in0=Lt[:], in1=lw_s[:], op=ALU.subtract), reads=[Lk, fk[2], kK], writes=[Ek])
                    S.op('act', lambda e: e.activation(out=Et[:], in_=Et[:], func=AF.Exp), reads=[Ek], writes=[Ek])
                    S.op('act', lambda e: e.activation(out=ft[7][:], in_=ft[4][:], func=AF.Copy, scale=-1.0), reads=[fk[4], kK], writes=[fk[7]])
                    S.op('dve', lambda e: e.tensor_tensor(out=QTt[z][d][:, :, 0, :], in0=ft[7][:].rearrange("p (c j) -> p c j", j=CH), in1=Et3, op=ALU.mult), reads=[fk[7], Ek], writes=[kQ])
                    yield
                    nb = NS // 128
                    cv = pb_[0][:, 0:nb * 2].rearrange("p (q h) -> p q h", h=2)
                    for q in range(nb):
                        S.op('pe', lambda e, q=q: e.matmul(out=cv[:, q, :], lhsT=prodT[:, q * 128:(q + 1) * 128], rhs=hsel[:], start=True, stop=True), reads=['prodT', 'hsel'], writes=[pk[0]])
                    S.op('act', lambda e: e.activation(out=coefT[:, tok0 // 128:tok0 // 128 + nb, d * 8 + hp * 2:d * 8 + hp * 2 + 2], in_=cv, func=AF.Copy), reads=[pk[0]], writes=['coefT'])
                    yield
                    mk = cstg[:, d, :]
                    combos = [(c, hh) for c in range(NCH) for hh in range(2)]
                    ATf = ATt[z][d][:].rearrange("p c h s -> p (c h) s")
                    Ttf_all = Ttt[z][d][:].rearrange("p c h s -> p (c h) s")
                    for g4 in range(len(combos) // 4):
                        bk = 2 + (g4 % 2)
                        av = pb_[bk][:].rearrange("p (j s) -> p j s", s=128)
                        for j in range(4):
                            c, hh = combos[g4 * 4 + j]
                            S.op('pe', lambda e, av=av, j=j, c=c, hh=hh: e.matmul(out=av[:, j, :], lhsT=KTm[z][d][:, c, hh, :, :].rearrange("p a b -> p (a b)"), rhs=QTt[z][d][:, c, :, :].rearrange("p a b -> p (a b)"), start=True, stop=True),
                                 reads=[kKm_, kQ], writes=[pk[bk]])
                        S.op('dve', lambda e, av=av, g4=g4: e.tensor_tensor(out=ATf[:, g4 * 4:g4 * 4 + 4, :], in0=av, in1=mk.unsqueeze(1).broadcast_to([128, 4, 128]), op=ALU.mult),
                             reads=[pk[bk], 'cstg'], writes=[kAT])
                        yield
                    for g8 in range(NCH // 8):
                        kv = pb_[4][:].bitcast(BF16)[:, 0:1024].rearrange("p (j s) -> p j s", s=128)
                        for j in range(8):
                            c = g8 * 8 + j
                            S.op('pe', lambda e, kv=kv, j=j, c=c: e.transpose(out=kv[:, j, :], in_=KTt[z][d][:, c, :, :].rearrange("p a b -> p (a b)"), identity=ident[:]),
                                 reads=[kK, 'ident'], writes=[pk[4]])
                        S.op('act', lambda e, kv=kv, g8=g8: e.activation(out=Km[z][d][:, g8 * 8:g8 * 8 + 8, :, :].rearrange("p c h s -> p c (h s)"), in_=kv, func=AF.Copy), reads=[pk[4]], writes=[kKmm])
                        yield
                    mnt = cstg[0:64, 3, d * 64:(d + 1) * 64]
                    ngrp = len(combos) // 8
                    for gp in range(0, ngrp, 2):
                        units = [u for u in range(2) if gp + u < ngrp]
                        stt_ = {}
                        for u in units:
                            g8 = gp + u
                            b5, b6, b0 = INV_BANKS[u]
                            v5 = pb_[b5][0:64, :].rearrange("p (j s) -> p j s", s=64)
                            v6 = pb_[b6][0:64, :].rearrange("p (j s) -> p j s", s=64)
                            v0 = pb_[b0][0:64, :].rearrange("p (j s) -> p j s", s=64)
                            gs = slice(g8 * 8, g8 * 8 + 8)
                            X0 = ATf[0:64, gs, 0:64]
                            U_ = str(u)
                            for j in range(8):
                                c, hh = combos[g8 * 8 + j]
                                S.op('pe', lambda e, j=j, c=c, hh=hh, v5=v5: e.matmul(out=v5[:, j, :], lhsT=QTt[z][d][:, c, 0, :], rhs=KTm[z][d][:, c, hh, 0, :], start=True, stop=True),
                                     reads=[kQ, kKm_], writes=[pk[b5]])
                            S.op('dve', lambda e, v5=v5, u=u: e.tensor_tensor(out=XTb[u][0][:], in0=v5, in1=mnt.unsqueeze(1).broadcast_to([64, 8, 64]), op=ALU.mult), reads=[pk[b5], 'cstg'], writes=['XTb' + U_ + '0'])
                            S.op('dve', lambda e, X0=X0, u=u: e.tensor_tensor(out=Ttf[u][:], in0=X0, in1=ident_f[0:64, 0:64].unsqueeze(1).broadcast_to([64, 8, 64]), op=ALU.add), reads=[kAT, 'ident_f'], writes=['Ttf' + U_])
                            S.op('act', lambda e, u=u: e.activation(out=Ttb[u][:], in_=Ttf[u][:], func=AF.Copy), reads=['Ttf' + U_], writes=['Ttb' + U_])
                            stt_[u] = dict(Xc=X0, Xck=kAT, XTc=XTb[u][0], XTck='XTb' + U_ + '0', v5=v5, v6=v6, v0=v0, b5=b5, b6=b6, b0=b0, gs=gs)
                            yield
                        for it in range(1, 6):
                            nx = it % 2
                            for u in units:
                                s_ = stt_[u]
                                U_ = str(u)
                                if it < 5:
                                    for j in range(8):
                                        S.op('pe', lambda e, j=j, s_=s_, Xc=s_['Xc'], XTc=s_['XTc']: e.matmul(out=s_['v6'][:, j, :], lhsT=XTc[:, j, :], rhs=Xc[:, j, :], start=True, stop=True), reads=[s_['Xck'], s_['XTck']], writes=[pk[s_['b6']]])
                                for j in range(8):
                                    S.op('pe', lambda e, j=j, s_=s_, Xc=s_['Xc'], XTc=s_['XTc']: e.matmul(out=s_['v5'][:, j, :], lhsT=Xc[:, j, :], rhs=XTc[:, j, :], start=True, stop=True), reads=[s_['Xck'], s_['XTck']], writes=[pk[s_['b5']]])
                            yield
                            for u in units:
                                s_ = stt_[u]
                                U_ = str(u)
                                if it < 5:
                                    S.op('act', lambda e, nx=nx, s_=s_, u=u: e.activation(out=Xb[u][nx][:], in_=s_['v6'], func=AF.Copy), reads=[pk[s_['b6']]], writes=['Xb%s%d' % (U_, nx)])
                                S.op('dve', lambda e, nx=nx, s_=s_, u=u: e.tensor_copy(out=XTb[u][nx][:], in_=s_['v5']), reads=[pk[s_['b5']]], writes=['XTb%s%d' % (U_, nx)])
                                s_['Xc'], s_['Xck'] = Xb[u][nx], 'Xb%s%d' % (U_, nx)
                                s_['XTc'], s_['XTck'] = XTb[u][nx], 'XTb%s%d' % (U_, nx)
                            yield
                            for u in units:
                                s_ = stt_[u]
                                U_ = str(u)
                                for j in range(8):
                                    S.op('pe', lambda e, j=j, s_=s_, XTc=s_['XTc'], u=u: e.matmul(out=s_['v0'][:, j, :], lhsT=XTc[:, j, :], rhs=Ttb[u][:, j, :], start=True, stop=True), reads=[s_['XTck'], 'Ttb' + U_], writes=[pk[s_['b0']]])
                            yield
                            for u in units:
                                s_ = stt_[u]
                                U_ = str(u)
                                S.op('dve', lambda e, s_=s_, u=u: e.tensor_tensor(out=Ttf[u][:], in0=s_['v0'], in1=Ttf[u][:], op=ALU.add), reads=[pk[s_['b0']], 'Ttf' + U_], writes=['Ttf' + U_])
                                if it < 5:
                                    S.op('act', lambda e, u=u: e.activation(out=Ttb[u][:], in_=Ttf[u][:], func=AF.Copy), reads=['Ttf' + U_], writes=['Ttb' + U_])
                                else:
                                    S.op('act', lambda e, s_=s_, u=u: e.activation(out=Ttf_all[:, s_['gs'], :], in_=Ttf[u][:], func=AF.Copy), reads=['Ttf' + U_], writes=[kTt])
                            yield

                def chain_step(i, z):
                    vx = pb_[1][0:64, 0:256].rearrange("p (h s) -> p h s", s=64)
                    vy = pb_[1][0:64, 256:512].rearrange("p (h s) -> p h s", s=64)
                    vu = pb_[7][0:64, 0:256].rearrange("p (h s) -> p h s", s=64)
                    vh = pb_[7][:, 256:512].rearrange("p (h s) -> p h s", s=64)
                    cs_ = (i, NCH - 1 - i)
                    for d in range(2):
                        c = cs_[d]
                        D_ = str(d) + '_' + str(z)
                        S.op('dve', lambda e, c=c, d=d: e.tensor_scalar(out=Hg[:, d, :], in0=Hf[:, d, :], scalar1=gam[z][d][:, c:c + 1], scalar2=None, op0=ALU.mult), reads=['Hf', 'gam' + D_], writes=['Hg'])
                    for d in range(2):
                        c = cs_[d]
                        D_ = str(d) + '_' + str(z)
                        for hh in range(2):
                            k = d * 2 + hh
                            S.op('pe', lambda e, c=c, hh=hh, d=d, k=k: e.matmul(out=vx[:, k, :], lhsT=QTt[z][d][:, c, 0, :], rhs=Hb[:, k, :], start=True, stop=False), reads=['QTt' + D_, 'Hb'], writes=[pk[1]])
                            S.op('pe', lambda e, c=c, hh=hh, d=d, k=k: e.matmul(out=vx[:, k, :], lhsT=ATt[z][d][:, c, hh, 0:64], rhs=Vm[z][d][:, c, hh, :], start=False, stop=True), reads=['AT' + D_, 'VmV' + D_, 'VmU' + D_], writes=[pk[1]])
                    S.op('act', lambda e: e.activation(out=Xs[:], in_=vx, func=AF.Copy), reads=[pk[1]], writes=['Xs'])
                    yield
                    for d in range(2):
                        c = cs_[d]
                        D_ = str(d) + '_' + str(z)
                        for hh in range(2):
                            k = d * 2 + hh
                            S.op('pe', lambda e, c=c, hh=hh, d=d, k=k: e.matmul(out=vu[:, k, :], lhsT=Ttt[z][d][:, c, hh, :], rhs=Xs[:, k, :], start=True, stop=True), reads=['Tt' + D_, 'Xs'], writes=[pk[7]])
                    yield
                    for d in range(2):
                        c = cs_[d]
                        D_ = str(d) + '_' + str(z)
                        S.op('dve', lambda e, c=c, d=d: e.tensor_copy(out=Vm[z][d][0:64, c, :, :], in_=vu[:, d * 2:d * 2 + 2, :]), reads=[pk[7]], writes=['VmU' + D_])
                    yield
                    for d in range(2):
                        c = cs_[d]
                        D_ = str(d) + '_' + str(z)
                        for hh in range(2):
                            k = d * 2 + hh
                            S.op('pe', lambda e, c=c, hh=hh, d=d, k=k: e.matmul(out=vy[:, k, :], lhsT=QTt[z][d][:, c, 1, :], rhs=Hb[:, k, :], start=True, stop=False), reads=['QTt' + D_, 'Hb'], writes=[pk[1]])
                            S.op('pe', lambda e, c=c, hh=hh, d=d, k=k: e.matmul(out=vy[:, k, :], lhsT=ATt[z][d][:, c, hh, 64:128], rhs=Vm[z][d][:, c, hh, :], start=False, stop=True), reads=['AT' + D_, 'VmV' + D_, 'VmU' + D_], writes=[pk[1]])
                    for d in range(2):
                        c = cs_[d]
                        D_ = str(d) + '_' + str(z)
                        for hh in range(2):
                            k = d * 2 + hh
                            S.op('pe', lambda e, c=c, hh=hh, d=d, k=k: e.matmul(out=vh[:, k, :], lhsT=Km[z][d][:, c, :, :].rearrange("p h s -> p (h s)"), rhs=Vm[z][d][:, c, hh, :], start=True, stop=True), reads=['Km' + D_, 'VmV' + D_, 'VmU' + D_], writes=[pk[7]])
                    yield
                    for d in range(2):
                        c = cs_[d]
                        S.op('act', lambda e, c=c, d=d: e.activation(out=Yt[z][d][:, c, :, :], in_=vy[:, d * 2:d * 2 + 2, :], func=AF.Copy), reads=[pk[1]], writes=['Y' + str(d) + '_' + str(z)])
                    for d in range(2):
                        c = cs_[d]
                        D_ = str(d) + '_' + str(z)
                        for hh in range(2):
                            k = d * 2 + hh
                            p0 = hh * 64
                            S.op('dve', lambda e, c=c, d=d, k=k, p0=p0: e.scalar_tensor_tensor(out=Hb[p0:p0 + 64, k, :], in0=vh[p0:p0 + 64, k, :], scalar=gam[z][d][p0:p0 + 64, c:c + 1], in1=Hg[p0:p0 + 64, d, :], op0=ALU.mult, op1=ALU.add), reads=[pk[7], 'gam' + D_, 'Hg'], writes=['Hb'])
                    for d in range(2):
                        c = cs_[d]
                        D_ = str(d) + '_' + str(z)
                        for hh in range(2):
                            k = d * 2 + hh
                            p0 = hh * 64
                            S.op('dve', lambda e, c=c, d=d, k=k, p0=p0: e.scalar_tensor_tensor(out=Hf[p0:p0 + 64, d, :], in0=vh[p0:p0 + 64, k, :], scalar=gam[z][d][p0:p0 + 64, c:c + 1], in1=Hg[p0:p0 + 64, d, :], op0=ALU.mult, op1=ALU.add), reads=[pk[7], 'gam' + D_, 'Hg'], writes=['Hf'])
                    yield

                tasks = [(hp, s_i) for hp in range(4) for s_i in range(NSC)]

                def pre_task(k):
                    hp, s_i = tasks[k]
                    scs = (s_i, NSC - 1 - s_i)
                    for d in range(2):
                        yield from precompute(hp, d, scs[d], k % 2)

                def chain_task(k):
                    hp, s_i = tasks[k]
                    z = k % 2
                    scs = (s_i, NSC - 1 - s_i)
                    if s_i == 0:
                        S.op('pool', lambda e: e.memset(Hf[:], 0.0), writes=['Hf'])
                        S.op('pool', lambda e: e.memset(Hb[:], 0.0), writes=['Hb'])
                    for i in range(NCH):
                        yield from chain_step(i, z)
                    for d in range(2):
                        tok0 = scs[d] * NS
                        S.dma(YD[d][tok0:tok0 + NS, hp * 128:(hp + 1) * 128].rearrange("(c p) j -> p c j", p=64), Yt[z][d][:].rearrange("p c h s -> p c (h s)"), reads=['Y' + str(d) + '_' + str(z)], writes=['YD'], queue='pool')

                for _ in pre_task(0):
                    pass
                for k in range(len(tasks)):
                    a = chain_task(k)
                    b = pre_task(k + 1) if k + 1 < len(tasks) else iter(())
                    a_alive = b_alive = True
                    while a_alive or b_alive:
                        for _rep in range(2):
                            if a_alive:
                                try:
                                    next(a)
                                except StopIteration:
                                    a_alive = False
                        if b_alive:
                            try:
                                next(b)
                            except StopIteration:
                                b_alive = False
                    if k % 16 == 15:
                        S.flush()
                S.dma(COEFS16[0:T, :].rearrange("(q p) h -> p q h", p=128), coefT[:], reads=['coefT'], writes=['COEFS'], queue='pool')
                S.flush()
            if debug and 'YD0' in dbg:
                with contextlib.ExitStack() as st:
                    def dump(name, src, shape, dt):
                        t = st.enter_context(SBT("dbgt_" + name, shape, dt))
                        S.dma(t[:], src, writes=[name])
                        S.dma(dbg[name], t[:], reads=[name], writes=[name + 'o'])
                    dump('YD0', YD[0][0:T, :].rearrange("(s p) c -> p s c", p=128), [128, T // 128, RW], F32)
                    dump('YD1', YD[1][0:T, :].rearrange("(s p) c -> p s c", p=128), [128, T // 128, RW], F32)
                    dump('CF', COEFS16[0:T, :].rearrange("(s p) c -> p s c", p=128), [128, T // 128, 16], F32)
                    S.flush()
                continue

            NKT = T // 128
            with contextlib.ExitStack() as st:
                def T_(name, shape, dt):
                    return st.enter_context(SBT(name, list(shape), dt))

                def PS(name, shape, dt):
                    return st.enter_context(PST(name, list(shape), dt))
                Kt = T_("C_Kt", [64, 2, T], BF16)
                Va = T_("C_Va", [128, NKT, 2, 65], BF16)
                Qg = [T_("C_Qg%d" % i, [64, 4, 128], BF16) for i in range(2)]
                Pt = [T_("C_Pt%d" % i, [128, 512], BF16) for i in range(2)]
                ob = [T_("C_ob%d" % i, [128, 4, 64], BF16) for i in range(2)]
                rec = T_("C_rec", [128, 4], F32)
                psS = [PS("C_psS%d" % i, [128, 512], F32) for i in range(2)]
                psO = [PS("C_psO%d" % i, [128, 512], F32) for i in range(4)]
                if QUART:
                    for hp in range(4):
                        S.dma(QTO[hp][:, 0:NQ], (lambda eng, hp=hp: QT[hp][:, bass.ds(qoff(eng), NQ)]), writes=['QTO'])
                        if hp % 2 == 1:
                            S.flush()
                    QTv, YBv = QTO, YBO
                else:
                    QTv, YBv = QT, YB
                for kv in range(2):
                    S.dma(Kt[:, kv, :], KTA[kv * 64:(kv + 1) * 64, 0:T], writes=['Kt'])
                S.op('pool', lambda e: e.memset(Va[:, :, :, 64:65], 1.0), writes=['Va1'])
                for kv in range(2):
                    S.dma(Va[:, :, kv, 0:64], VA[0:T, kv * 64:(kv + 1) * 64].rearrange("(k p) j -> p k j", p=128), writes=['Va'])
                it = 0
                for kv in range(2):
                    for qt in range(NQ // 128):
                        qb = it % 2
                        it += 1
                        qcols = slice(qt * 128, (qt + 1) * 128)
                        for g in range(4):
                            h = kv * 4 + g
                            S.dma(Qg[qb][:, g, :], QTv[h // 2][(h % 2) * 64:(h % 2 + 1) * 64, qt * 128:(qt + 1) * 128], reads=['QTO'], writes=['Qg%d' % qb])
                        def qk(kt):
                            sb = kt % 2
                            S.op('pe', lambda e, sb=sb, kt=kt, kv=kv, qb=qb: e.matmul(out=psS[sb][:], lhsT=Kt[:, kv, kt * 128:(kt + 1) * 128], rhs=Qg[qb][:].rearrange("p g q -> p (g q)"), start=True, stop=True),
                                 reads=['Kt', 'Qg%d' % qb], writes=['psS%d' % sb])
                        qk(0)
                        for kt in range(NKT):
                            sb = kt % 2
                            if kt + 1 < NKT:
                                qk(kt + 1)
                            S.op('act', lambda e, sb=sb: e.activation(out=Pt[sb][:], in_=psS[sb][:], func=AF.Exp, scale=0.125), reads=['psS%d' % sb], writes=['Pt%d' % sb])
                            for g in range(4):
                                S.op('pe', lambda e, sb=sb, g=g, kt=kt, kv=kv: e.matmul(out=psO[g][:, 0:65], lhsT=Pt[sb][:, g * 128:(g + 1) * 128], rhs=Va[:, kt, kv, :], start=(kt == 0), stop=(kt == NKT - 1)),
                                     reads=['Pt%d' % sb, 'Va', 'Va1'], writes=['psO%d' % g])
                        for g in range(4):
                            S.op('dve', lambda e, g=g: e.reciprocal(out=rec[:, g:g + 1], in_=psO[g][:, 64:65]), reads=['psO%d' % g], writes=['rec%d' % g])
                            S.op('act', lambda e, g=g, qb=qb: e.activation(out=ob[qb][:, g, :], in_=psO[g][:, 0:64], func=AF.Copy, scale=rec[:, g:g + 1]), reads=['psO%d' % g, 'rec%d' % g], writes=['ob%d' % qb])
                        S.dma(YBv[qt * 128:(qt + 1) * 128, kv * 256:(kv + 1) * 256], ob[qb][:].rearrange("p g j -> p (g j)"), reads=['ob%d' % qb], writes=['YB'], queue='pool')
                    S.flush()

            with contextlib.ExitStack() as st:
                def T_(name, shape, dt):
                    return st.enter_context(SBT(name, list(shape), dt))
                P = alloc_ffn(st, "D_")
                xt = T_("D_xt", [128, 4, D], F32)
                h2T = T_("D_h2T", [128, KC, 512], BF16)
                wba = T_("D_wba", [128, 4, D], BF16)
                wbb = T_("D_wbb", [128, 4, D], BF16)
                wout = T_("D_wout", [128, KC, D], BF16)
                S.dma(wba[:], WBA.rearrange("f p c -> p f c"), writes=['wba'])
                S.dma(wbb[:], WBB.rearrange("f p c -> p f c"), writes=['wbb'])
                S.dma(wout[:], WOUT.rearrange("f p c -> p f c"), writes=['wout'])
                lnw = T_("D_lnw", [128, 2, RW], F32)
                S.dma(lnw[:, 0, :], lnx_w.partition_broadcast(128), writes=['lnw'])
                S.dma(lnw[:, 1, :], lnx_b.partition_broadcast(128), writes=['lnw'])
                bgT = T_("D_bgT", [128, 16], F32)
                S.dma(bgT[:], b_gate.rearrange("(o p) -> p o", p=128), writes=['bgT'], allow_slow_non_contiguous=True)
                yd = [T_("D_yd%d" % i, [128, 8, 64], F32) for i in range(2)]
                gt = T_("D_gt", [128, 8, 64], F32)
                vt = T_("D_vt", [128, 8, 64], BF16)
                ybt = T_("D_ybt", [128, RW], BF16)
                cf = T_("D_cf", [128, 16], F32)
                st8 = T_("D_st8", [128, 6, 8], F32)
                yab = T_("D_yab", [128, RW], BF16)
                sga = [T_("D_sga%d" % i, [128, 512], F32) for i in range(2)]
                mtmp = [T_("D_mtmp%d" % i, [128, 512], F32) for i in range(2)]
                tA = P['junk'][:, 0:512].rearrange("p (h j) -> p h j", j=64)
                tB = P['junk'][:, 512:1024].rearrange("p (h j) -> p h j", j=64)
                tC = P['tmpn'][:, 0:512].rearrange("p (h j) -> p h j", j=64)
                yaT = P['actT'][:, 8:12, :]
                ybT = P['actT'][:, 12:16, :]
                mT = P['actT'][:, 0:8, :]

                def bc8(ap):
                    return ap.unsqueeze(2).broadcast_to([128, 8, 64])
                if QUART:
                    jobs = [(X1O[0:NQ, :], lambda eng: X1[bass.ds(qoff(eng), NQ), :]),
                            (H2O[:, :, 0:NQ], lambda eng: H2[:, :, bass.ds(qoff(eng) + 1, NQ)]),
                            (YDO[0][0:NQ, :], lambda eng: YD[0][bass.ds(qoff(eng), NQ), :]),
                            (YDO[1][0:NQ, :], lambda eng: YD[1][bass.ds(qoff(eng), NQ), :]),
                            (GO[0:NQ, :], lambda eng: G_TM[bass.ds(qoff(eng), NQ), :]),
                            (VO[0:NQ, :], lambda eng: V_TM[bass.ds(qoff(eng), NQ), :]),
                            (CFO[0:NQ, :], lambda eng: COEFS16[bass.ds(qoff(eng), NQ), :])]
                    for ji, (dst, src) in enumerate(jobs):
                        S.dma(dst, src, writes=['slab'])
                        if ji % 2 == 1:
                            S.flush()
                    S.flush()
                    X1v, H2v, YDv, Gv, Vv, CFv, YBv = X1O, H2O, YDO, GO, VO, CFO, YBO
                else:
                    X1v, H2v, YDv, Gv, Vv, CFv, YBv = X1, H2[:, :, 1:T + 1], YD, G_TM, V_TM, COEFS16, YB
                for ti in range(NQ // 512):
                    S.dma(xt[:], X1v[ti * 512:(ti + 1) * 512, :].rearrange("(s p) d -> p s d", p=128), writes=['xt'])
                    S.dma(h2T[:], H2v[:, :, ti * 512:(ti + 1) * 512], writes=['h2T'])
                    for sub in range(4):
                        rows = slice(ti * 512 + sub * 128, ti * 512 + (sub + 1) * 128)
                        S.dma(yd[0][:].rearrange("p h j -> p (h j)"), YDv[0][rows, :], writes=['yd0'])
                        S.dma(yd[1][:].rearrange("p h j -> p (h j)"), YDv[1][rows, :], writes=['yd1'])
                        S.dma(gt[:].rearrange("p h j -> p (h j)"), Gv[rows, :], writes=['gt'])
                        S.dma(vt[:].rearrange("p h j -> p (h j)"), Vv[rows, :], writes=['vt'])
                        S.dma(ybt[:], YBv[rows, :], writes=['ybt'])
                        S.dma(cf[:], CFv[rows, :], writes=['cf'])
                        S.op('dve', lambda e: e.tensor_tensor(out=yd[0][:], in0=yd[0][:], in1=yd[1][:], op=ALU.add), reads=['yd0', 'yd1'], writes=['yd0'])
                        S.op('dve', lambda e: e.tensor_reduce(out=st8[:, 0, :], in_=yd[0][:], axis=AX.X, op=ALU.add), reads=['yd0'], writes=['st8_0'])
                        S.op('dve', lambda e: e.tensor_scalar(out=st8[:, 1, :], in0=st8[:, 0, :], scalar1=1.0 / 64, scalar2=None, op0=ALU.mult), reads=['st8_0'], writes=['st8_1'])
                        S.op('dve', lambda e: e.tensor_tensor(out=tA, in0=yd[0][:], in1=bc8(st8[:, 1, :]), op=ALU.subtract), reads=['yd0', 'st8_1'], writes=['junkA'])
                        S.op('act', lambda e: e.activation(out=tB, in_=tA, func=AF.Square), reads=['junkA'], writes=['junkB'])
                        S.op('dve', lambda e: e.tensor_reduce(out=st8[:, 2, :], in_=tB, axis=AX.X, op=ALU.add), reads=['junkB'], writes=['st8_2'])
                        S.op('dve', lambda e: e.tensor_scalar(out=st8[:, 3, :], in0=st8[:, 2, :], scalar1=1.0 / 64, scalar2=LNX_EPS, op0=ALU.mult, op1=ALU.add), reads=['st8_2'], writes=['st8_3'])
                        S.op('pool', lambda e: e.tensor_tensor(out=st8[:, 4, :], in0=st8[:, 3, :], in1=mhalf[:, 0:8], op=ALU.pow), reads=['st8_3', 'mhalf'], writes=['st8_4'])
                        S.op('dve', lambda e: e.tensor_tensor(out=tB, in0=tA, in1=bc8(st8[:, 4, :]), op=ALU.mult), reads=['junkA', 'st8_4'], writes=['junkB'])
                        S.op('dve', lambda e: e.tensor_tensor(out=tA, in0=tB, in1=lnw[:, 0, :].rearrange("p (h j) -> p h j", j=64), op=ALU.mult), reads=['junkB', 'lnw'], writes=['junkA'])
                        S.op('dve', lambda e: e.tensor_tensor(out=tA, in0=tA, in1=lnw[:, 1, :].rearrange("p (h j) -> p h j", j=64), op=ALU.add), reads=['junkA', 'lnw'], writes=['junkA'])
                        S.op('dve', lambda e: e.tensor_tensor(out=st8[:, 5, :], in0=cf[:, 0:8], in1=cf[:, 8:16], op=ALU.add), reads=['cf'], writes=['st8_5'])
                        S.op('dve', lambda e: e.tensor_tensor(out=tC, in0=vt[:], in1=bc8(st8[:, 5, :]), op=ALU.mult), reads=['vt', 'st8_5'], writes=['tmpnC'])
                        S.op('dve', lambda e: e.tensor_tensor(out=tA, in0=tA, in1=tC, op=ALU.add), reads=['junkA', 'tmpnC'], writes=['junkA'])
                        S.op('dve', lambda e: e.tensor_tensor(out=yab[:].rearrange("p (h j) -> p h j", j=64), in0=tA, in1=gt[:], op=ALU.mult), reads=['junkA', 'gt'], writes=['yab'])
                        for kc in range(4):
                            S.op('pe', lambda e, kc=kc: e.transpose(out=P['pst'][:, kc, :], in_=yab[:, kc * 128:(kc + 1) * 128], identity=ident[:]), reads=['yab', 'ident'], writes=['pst'])
                            S.op('pe', lambda e, kc=kc: e.transpose(out=P['pst'][:, 4 + kc, :], in_=ybt[:, kc * 128:(kc + 1) * 128], identity=ident[:]), reads=['ybt', 'ident'], writes=['pst'])
                        S.op('dve', lambda e, sub=sub: e.tensor_copy(out=yaT[:, :, sub * 128:(sub + 1) * 128], in_=P['pst'][:, 0:4, :]), reads=['pst'], writes=['actT'])
                        S.op('act', lambda e, sub=sub: e.activation(out=ybT[:, :, sub * 128:(sub + 1) * 128], in_=P['pst'][:, 4:8, :], func=AF.Copy), reads=['pst'], writes=['actT'])
                    for oc in range(8):
                        sl = oc % 3
                        S.dma(P['wg'][sl][:], WIN1[:, :, 2688 + oc * 128:2688 + (oc + 1) * 128].rearrange("k p c -> p k c"), writes=['wg%d' % sl])
                        S.dma(P['wu'][sl][:], WIN1[:, :, 2688 + 1024 + oc * 128:2688 + 1024 + (oc + 1) * 128].rearrange("k p c -> p k c"), writes=['wu%d' % sl])
                        ocs = slice(oc * 128, (oc + 1) * 128)
                        for kc in range(KC):
                            S.op('pe', lambda e, kc=kc, sl=sl: e.matmul(out=P['psg'][0][:], lhsT=P['wg'][sl][:, kc, :], rhs=h2T[:, kc, :], start=(kc == 0), stop=(kc == KC - 1)), reads=['wg%d' % sl, 'h2T'], writes=['psg0'])
                        for kc in range(KC):
                            S.op('pe', lambda e, kc=kc, sl=sl: e.matmul(out=P['psg'][1][:], lhsT=P['wu'][sl][:, kc, :], rhs=h2T[:, kc, :], start=(kc == 0), stop=(kc == KC - 1)), reads=['wu%d' % sl, 'h2T'], writes=['psg1'])
                        for kc in range(4):
                            S.op('pe', lambda e, kc=kc, ocs=ocs: e.matmul(out=P['psu'][0][:], lhsT=wba[:, kc, ocs], rhs=yaT[:, kc, :], start=(kc == 0), stop=(kc == 3)), reads=['wba', 'actT'], writes=['psu0'])
                        for kc in range(4):
                            S.op('pe', lambda e, kc=kc, ocs=ocs: e.matmul(out=P['psu'][1][:], lhsT=wbb[:, kc, ocs], rhs=ybT[:, kc, :], start=(kc == 0), stop=(kc == 3)), reads=['wbb', 'actT'], writes=['psu1'])
                        S.op('act', lambda e, oc=oc: e.activation(out=sga[0][:], in_=P['psg'][0][:], func=AF.Sigmoid, bias=bgT[:, oc:oc + 1]), reads=['psg0', 'bgT'], writes=['sga0'])
                        S.op('act', lambda e, oc=oc: e.activation(out=sga[1][:], in_=P['psg'][1][:], func=AF.Sigmoid, bias=bgT[:, 8 + oc:9 + oc]), reads=['psg1', 'bgT'], writes=['sga1'])
                        S.op('dve', lambda e: e.tensor_tensor(out=mtmp[0][:], in0=P['psu'][0][:], in1=sga[0][:], op=ALU.mult), reads=['psu0', 'sga0'], writes=['mtmp0'])
                        S.op('dve', lambda e: e.tensor_tensor(out=mtmp[1][:], in0=P['psu'][1][:], in1=sga[1][:], op=ALU.mult), reads=['psu1', 'sga1'], writes=['mtmp1'])
                        S.op('pool', lambda e, oc=oc: e.tensor_tensor(out=mT[:, oc, :], in0=mtmp[0][:], in1=mtmp[1][:], op=ALU.add), reads=['mtmp0', 'mtmp1'], writes=['actT'])
                    for sub in range(4):
                        for hf in range(2):
                            for kc in range(KC):
                                S.op('pe', lambda e, kc=kc, sub=sub, hf=hf: e.matmul(out=P['psd'][hf][:], lhsT=mT[:, kc, sub * 128:(sub + 1) * 128], rhs=wout[:, kc, hf * 512:(hf + 1) * 512], start=(kc == 0), stop=(kc == KC - 1)),
                                     reads=['actT', 'wout'], writes=['psd%d' % hf])
                            S.op('act', lambda e, hf=hf: e.activation(out=P['junk'][:, 0:512], in_=P['psd'][hf][:], func=AF.Square, accum_out=P['ssd'][:, hf:hf + 1]),
                                 reads=['psd%d' % hf], writes=['junk', 'junkA', 'ssd'])
                        post_norm_res(S, P, [P['psd'][0], P['psd'][1]], ['psd0', 'psd1'], gpost[1], xt, 'xt', xt, 'xt', sub)
                    for sub in range(4):
                        norm_T(S, P, xt, 'xt', sub, h2T, 'h2T')
                    ffn(S, P, 1, h2T, 'h2T', xt, 'xt', gpost[2], xt, 'xt')
                    S.dma(yout[ti * 512:(ti + 1) * 512, :].rearrange("(s p) d -> p s d", p=128), xt[:], reads=['xt'], writes=['yout'], queue='pool')
                    if ti % 4 == 3 and ti + 1 < NQ // 512:
                        S.flush()
                S.flush()
    return nc


_NC_CACHE = {}


def kernel(x_prompt, x_sample, norm_g, ffn_w_gate, ffn_w_up, ffn_w_down, w_in, mu_shift,
           w0, w_lora_up, a0, a_lora_up, g_lora_up, k_k, k_a, r_k, lnx_w, lnx_b,
           qk_norm_g, w_branch_a, w_branch_b, b_gate, w_out):
    f = lambda a: np.ascontiguousarray(np.asarray(a, dtype=np.float32))
    x_prompt = f(x_prompt)
    x_sample = f(x_sample)
    NP, TP, _ = x_prompt.shape
    NSMP, TS, _ = x_sample.shape
    n = 8
    per = NSMP // n
    seq_lens = [TP] + [TS] * per
    key = tuple(seq_lens)
    if key not in _NC_CACHE:
        _NC_CACHE[key] = build_nc(seq_lens, quarter=(0,))
    nc = _NC_CACHE[key]
    shared = {
        "norm_g": f(norm_g)[0], "ffn_w_gate": f(ffn_w_gate)[0], "ffn_w_up": f(ffn_w_up)[0],
        "ffn_w_down": f(ffn_w_down)[0], "w_in": f(w_in)[0], "mu_shift": f(mu_shift)[0],
        "w0": f(w0)[0], "w_lora_up": f(w_lora_up)[0], "a0": f(a0)[0], "a_lora_up": f(a_lora_up)[0],
        "g_lora_up": f(g_lora_up)[0], "k_k": f(k_k)[0], "k_a": f(k_a)[0], "r_k": f(r_k)[0].reshape(RW),
        "lnx_w": f(lnx_w)[0], "lnx_b": f(lnx_b)[0], "qk_norm_g": f(qk_norm_g)[0],
        "w_branch_a": f(w_branch_a)[0], "w_branch_b": f(w_branch_b)[0],
        "b_gate": f(b_gate)[0].reshape(2 * D), "w_out": f(w_out)[0],
    }
    for k_, v_ in _consts().items():
        shared["c_" + k_] = v_
    for T_ in sorted(set(seq_lens)):
        c_, s_ = _rope_tables(T_)
        shared["cos%d" % T_] = c_
        shared["sin%d" % T_] = s_
    in_maps = []
    cores_per_prompt = n // NP
    for c in range(n):
        m = dict(shared)
        m["x0"] = x_prompt[c // cores_per_prompt]
        for j in range(per):
            m["x%d" % (j + 1)] = x_sample[c * per + j]
        in_maps.append(m)
    res = run_bass_kernel_spmd(nc, in_maps, core_ids=list(range(n)))
    y_prompt = np.stack([np.concatenate([res.results[p * cores_per_prompt + q]["y0"] for q in range(cores_per_prompt)], axis=0)
                         for p in range(NP)], axis=0).astype(np.float32)
    y_sample = np.stack([res.results[c]["y%d" % (j + 1)] for c in range(n) for j in range(per)], axis=0).astype(np.float32)
    return (y_prompt, y_sample)
```

```python
import contextlib
import numpy as np
import ml_dtypes
import concourse.bass as bass
import concourse.mybir as mybir
from concourse.bass_utils import run_bass_kernel_spmd

F32 = mybir.dt.float32
BF16 = mybir.dt.bfloat16
AF = mybir.ActivationFunctionType
ALU = mybir.AluOpType
AX = mybir.AxisListType

D = 1024
KC = 8
FF = 2816
FC = 22
NIN = 4736
RW = 512
NRW = 1920
EPS = 1e-6
LNX_EPS = 64e-5
CH = 64
NS_CAP = 1024
NS_CAP_B = 512


class Sched:
    ENGS = ('pe', 'act', 'dve', 'pool', 'sp')
    NLANES = 16

    def __init__(self, nc, st):
        self.nc = nc
        self.sems = {}
        self.st = st
        self.cnt = {e: 0 for e in self.ENGS}
        self.lane_cnt = {}
        self.lane_rr = {e: 0 for e in self.ENGS}
        self.nflush = 0
        self._reset()

    def _reset(self):
        self.q = {e: [] for e in self.ENGS}
        self.lw = {}
        self.rd = {}
        self.seen = {e: {} for e in self.ENGS}
        self.signaled = {e: set() for e in self.ENGS}

    def _sem(self, name):
        if name not in self.sems:
            self.sems[name] = self.st.enter_context(self.nc.semaphore(name))
        return self.sems[name]

    def _deps(self, reads, writes):
        toks = set()
        for k in reads:
            t = self.lw.get(k)
            if t is not None:
                toks.add(t)
        for k in writes:
            t = self.lw.get(k)
            if t is not None:
                toks.add(t)
            for r in self.rd.get(k, {}).values():
                toks.add(r)
        return toks

    def _commit(self, tok, reads, writes):
        src = tok[1]
        for k in reads:
            self.rd.setdefault(k, {})[src] = tok
        for k in writes:
            self.lw[k] = tok
            self.rd[k] = {}

    def _filter(self, eng, toks):
        best = {}
        for t in toks:
            kind, src, n = t
            if kind == 'c' and src == 'pe' and eng == 'pe':
                continue
            if self.seen[eng].get(src, 0) >= n:
                continue
            if src not in best or best[src][2] < n:
                best[src] = t
        for t in best.values():
            self.seen[eng][t[1]] = t[2]
            if t[0] == 'c':
                self.signaled[t[1]].add(t[2])
        return list(best.values())

    def op(self, eng, fn, reads=(), writes=()):
        toks = self._deps(reads, writes)
        waits = self._filter(eng, toks)
        idx = len(self.q[eng]) + 1
        self.q[eng].append(dict(fn=fn, waits=waits, idx=idx, dma=None))
        self._commit(('c', eng, idx), reads, writes)

    def dma(self, out, in_, reads=(), writes=(), queue='sp', **kw):
        toks = self._deps(reads, writes)
        li = self.lane_rr[queue]
        self.lane_rr[queue] = (li + 1) % self.NLANES
        lane = 'L_%s_%d' % (queue, li)
        c = self.lane_cnt.get(lane, 0)
        if c > 0:
            toks.add(('d', lane, 16 * c))
        self.lane_cnt[lane] = c + 1
        waits = self._filter(queue, toks)
        idx = len(self.q[queue]) + 1
        self.q[queue].append(dict(fn=None, waits=waits, idx=idx, dma=(out, in_, lane, kw)))
        self._commit(('d', lane, 16 * (c + 1)), reads, writes)

    def coll(self, kind, ins, outs, groups, reads=(), writes=()):
        queue = 'pool'
        toks = self._deps(reads, writes)
        lane = 'L_coll'
        c = self.lane_cnt.get(lane, 0)
        if c > 0:
            toks.add(('d', lane, 16 * c))
        self.lane_cnt[lane] = c + 1
        waits = self._filter(queue, toks)
        idx = len(self.q[queue]) + 1
        fn = lambda e: e.collective_compute(kind, ALU.bypass, replica_groups=groups, ins=ins, outs=outs)
        self.q[queue].append(dict(fn=None, waits=waits, idx=idx, dma=(fn, None, lane, None)))
        self._commit(('d', lane, 16 * (c + 1)), reads, writes)

    def flush(self):
        nc = self.nc
        toks = set(('d', lane, 16 * c) for lane, c in self.lane_cnt.items())
        waits = self._filter('sp', toks)
        self.q['sp'].append(dict(fn=None, waits=waits, idx=len(self.q['sp']) + 1, dma=None))
        cmap = {}
        for e in self.ENGS:
            m = {}
            c = self.cnt[e]
            for i in sorted(self.signaled[e]):
                c += 1
                m[i] = c
            cmap[e] = m
            self.cnt[e] = c
            self._sem('s_' + e)
        for lane in self.lane_cnt:
            self._sem(lane)
        sems = self.sems
        engobj = {'pe': 'tensor', 'act': 'scalar', 'dve': 'vector', 'pool': 'gpsimd', 'sp': 'sync'}
        q = self.q

        def make(e):
            def body(eng):
                for o in q[e]:
                    for (kind, src, n) in o['waits']:
                        if kind == 'c':
                            eng.wait_ge(sems['s_' + src], cmap[src][n])
                        else:
                            eng.wait_ge(sems[src], n)
                    if o['dma'] is not None:
                        out, in_, lane, kw = o['dma']
                        if kw is None:
                            out(eng).then_inc(sems[lane], 16)
                        else:
                            if callable(out):
                                out = out(eng)
                            if callable(in_):
                                in_ = in_(eng)
                            try:
                                eng.dma_start(out=out, in_=in_, **kw).then_inc(sems[lane], 16)
                            except Exception:
                                print("DMA FAIL", e, lane, getattr(out, 'shape', None), getattr(in_, 'shape', None), out, in_)
                                raise
                    elif o['fn'] is not None:
                        ins = o['fn'](eng)
                        if o['idx'] in cmap[e]:
                            ins.then_inc(sems['s_' + e], 1)
            return body

        with nc.Block() as block:
            for e in self.ENGS:
                if q[e]:
                    getattr(block, engobj[e])(make(e))
        self.nflush += 1
        self._reset()


def _consts():
    c = {}
    c['ident'] = np.eye(128, dtype=np.float32)
    s = np.arange(64)[:, None]
    t = np.arange(64)[None, :]
    su = (s < t).astype(np.float32)
    iu = (s <= t).astype(np.float32)
    mf = np.block([[su, iu], [su, iu]])
    sl = (s > t).astype(np.float32)
    il = (s >= t).astype(np.float32)
    mb = np.block([[sl, il], [sl, il]])
    c['maskf'] = mf
    c['maskb'] = mb
    nt = np.zeros((2, 128, 64), np.float32)
    nt[0, :64] = (t.T > s.T).astype(np.float32)
    nt[1, :64] = (t.T < s.T).astype(np.float32)
    c['masknt'] = nt
    bo = np.zeros((128, 128), np.float32)
    bo[:64, :64] = 1.0
    bo[64:, 64:] = 1.0
    c['blockones'] = bo
    hs = np.zeros((128, 2), np.float32)
    hs[:64, 0] = 1.0
    hs[64:, 1] = 1.0
    c['headsel'] = hs
    return c


def _rope_tables(T):
    rows = T // 64
    r_idx = np.repeat(np.arange(rows, dtype=np.float32), 64)
    c_idx = np.tile(np.arange(64, dtype=np.float32), rows)
    inv = (10000.0 ** (-np.arange(0, 32, 2, dtype=np.float32) / 32)).astype(np.float32)
    ang = np.concatenate([r_idx[:, None] * inv, c_idx[:, None] * inv], axis=-1)
    return np.cos(ang).astype(np.float32), np.sin(ang).astype(np.float32)


def build_nc(seq_lens, debug=None, quarter=()):
    nc = bass.Bass("TRN2", target_bir_lowering=False)
    TMAX = max(seq_lens)
    NSEQ = len(seq_lens)
    TT = sum(seq_lens)

    def din(name, shape, dt=F32):
        return nc.dram_tensor(name, list(shape), dt, kind="ExternalInput").ap()

    def dscr(name, shape, dt):
        return nc.dram_tensor(name, list(shape), dt, kind="Internal").ap()

    xs_in = [din("x%d" % i, [T, D]) for i, T in enumerate(seq_lens)]
    ys_out = [nc.dram_tensor("y%d" % i, [T // 4 if i in quarter else T, D], F32, kind="ExternalOutput").ap()
              for i, T in enumerate(seq_lens)]
    cos_in = {T: din("cos%d" % T, [T, 32]) for T in sorted(set(seq_lens))}
    sin_in = {T: din("sin%d" % T, [T, 32]) for T in sorted(set(seq_lens))}
    norm_g = din("norm_g", [6, D])
    w_gate = din("ffn_w_gate", [2, D, FF])
    w_up = din("ffn_w_up", [2, D, FF])
    w_down = din("ffn_w_down", [2, FF, D])
    w_in = din("w_in", [D, NIN])
    mu = din("mu_shift", [NRW])
    w0 = din("w0", [2, RW])
    wl_up = din("w_lora_up", [2, 64, RW])
    a0 = din("a0", [2, RW])
    al_up = din("a_lora_up", [2, 64, RW])
    gl_up = din("g_lora_up", [128, RW])
    k_k = din("k_k", [RW])
    k_a = din("k_a", [RW])
    r_k = din("r_k", [RW])
    lnx_w = din("lnx_w", [RW])
    lnx_b = din("lnx_b", [RW])
    qk_g = din("qk_norm_g", [2, 64])
    w_ba = din("w_branch_a", [RW, D])
    w_bb = din("w_branch_b", [RW, D])
    b_gate = din("b_gate", [2 * D])
    w_out = din("w_out", [D, D])
    C = _consts()
    cin = {k: din("c_" + k, v.shape) for k, v in C.items()}

    dbg = {}
    if debug:
        for name, shape, dt in debug:
            dbg[name] = nc.dram_tensor("dbg_" + name, list(shape), dt, kind="ExternalOutput").ap()

    WG = [dscr("WG%d" % l, [FC, 128, KC, 128], BF16) for l in range(2)]
    WU = [dscr("WU%d" % l, [FC, 128, KC, 128], BF16) for l in range(2)]
    WD = [dscr("WD%d" % l, [FC, 128, D], BF16) for l in range(2)]
    WIN1 = dscr("WIN1", [KC, 128, NIN], BF16)
    WIN2 = dscr("WIN2", [KC, 128, NRW], BF16)
    WBA = dscr("WBA", [4, 128, D], BF16)
    WBB = dscr("WBB", [4, 128, D], BF16)
    WOUT = dscr("WOUT", [KC, 128, D], BF16)
    X1 = dscr("X1", [TMAX, D], F32)
    H2 = dscr("H2", [128, KC, TMAX + 2], BF16)
    R_T = dscr("R_T", [4, 128, TMAX], F32)
    K_T = dscr("K_T", [4, 128, TMAX], F32)
    LW = dscr("LW", [2, 4, 128, TMAX], F32)
    AA = dscr("AA", [2, 4, 128, TMAX], F32)
    V_TM = dscr("V_TM", [TMAX, RW], BF16)
    VF_TM = dscr("VF_TM", [TMAX, RW], F32)
    G_TM = dscr("G_TM", [TMAX, RW], F32)
    QT = dscr("QT", [4, 128, TMAX], BF16)
    KTA = dscr("KTA", [128, TMAX], BF16)
    VA = dscr("VA", [TMAX, 128], BF16)
    YD = [dscr("YD%d" % d, [TMAX, RW], F32) for d in range(2)]
    COEF = [dscr("COEF%d" % d, [TMAX, 8], F32) for d in range(2)]
    YB = dscr("YB", [TMAX, RW], BF16)
    COEFS16 = dscr("COEFS16", [TMAX, 16], F32)
    TQ = max([seq_lens[i] // 4 for i in quarter] + [128])
    QTO = dscr("QTO", [4, 128, TQ], BF16)
    YBO = dscr("YBO", [TQ, RW], BF16)
    X1O = dscr("X1O", [TQ, D], F32)
    H2O = dscr("H2O", [128, KC, TQ], BF16)
    YDO = [dscr("YDO%d" % d, [TQ, RW], F32) for d in range(2)]
    GO = dscr("GO", [TQ, RW], F32)
    VO = dscr("VO", [TQ, RW], BF16)
    CFO = dscr("CFO", [TQ, 16], F32)

    SFX = [""]
    PIDC = {}

    def SBT(name, shape, dt):
        return nc.sbuf_tensor(name + SFX[0], shape, dt)

    def PST(name, shape, dt):
        return nc.psum_tensor(name + SFX[0], shape, dt)

    gst = contextlib.ExitStack()
    with gst:
        def GT(name, shape, dt):
            return gst.enter_context(SBT(name, list(shape), dt))

        S = Sched(nc, gst)
        ident_f = GT("ident_f", [128, 128], F32)
        ident = GT("ident", [128, 128], BF16)
        mhalf = GT("mhalf", [128, 16], F32)
        S.dma(ident_f[:], cin['ident'], writes=['ident_f'])
        S.op('dve', lambda e: e.tensor_copy(out=ident[:], in_=ident_f[:]), reads=['ident_f'], writes=['ident'])
        S.op('pool', lambda e: e.memset(mhalf[:], -0.5), writes=['mhalf'])
        gpost = [GT("gpost%d" % i, [128, D], F32) for i in range(3)]
        for i, (gi, sc) in enumerate([(1, 0.5), (3, 1.0), (5, 0.5)]):
            S.dma(gpost[i][:], norm_g[gi].partition_broadcast(128), writes=['gpost%d' % i])
            S.op('dve', lambda e, i=i, sc=sc: e.tensor_scalar(out=gpost[i][:], in0=gpost[i][:], scalar1=sc, scalar2=None, op0=ALU.mult),
                 reads=['gpost%d' % i], writes=['gpost%d' % i])
        S.flush()

        with contextlib.ExitStack() as st:
            def T_(name, shape, dt):
                return st.enter_context(SBT(name, list(shape), dt))
            gT = T_("gT", [128, 3, KC], F32)
            for i, gi in enumerate([0, 2, 4]):
                S.dma(gT[:, i, :], norm_g[gi].rearrange("(kc p) -> p kc", p=128), writes=['gT'],
                      allow_slow_non_contiguous=True)
            omm = T_("omm", [128, NRW], F32)
            hmu = T_("hmu", [128, NRW], F32)
            S.dma(omm[:], mu.partition_broadcast(128), writes=['omm'])
            S.op('dve', lambda e: e.tensor_scalar(out=hmu[:], in0=omm[:], scalar1=0.5, scalar2=None, op0=ALU.mult), reads=['omm'], writes=['hmu'])
            S.op('dve', lambda e: e.tensor_scalar(out=omm[:], in0=omm[:], scalar1=-1.0, scalar2=1.0, op0=ALU.mult, op1=ALU.add), reads=['omm', 'hmu'], writes=['omm'])
            stg = [T_("stg%d" % i, [128, NIN], F32) for i in range(2)]
            stb = [T_("stb%d" % i, [128, NIN], BF16) for i in range(2)]
            stb2 = [T_("stb2%d" % i, [128, NRW], BF16) for i in range(2)]
            cnt = [0]

            def cast_job(src_ap, ncols, dst_ap, gcol=None, rw=None, dst2_ap=None, fin=None, fout=None):
                i = cnt[0] % 2
                cnt[0] += 1
                sk, bk, b2k = 'stg%d' % i, 'stb%d' % i, 'stb2%d' % i
                sv = stg[i][:, 0:ncols]
                if fin:
                    sv = sv.rearrange("p (f c) -> p f c", c=fin)
                S.dma(sv, src_ap, writes=[sk])
                eng = 'act' if (cnt[0] % 2) else 'dve'
                if rw is None:
                    if gcol is None:
                        if eng == 'act':
                            S.op('act', lambda e: e.activation(out=stb[i][:, 0:ncols], in_=stg[i][:, 0:ncols], func=AF.Copy), reads=[sk], writes=[bk])
                        else:
                            S.op('dve', lambda e: e.tensor_copy(out=stb[i][:, 0:ncols], in_=stg[i][:, 0:ncols]), reads=[sk], writes=[bk])
                    else:
                        if eng == 'act':
                            S.op('act', lambda e: e.activation(out=stb[i][:, 0:ncols], in_=stg[i][:, 0:ncols], func=AF.Copy, scale=gcol), reads=[sk, 'gT'], writes=[bk])
                        else:
                            S.op('dve', lambda e: e.tensor_scalar(out=stb[i][:, 0:ncols], in0=stg[i][:, 0:ncols], scalar1=gcol, scalar2=None, op0=ALU.mult), reads=[sk, 'gT'], writes=[bk])
                else:
                    S.op('act', lambda e: e.activation(out=stb[i][:, NRW:ncols], in_=stg[i][:, NRW:ncols], func=AF.Copy, scale=gcol), reads=[sk, 'gT'], writes=[bk])
                    S.op('act', lambda e: e.activation(out=stg[i][:, 0:NRW], in_=stg[i][:, 0:NRW], func=AF.Copy, scale=gcol), reads=[sk, 'gT'], writes=[sk])
                    S.op('dve', lambda e: e.tensor_tensor(out=stb2[i][:], in0=stg[i][:, 0:NRW], in1=hmu[:], op=ALU.mult), reads=[sk, 'hmu'], writes=[b2k])
                    S.op('dve', lambda e: e.tensor_tensor(out=stb[i][:, 0:NRW], in0=stg[i][:, 0:NRW], in1=omm[:], op=ALU.mult), reads=[sk, 'omm'], writes=[bk])
                    S.dma(dst2_ap, stb2[i][:], reads=[b2k], writes=['wscr'], queue='pool')
                bv = stb[i][:, 0:ncols]
                if fout:
                    bv = bv.rearrange("p (f c) -> p f c", c=fout)
                S.dma(dst_ap, bv, reads=[bk], writes=['wscr'], queue='pool')

            for l in range(2):
                gi = 0 if l == 0 else 2
                for kc in range(KC):
                    rows = slice(kc * 128, (kc + 1) * 128)
                    for (Wsrc, Wdst) in ((w_gate, WG), (w_up, WU)):
                        cast_job(Wsrc[l, rows, :], FF,
                                 Wdst[l][:, :, kc, :].rearrange("f p c -> p f c"),
                                 gcol=gT[:, gi, kc:kc + 1], fout=128)
                for fc in range(0, FC, 4):
                    n = min(4, FC - fc)
                    cast_job(w_down[l, fc * 128:(fc + n) * 128, :].rearrange("(f p) c -> p f c", p=128), n * D,
                             WD[l][fc:fc + n].rearrange("f p c -> p f c"), fin=D, fout=D)
            for kc in range(KC):
                rows = slice(kc * 128, (kc + 1) * 128)
                cast_job(w_in[rows, :], NIN, WIN1[kc], gcol=gT[:, 1, kc:kc + 1], rw=True, dst2_ap=WIN2[kc])
            cast_job(w_ba.rearrange("(f p) c -> p f c", p=128), 4 * D, WBA.rearrange("f p c -> p f c"), fin=D, fout=D)
            cast_job(w_bb.rearrange("(f p) c -> p f c", p=128), 4 * D, WBB.rearrange("f p c -> p f c"), fin=D, fout=D)
            for h in range(2):
                cast_job(w_out[h * 512:(h + 1) * 512, :].rearrange("(f p) c -> p f c", p=128), 4 * D,
                         WOUT[h * 4:(h + 1) * 4].rearrange("f p c -> p f c"), fin=D, fout=D)
            S.flush()

        def norm_T(S, P, xin, xin_key, sub, hT, hT_key):
            S.op('act', lambda e: e.activation(out=P['junk'][:], in_=xin[:, sub, :], func=AF.Square, accum_out=P['ss'][:, 0:1]),
                 reads=[xin_key], writes=['junk', 'ss'])
            S.op('dve', lambda e: e.tensor_scalar(out=P['ss'][:, 1:2], in0=P['ss'][:, 0:1], scalar1=1.0 / D, scalar2=EPS, op0=ALU.mult, op1=ALU.add),
                 reads=['ss'], writes=['ss1'])
            S.op('pool', lambda e: e.tensor_tensor(out=P['ss'][:, 2:3], in0=P['ss'][:, 1:2], in1=mhalf[:, 0:1], op=ALU.pow),
                 reads=['ss1', 'mhalf'], writes=['ss2'])
            S.op('act', lambda e: e.activation(out=P['xnb'][:], in_=xin[:, sub, :], func=AF.Copy, scale=P['ss'][:, 2:3]),
                 reads=[xin_key, 'ss2'], writes=['xnb'])
            for kc in range(KC):
                S.op('pe', lambda e, kc=kc: e.transpose(out=P['pst'][:, kc, :], in_=P['xnb'][:, kc * 128:(kc + 1) * 128], identity=ident[:]),
                     reads=['xnb', 'ident'], writes=['pst'])
            S.op('dve', lambda e: e.tensor_copy(out=hT[:, :, sub * 128:(sub + 1) * 128], in_=P['pst'][:]),
                 reads=['pst'], writes=[hT_key])

        def ffn(S, P, l, hT, hT_key, xres, xres_key, gp, xout, xout_key):
            S.dma(P['wd'][:], WD[l].rearrange("f p c -> p f c"), writes=['wd'])
            for fc in range(FC):
                sl = fc % 3
                S.dma(P['wg'][sl][:], WG[l][fc], writes=['wg%d' % sl])
                S.dma(P['wu'][sl][:], WU[l][fc], writes=['wu%d' % sl])
                b = fc % 2
                for kc in range(KC):
                    S.op('pe', lambda e, kc=kc, sl=sl, b=b: e.matmul(out=P['psg'][b][:], lhsT=P['wg'][sl][:, kc, :], rhs=hT[:, kc, :], start=(kc == 0), stop=(kc == KC - 1)),
                         reads=['wg%d' % sl, hT_key], writes=['psg%d' % b])
                for kc in range(KC):
                    S.op('pe', lambda e, kc=kc, sl=sl, b=b: e.matmul(out=P['psu'][b][:], lhsT=P['wu'][sl][:, kc, :], rhs=hT[:, kc, :], start=(kc == 0), stop=(kc == KC - 1)),
                         reads=['wu%d' % sl, hT_key], writes=['psu%d' % b])
                S.op('act', lambda e, b=b: e.activation(out=P['sg'][b][:], in_=P['psg'][b][:], func=AF.Silu),
                     reads=['psg%d' % b], writes=['sg%d' % b])
                S.op('dve', lambda e, b=b, fc=fc: e.tensor_tensor(out=P['actT'][:, fc, :], in0=P['psu'][b][:], in1=P['sg'][b][:], op=ALU.mult),
                     reads=['psu%d' % b, 'sg%d' % b], writes=['actT'])
            for sub in range(4):
                bks = [(2 * sub + hf) % 3 for hf in range(2)]
                for hf in range(2):
                    bk = bks[hf]
                    for fc in range(FC):
                        S.op('pe', lambda e, fc=fc, sub=sub, hf=hf, bk=bk: e.matmul(out=P['psd'][bk][:], lhsT=P['actT'][:, fc, sub * 128:(sub + 1) * 128], rhs=P['wd'][:, fc, hf * 512:(hf + 1) * 512], start=(fc == 0), stop=(fc == FC - 1)),
                             reads=['actT', 'wd'], writes=['psd%d' % bk])
                    S.op('act', lambda e, hf=hf, bk=bk: e.activation(out=P['junk'][:, 0:512], in_=P['psd'][bk][:], func=AF.Square, accum_out=P['ssd'][:, hf:hf + 1]),
                         reads=['psd%d' % bk], writes=['junk', 'ssd'])
                post_norm_res(S, P, [P['psd'][bks[0]], P['psd'][bks[1]]], ['psd%d' % bks[0], 'psd%d' % bks[1]], gp, xres, xres_key, xout, xout_key, sub)

        def post_norm_res(S, P, ps, ps_keys, gp, xres, xres_key, xout, xout_key, sub):
            S.op('dve', lambda e: e.tensor_tensor(out=P['ssd'][:, 2:3], in0=P['ssd'][:, 0:1], in1=P['ssd'][:, 1:2], op=ALU.add),
                 reads=['ssd'], writes=['ssd2'])
            S.op('dve', lambda e: e.tensor_scalar(out=P['ssd'][:, 3:4], in0=P['ssd'][:, 2:3], scalar1=1.0 / D, scalar2=EPS, op0=ALU.mult, op1=ALU.add),
                 reads=['ssd2'], writes=['ssd3'])
            S.op('pool', lambda e: e.tensor_tensor(out=P['ssd'][:, 4:5], in0=P['ssd'][:, 3:4], in1=mhalf[:, 0:1], op=ALU.pow),
                 reads=['ssd3', 'mhalf'], writes=['ssd4'])
            for hf in range(2):
                cs = slice(hf * 512, (hf + 1) * 512)
                S.op('act', lambda e, hf=hf, cs=cs: e.activation(out=P['tmpn'][:, cs], in_=ps[hf][:], func=AF.Copy, scale=P['ssd'][:, 4:5]),
                     reads=[ps_keys[hf], 'ssd4'], writes=['tmpn%d' % hf])
                S.op('dve', lambda e, hf=hf, cs=cs: e.tensor_tensor(out=P['tmpn'][:, cs], in0=P['tmpn'][:, cs], in1=gp[:, cs], op=ALU.mult),
                     reads=['tmpn%d' % hf], writes=['tmpn%d' % hf])
                S.op('pool', lambda e, hf=hf, cs=cs: e.tensor_tensor(out=xout[:, sub, cs], in0=P['tmpn'][:, cs], in1=xres[:, sub, cs], op=ALU.add),
                     reads=['tmpn%d' % hf, xres_key], writes=[xout_key])

        def alloc_ffn(st, pfx):
            def T_(name, shape, dt):
                return st.enter_context(SBT(pfx + name, list(shape), dt))

            def PS(name, shape, dt):
                return st.enter_context(PST(pfx + name, list(shape), dt))
            P = {}
            P['junk'] = T_("junk", [128, D], F32)
            P['ss'] = T_("ss", [128, 4], F32)
            P['ssd'] = T_("ssd", [128, 8], F32)
            P['xnb'] = T_("xnb", [128, D], BF16)
            P['tmpn'] = T_("tmpn", [128, D], F32)
            P['wd'] = T_("wd", [128, FC, D], BF16)
            P['wg'] = [T_("wg%d" % i, [128, KC, 128], BF16) for i in range(3)]
            P['wu'] = [T_("wu%d" % i, [128, KC, 128], BF16) for i in range(3)]
            P['sg'] = [T_("sg%d" % i, [128, 512], F32) for i in range(2)]
            P['actT'] = T_("actT", [128, FC, 512], BF16)
            P['pst'] = PS("pst", [128, KC, 128], BF16)
            P['psg'] = [PS("psg%d" % i, [128, 512], F32) for i in range(2)]
            P['psu'] = [PS("psu%d" % i, [128, 512], F32) for i in range(2)]
            P['psd'] = [PS("psd%d" % i, [128, 512], F32) for i in range(3)]
            return P

        for si, T in enumerate(seq_lens):
            NT = T // 512
            SFX[0] = "_q%d" % si
            QUART = si in quarter
            NQ = T // 4 if QUART else T

            def qoff(eng, NQ=NQ):
                key = (str(eng.engine), S.nflush)
                if key not in PIDC:
                    PIDC[key] = eng.snap((eng.partition_id() % 4) * NQ)
                return PIDC[key]

            def rowsl(base, size):
                if not QUART:
                    return lambda eng: slice(base, base + size)
                return lambda eng: bass.ds(qoff(eng) + base, size)
            xin = xs_in[si]
            yout = ys_out[si]
            with contextlib.ExitStack() as st:
                def T_(name, shape, dt):
                    return st.enter_context(SBT(name, list(shape), dt))
                P = alloc_ffn(st, "A_")
                xres = [T_("A_xres%d" % i, [128, 4, D], F32) for i in range(2)]
                x1 = [T_("A_x1%d" % i, [128, 4, D], F32) for i in range(2)]
                hT = T_("A_hT", [128, KC, 512], BF16)
                h2T = [T_("A_h2T%d" % i, [128, KC, 512], BF16) for i in range(2)]
                zc = T_("A_zc", [128, KC, 1], BF16)
                S.op('pool', lambda e: e.memset(zc[:], 0.0), writes=['zc'])
                S.dma(H2[:, :, 0:1], zc[:], reads=['zc'], writes=['H2pad'], allow_slow_non_contiguous=True)
                S.dma(H2[:, :, T + 1:T + 2], zc[:], reads=['zc'], writes=['H2pad'], allow_slow_non_contiguous=True)
                for ti in range(NT):
                    b = ti % 2
                    xk, x1k, h2k = 'xres%d' % b, 'x1%d' % b, 'h2T%d' % b
                    S.dma(xres[b][:], xin[ti * 512:(ti + 1) * 512, :].rearrange("(s p) d -> p s d", p=128), writes=[xk])
                    for sub in range(4):
                        norm_T(S, P, xres[b], xk, sub, hT, 'hT')
                    ffn(S, P, 0, hT, 'hT', xres[b], xk, gpost[0], x1[b], x1k)
                    S.dma(X1[ti * 512:(ti + 1) * 512, :].rearrange("(s p) d -> p s d", p=128), x1[b][:], reads=[x1k], writes=['X1'], queue='pool')
                    for sub in range(4):
                        norm_T(S, P, x1[b], x1k, sub, h2T[b], h2k)
                    S.dma(H2[:, :, 1 + ti * 512:1 + (ti + 1) * 512], h2T[b][:], reads=[h2k], writes=['H2'], queue='pool')
                S.flush()
            if debug and 'X1' in dbg:
                with contextlib.ExitStack() as st:
                    t1 = st.enter_context(SBT("dbg_t1", [128, T // 128, D], F32))
                    S.dma(t1[:], X1[0:T, :].rearrange("(s p) d -> p s d", p=128), writes=['t1'])
                    S.dma(dbg['X1'].rearrange("(s p) d -> p s d", p=128), t1[:], reads=['t1'], writes=['o'])
                    t2 = st.enter_context(SBT("dbg_t2", [128, KC, T + 2], BF16))
                    S.dma(t2[:], H2[:, :, 0:T + 2], writes=['t2'])
                    S.dma(dbg['H2'], t2[:], reads=['t2'], writes=['o2'])
                    S.flush()
                continue

            with contextlib.ExitStack() as st:
                def T_(name, shape, dt):
                    return st.enter_context(SBT(name, list(shape), dt))

                def PS(name, shape, dt):
                    return st.enter_context(PST(name, list(shape), dt))
                win1 = T_("B_win1", [128, KC, NIN], BF16)
                win2 = T_("B_win2", [128, KC, NRW], BF16)
                for kc in range(KC):
                    S.dma(win1[:, kc, :], WIN1[kc], writes=['win1'])
                    S.dma(win2[:, kc, :], WIN2[kc], writes=['win2'])
                lstg = T_("B_lstg", [128, 3, RW], F32)
                lw_b = T_("B_lw_b", [128, 3, RW], BF16)
                S.dma(lstg[:, 0, :], wl_up.rearrange("d r c -> (d r) c"), writes=['lstg'])
                S.dma(lstg[:, 1, :], al_up.rearrange("d r c -> (d r) c"), writes=['lstg'])
                S.dma(lstg[:, 2, :], gl_up, writes=['lstg'])
                S.op('dve', lambda e: e.tensor_copy(out=lw_b[:], in_=lstg[:]), reads=['lstg'], writes=['lw_b'])
                w0T = T_("B_w0T", [128, 2, 4], F32)
                a0T = T_("B_a0T", [128, 2, 4], F32)
                S.dma(w0T[:], w0.rearrange("d (h p) -> p d h", p=128), writes=['w0T'], allow_slow_non_contiguous=True)
                S.dma(a0T[:], a0.rearrange("d (h p) -> p d h", p=128), writes=['a0T'], allow_slow_non_contiguous=True)
                g64 = T_("B_g64", [128, 2, 64], F32)
                S.dma(g64[:, 0, :], qk_g[0].partition_broadcast(128), writes=['g64'])
                S.dma(g64[:, 1, :], qk_g[1].partition_broadcast(128), writes=['g64'])
                gq = T_("B_gq", [128, 8, 64], F32)
                gk = T_("B_gk", [128, 2, 64], F32)
                S.op('dve', lambda e: e.tensor_copy(out=gq[:], in_=g64[:, 0:1, :].broadcast_to([128, 8, 64])), reads=['g64'], writes=['gq'])
                S.op('dve', lambda e: e.tensor_copy(out=gk[:], in_=g64[:, 1:2, :].broadcast_to([128, 2, 64])), reads=['g64'], writes=['gk'])
                hext = [T_("B_hext%d" % i, [128, KC, 514], BF16) for i in range(2)]
                hs = T_("B_hs", [128, KC, 512], BF16)
                fst = [T_("B_fst%d" % i, [128, 512], F32) for i in range(3)]
                tw = T_("B_tw", [128, 512], BF16)
                ta = T_("B_ta", [128, 512], BF16)
                tg = T_("B_tg", [128, 512], BF16)
                vst = [T_("B_vst%d" % i, [128, 512], BF16) for i in range(2)]
                cs = [T_("B_cs%d" % i, [128, 2, 32], F32) for i in range(2)]
                nq = T_("B_nq", [128, 8, 64], F32)
                nq2 = T_("B_nq2", [128, 8, 64], F32)
                rt = [T_("B_rt%d" % i, [128, 8, 32], F32) for i in range(4)]
                qr = T_("B_qr", [128, 8, 64], BF16)
                kr = T_("B_kr", [128, 2, 64], BF16)
                nss = T_("B_nss", [128, 3, 8], F32)
                qTs = T_("B_qTs", [128, 4, 128], BF16)
                kTs = T_("B_kTs", [128, 128], BF16)
                vas = T_("B_vas", [128, 128], BF16)
                psF = [PS("B_psF%d" % i, [128, 512], F32) for i in range(2)]
                psT = [PS("B_psT%d" % i, [128, 512], F32) for i in range(2)]
                psl = [PS("B_psl%d" % i, [128, 512], F32) for i in range(2)]
                pstr = PS("B_pstr", [128, 8, 128], BF16)
                fcnt = [0]
                lcnt = [0]
                tcnt = [0]

                def fm_proj(hb, co):
                    b = fcnt[0] % 2
                    fcnt[0] += 1
                    for kc in range(KC):
                        S.op('pe', lambda e, kc=kc, b=b: e.matmul(out=psF[b][:], lhsT=win1[:, kc, co:co + 128], rhs=hext[hb][:, kc, 1:513], start=(kc == 0), stop=False),
                             reads=['win1', 'hext%d' % hb], writes=['psF%d' % b])
                    for kc in range(KC):
                        S.op('pe', lambda e, kc=kc, b=b: e.matmul(out=psF[b][:], lhsT=win2[:, kc, co:co + 128], rhs=hs[:, kc, :], start=False, stop=(kc == KC - 1)),
                             reads=['win2', 'hs'], writes=['psF%d' % b])
                    return b

                def norm_rope(ps, ps_key, nh, gain, gain_key, cb, outb, out_key):
                    pv = ps.rearrange("p (h j) -> p h j", j=64)
                    S.op('act', lambda e: e.activation(out=nq[:, 0:nh, :], in_=pv, func=AF.Square), reads=[ps_key], writes=['nq'])
                    S.op('dve', lambda e: e.tensor_reduce(out=nss[:, 0, 0:nh], in_=nq[:, 0:nh, :], axis=AX.X, op=ALU.add), reads=['nq'], writes=['nss0'])
                    S.op('dve', lambda e: e.tensor_scalar(out=nss[:, 1, 0:nh], in0=nss[:, 0, 0:nh], scalar1=1.0 / 64, scalar2=EPS, op0=ALU.mult, op1=ALU.add), reads=['nss0'], writes=['nss1'])
                    S.op('pool', lambda e: e.tensor_tensor(out=nss[:, 2, 0:nh], in0=nss[:, 1, 0:nh], in1=mhalf[:, 0:nh], op=ALU.pow), reads=['nss1', 'mhalf'], writes=['nss2'])
                    S.op('dve', lambda e: e.tensor_tensor(out=nq2[:, 0:nh, :], in0=pv, in1=nss[:, 2, 0:nh].unsqueeze(2).broadcast_to([128, nh, 64]), op=ALU.mult), reads=[ps_key, 'nss2'], writes=['nq2'])
                    S.op('dve', lambda e: e.tensor_tensor(out=nq[:, 0:nh, :], in0=nq2[:, 0:nh, :], in1=gain[:, 0:nh, :], op=ALU.mult), reads=['nq2', gain_key], writes=['nq'])
                    x0 = nq[:, 0:nh, 0:64:2]
                    x1 = nq[:, 0:nh, 1:64:2]
                    cc = cs[cb][:, 0:1, :].broadcast_to([128, nh, 32])
                    sn = cs[cb][:, 1:2, :].broadcast_to([128, nh, 32])
                    ck = 'cs%d' % cb
                    S.op('dve', lambda e: e.tensor_tensor(out=rt[0][:, 0:nh, :], in0=x0, in1=cc, op=ALU.mult), reads=['nq', ck], writes=['rt0'])
                    S.op('dve', lambda e: e.tensor_tensor(out=rt[1][:, 0:nh, :], in0=x1, in1=sn, op=ALU.mult), reads=['nq', ck], writes=['rt1'])
                    S.op('dve', lambda e: e.tensor_tensor(out=rt[2][:, 0:nh, :], in0=x0, in1=sn, op=ALU.mult), reads=['nq', ck], writes=['rt2'])
                    S.op('dve', lambda e: e.tensor_tensor(out=rt[3][:, 0:nh, :], in0=x1, in1=cc, op=ALU.mult), reads=['nq', ck], writes=['rt3'])
                    S.op('dve', lambda e: e.tensor_tensor(out=outb[:, 0:nh, 0:32], in0=rt[0][:, 0:nh, :], in1=rt[1][:, 0:nh, :], op=ALU.subtract), reads=['rt0', 'rt1'], writes=[out_key])
                    S.op('dve', lambda e: e.tensor_tensor(out=outb[:, 0:nh, 32:64], in0=rt[2][:, 0:nh, :], in1=rt[3][:, 0:nh, :], op=ALU.add), reads=['rt2', 'rt3'], writes=[out_key])

                for ti in range(NT):
                    hb = ti % 2
                    tok = slice(ti * 512, (ti + 1) * 512)
                    S.dma(hext[hb][:], H2[:, :, ti * 512:ti * 512 + 514], writes=['hext%d' % hb])
                    S.op('pool', lambda e, hb=hb: e.tensor_tensor(out=hs[:], in0=hext[hb][:, :, 0:512], in1=hext[hb][:, :, 2:514], op=ALU.add),
                         reads=['hext%d' % hb], writes=['hs'])
                    for (co0, DST) in ((0, R_T), (512, K_T)):
                        for hp in range(4):
                            b = fm_proj(hb, co0 + hp * 128)
                            f = fcnt[0] % 3
                            S.op('act', lambda e, b=b, f=f: e.activation(out=fst[f][:], in_=psF[b][:], func=AF.Copy), reads=['psF%d' % b], writes=['fst%d' % f])
                            S.dma(DST[hp][:, tok], fst[f][:], reads=['fst%d' % f], writes=['rk_scr'], queue='pool')
                    b = fm_proj(hb, 1536)
                    S.op('act', lambda e, b=b: e.activation(out=tw[:], in_=psF[b][:], func=AF.Tanh), reads=['psF%d' % b], writes=['tw'])
                    b = fm_proj(hb, 1664)
                    S.op('act', lambda e, b=b: e.activation(out=ta[:], in_=psF[b][:], func=AF.Copy), reads=['psF%d' % b], writes=['ta'])
                    b = fm_proj(hb, 1792)
                    S.op('act', lambda e, b=b: e.activation(out=tg[:], in_=psF[b][:], func=AF.Sigmoid), reads=['psF%d' % b], writes=['tg'])
                    for (wi, src, src_key, biasT, bias_key, DST, scl) in ((0, tw, 'tw', w0T, 'w0T', LW, -0.6065306597126334), (1, ta, 'ta', a0T, 'a0T', AA, None)):
                        for d in range(2):
                            for hp in range(4):
                                lb = lcnt[0] % 2
                                lcnt[0] += 1
                                S.op('pe', lambda e, lb=lb, wi=wi, d=d, hp=hp, src=src: e.matmul(out=psl[lb][:], lhsT=lw_b[d * 64:(d + 1) * 64, wi, hp * 128:(hp + 1) * 128], rhs=src[d * 64:(d + 1) * 64, :], start=True, stop=True),
                                     reads=['lw_b', src_key], writes=['psl%d' % lb])
                                f = lcnt[0] % 3
                                S.op('act', lambda e, lb=lb, f=f, d=d, hp=hp, biasT=biasT: e.activation(out=fst[f][:], in_=psl[lb][:], func=AF.Sigmoid, bias=biasT[:, d, hp:hp + 1]),
                                     reads=['psl%d' % lb, bias_key], writes=['fst%d' % f])
                                if scl is not None:
                                    S.op('dve', lambda e, f=f, scl=scl: e.tensor_scalar(out=fst[f][:], in0=fst[f][:], scalar1=scl, scalar2=None, op0=ALU.mult), reads=['fst%d' % f], writes=['fst%d' % f])
                                S.dma(DST[d, hp][:, tok], fst[f][:], reads=['fst%d' % f], writes=['la_scr'], queue='pool')
                    for sub in range(4):
                        rows = slice(ti * 512 + sub * 128, ti * 512 + (sub + 1) * 128)
                        scol = slice(sub * 128, (sub + 1) * 128)
                        tb = tcnt[0] % 2
                        tcnt[0] += 1
                        S.op('pe', lambda e, tb=tb, scol=scol: e.matmul(out=psT[tb][:], lhsT=tg[:, scol], rhs=lw_b[:, 2, :], start=True, stop=True),
                             reads=['tg', 'lw_b'], writes=['psT%d' % tb])
                        f = tcnt[0] % 3
                        S.op('act', lambda e, tb=tb, f=f: e.activation(out=fst[f][:], in_=psT[tb][:], func=AF.Copy), reads=['psT%d' % tb], writes=['fst%d' % f])
                        S.dma(G_TM[rows, :], fst[f][:], reads=['fst%d' % f], writes=['g_scr'], queue='pool')
                        tb = tcnt[0] % 2
                        tcnt[0] += 1
                        for kc in range(KC):
                            S.op('pe', lambda e, kc=kc, tb=tb, sub=sub, hb=hb: e.matmul(out=psT[tb][:], lhsT=hext[hb][:, kc, 1 + sub * 128:1 + (sub + 1) * 128], rhs=win1[:, kc, 1024:1536], start=(kc == 0), stop=False),
                                 reads=['win1', 'hext%d' % hb], writes=['psT%d' % tb])
                        for kc in range(KC):
                            S.op('pe', lambda e, kc=kc, tb=tb, scol=scol: e.matmul(out=psT[tb][:], lhsT=hs[:, kc, scol], rhs=win2[:, kc, 1024:1536], start=False, stop=(kc == KC - 1)),
                                 reads=['win2', 'hs'], writes=['psT%d' % tb])
                        vb = tcnt[0] % 2
                        S.op('act', lambda e, tb=tb, vb=vb: e.activation(out=vst[vb][:], in_=psT[tb][:], func=AF.Copy), reads=['psT%d' % tb], writes=['vst%d' % vb])
                        S.dma(V_TM[rows, :], vst[vb][:], reads=['vst%d' % vb], writes=['v_scr'], queue='pool')
                        cb = sub % 2
                        S.dma(cs[cb][:, 0, :], cos_in[T][rows, :], writes=['cs%d' % cb])
                        S.dma(cs[cb][:, 1, :], sin_in[T][rows, :], writes=['cs%d' % cb])
                        tb = tcnt[0] % 2
                        tcnt[0] += 1
                        for kc in range(KC):
                            S.op('pe', lambda e, kc=kc, tb=tb, sub=sub, hb=hb: e.matmul(out=psT[tb][:], lhsT=hext[hb][:, kc, 1 + sub * 128:1 + (sub + 1) * 128], rhs=win1[:, kc, 1920:2432], start=(kc == 0), stop=(kc == KC - 1)),
                                 reads=['win1', 'hext%d' % hb], writes=['psT%d' % tb])
                        norm_rope(psT[tb][:], 'psT%d' % tb, 8, gq, 'gq', cb, qr, 'qr')
                        for hp in range(4):
                            S.op('pe', lambda e, hp=hp: e.transpose(out=pstr[:, hp, :], in_=qr[:, 2 * hp:2 * hp + 2, :].rearrange("p a b -> p (a b)"), identity=ident[:]),
                                 reads=['qr', 'ident'], writes=['pstr'])
                        S.op('dve', lambda e: e.tensor_copy(out=qTs[:], in_=pstr[:, 0:4, :]), reads=['pstr'], writes=['qTs'])
                        S.dma(QT[:, :, rows].rearrange("h p t -> p h t"), qTs[:], reads=['qTs'], writes=['q_scr'], queue='pool')
                        tb = tcnt[0] % 2
                        tcnt[0] += 1
                        for kc in range(KC):
                            S.op('pe', lambda e, kc=kc, tb=tb, sub=sub, hb=hb: e.matmul(out=psT[tb][:, 0:256], lhsT=hext[hb][:, kc, 1 + sub * 128:1 + (sub + 1) * 128], rhs=win1[:, kc, 2432:2688], start=(kc == 0), stop=(kc == KC - 1)),
                                 reads=['win1', 'hext%d' % hb], writes=['psT%d' % tb])
                        S.op('act', lambda e, tb=tb: e.activation(out=vas[:], in_=psT[tb][:, 128:256], func=AF.Copy), reads=['psT%d' % tb], writes=['vas'])
                        S.dma(VA[rows, :], vas[:], reads=['vas'], writes=['va_scr'], queue='pool')
                        norm_rope(psT[tb][:, 0:128], 'psT%d' % tb, 2, gk, 'gk', cb, kr, 'kr')
                        S.op('pe', lambda e: e.transpose(out=pstr[:, 4, :], in_=kr[:].rearrange("p a b -> p (a b)"), identity=ident[:]),
                             reads=['kr', 'ident'], writes=['pstr'])
                        S.op('dve', lambda e: e.tensor_copy(out=kTs[:], in_=pstr[:, 4, :]), reads=['pstr'], writes=['kTs'])
                        S.dma(KTA[:, rows], kTs[:], reads=['kTs'], writes=['k_scr'], queue='pool')
                S.flush()
            if debug and 'RT' in dbg:
                with contextlib.ExitStack() as st:
                    def dump(name, src, shape, dt):
                        t = st.enter_context(SBT("dbgt_" + name, shape, dt))
                        S.dma(t[:], src, writes=[name])
                        S.dma(dbg[name], t[:], reads=[name], writes=[name + 'o'])
                    dump('RT', R_T[0][:, 0:T], [128, T], F32)
                    dump('KT', K_T[1][:, 0:T], [128, T], F32)
                    dump('LW', LW[1, 2][:, 0:T], [128, T], F32)
                    dump('AA', AA[0, 3][:, 0:T], [128, T], F32)
                    dump('V', V_TM[0:128, :], [128, RW], BF16)
                    dump('G', G_TM[128:256, :], [128, RW], F32)
                    dump('QT', QT[1][:, 0:T], [128, T], BF16)
                    dump('KTA', KTA[:, 0:T], [128, T], BF16)
                    dump('VA', VA[0:128, :], [128, 128], BF16)
                    S.flush()
                continue

            NS = min(T, NS_CAP_B)
            NSC = T // NS
            NCH = NS // CH
            with contextlib.ExitStack() as st:
                def T_(name, shape, dt):
                    return st.enter_context(SBT(name, list(shape), dt))
                pb_ = [st.enter_context(PST("R_pb%d" % i, [128, 512], F32)) for i in range(8)]
                pk = ['pb%d' % i for i in range(8)]
                cstg = T_("R_cstg", [128, 5, 128], F32)
                S.dma(cstg[:, 0, :], cin['maskf'], writes=['cstg'])
                S.dma(cstg[:, 1, :], cin['maskb'], writes=['cstg'])
                S.dma(cstg[:, 2, :], cin['blockones'], writes=['cstg'])
                S.dma(cstg[:, 3, 0:64], cin['masknt'][0], writes=['cstg'])
                S.dma(cstg[:, 3, 64:128], cin['masknt'][1], writes=['cstg'])
                S.dma(cstg[:, 4, 0:2], cin['headsel'], writes=['cstg'])
                bones = T_("R_bones", [128, 128], BF16)
                hsel = T_("R_hsel", [128, 2], BF16)
                hself = T_("R_hself", [128, 2], F32)
                S.op('dve', lambda e: e.tensor_copy(out=bones[:], in_=cstg[:, 2, :]), reads=['cstg'], writes=['bones'])
                S.op('dve', lambda e: e.tensor_copy(out=hsel[:], in_=cstg[:, 4, 0:2]), reads=['cstg'], writes=['hsel'])
                S.op('dve', lambda e: e.tensor_copy(out=hself[:], in_=cstg[:, 4, 0:2]), reads=['cstg'], writes=['hself'])
                pvec = T_("R_pvec", [128, 4, 4], F32)
                for i, v in enumerate((k_k, k_a, k_a, r_k)):
                    S.dma(pvec[:, i, :], v.rearrange("(h p) -> p h", p=128), writes=['pvec'], allow_slow_non_contiguous=True)
                S.op('dve', lambda e: e.tensor_scalar(out=pvec[:, 2, :], in0=pvec[:, 2, :], scalar1=-1.0, scalar2=1.0, op0=ALU.mult, op1=ALU.add), reads=['pvec'], writes=['pvec'])
                rmask = T_("R_rmask", [128, NCH, CH], F32)
                S.op('pool', lambda e: e.memset(rmask[:], 1.0), writes=['rmask'])
                S.op('pool', lambda e: e.memset(rmask[:, :, 0:1], 0.0), reads=[], writes=['rmask'])
                ft = [T_("R_ft%d" % i, [128, NS], F32) for i in range(10)]
                fk = ['ft%d' % i for i in range(10)]
                sqb = T_("R_sqb", [128, NS], BF16)
                prodT = T_("R_prodT", [128, NS], BF16)
                QTt = [[T_("R_QTt%d_%d" % (z, d), [128, NCH, 2, CH], BF16) for d in range(2)] for z in range(2)]
                KTt = [[T_("R_KTt%d_%d" % (z, d), [128, NCH, 2, CH], BF16) for d in range(2)] for z in range(2)]
                Vm = [[T_("R_Vm%d_%d" % (z, d), [128, NCH, 2, CH], BF16) for d in range(2)] for z in range(2)]
                ATt = [[T_("R_AT%d_%d" % (z, d), [128, NCH, 2, 128], BF16) for d in range(2)] for z in range(2)]
                Km = [[T_("R_Km%d_%d" % (z, d), [128, NCH, 2, CH], BF16) for d in range(2)] for z in range(2)]
                KTm = [[T_("R_KTm%d_%d" % (z, d), [128, NCH, 2, 2, CH], BF16) for d in range(2)] for z in range(2)]
                Ttt = [[T_("R_Tt%d_%d" % (z, d), [64, NCH, 2, CH], BF16) for d in range(2)] for z in range(2)]
                Yt = [[T_("R_Y%d_%d" % (z, d), [64, NCH, 2, CH], F32) for d in range(2)] for z in range(2)]
                gam = [[T_("R_gam%d_%d" % (z, d), [128, NCH], F32) for d in range(2)] for z in range(2)]
                Xb = [[T_("R_Xb%d_%d" % (u, i), [64, 8, CH], BF16) for i in range(2)] for u in range(2)]
                XTb = [[T_("R_XTb%d_%d" % (u, i), [64, 8, CH], BF16) for i in range(2)] for u in range(2)]
                Ttf = [T_("R_Ttf%d" % u, [64, 8, CH], F32) for u in range(2)]
                Ttb = [T_("R_Ttb%d" % u, [64, 8, CH], BF16) for u in range(2)]
                Hf = T_("R_Hf", [128, 2, CH], F32)
                Hg = T_("R_Hg", [128, 2, CH], F32)
                Hb = T_("R_Hb", [128, 4, CH], BF16)
                Xs = T_("R_Xs", [64, 4, CH], BF16)
                coefT = T_("R_coefT", [128, T // 128, 16], F32)
                S.flush()
                INV_BANKS = ((5, 6, 0), (2, 3, 4))

                def precompute(hp, d, sc, z):
                    tok0 = sc * NS
                    cols = slice(tok0, tok0 + NS)
                    D_ = str(d) + '_' + str(z)
                    kQ, kK, kKm_, kAT, kKmm, kTt, kVV, kVU, kG = 'QTt' + D_, 'KTt' + D_, 'KTm' + D_, 'AT' + D_, 'Km' + D_, 'Tt' + D_, 'VmV' + D_, 'VmU' + D_, 'gam' + D_
                    r_s, k_s, lw_s, a_s = ft[0], ft[1], ft[2], ft[3]
                    S.dma(r_s[:], R_T[hp][:, cols], writes=[fk[0]])
                    S.dma(k_s[:], K_T[hp][:, cols], writes=[fk[1]])
                    S.dma(lw_s[:], LW[d, hp][:, cols], writes=[fk[2]])
                    S.dma(a_s[:], AA[d, hp][:, cols], writes=[fk[3]])
                    for hh in range(2):
                        S.dma(Vm[z][d][64:128, :, hh, :], V_TM[tok0:tok0 + NS, (hp * 2 + hh) * 64:(hp * 2 + hh + 1) * 64].rearrange("(c p) j -> p c j", p=64), writes=[kVV])
                    S.op('pool', lambda e: e.memset(Vm[z][d][0:64, :, :, :], 0.0), writes=[kVU])
                    yield
                    S.op('dve', lambda e: e.tensor_scalar(out=ft[4][:], in0=k_s[:], scalar1=pvec[:, 0, hp:hp + 1], scalar2=None, op0=ALU.mult), reads=[fk[1], 'pvec'], writes=[fk[4]])
                    S.op('act', lambda e: e.activation(out=sqb[:], in_=ft[4][:], func=AF.Square), reads=[fk[4]], writes=['sqb'])
                    for q in range(NS // 512):
                        qs = slice(q * 512, (q + 1) * 512)
                        S.op('pe', lambda e, q=q, qs=qs: e.matmul(out=pb_[0][:], lhsT=bones[:], rhs=sqb[:, qs], start=True, stop=True), reads=['bones', 'sqb'], writes=[pk[0]])
                        S.op('dve', lambda e, q=q, qs=qs: e.tensor_scalar(out=ft[5][:, qs], in0=pb_[0][:], scalar1=1e-24, scalar2=None, op0=ALU.max), reads=[pk[0]], writes=[fk[5]])
                    S.op('act', lambda e: e.activation(out=ft[5][:], in_=ft[5][:], func=AF.Ln), reads=[fk[5]], writes=[fk[5]])
                    S.op('act', lambda e: e.activation(out=ft[5][:], in_=ft[5][:], func=AF.Exp, scale=-0.5), reads=[fk[5]], writes=[fk[5]])
                    S.op('dve', lambda e: e.tensor_tensor(out=ft[4][:], in0=ft[4][:], in1=ft[5][:], op=ALU.mult), reads=[fk[4], fk[5]], writes=[fk[4]])
                    yield
                    S.op('dve', lambda e: e.tensor_scalar(out=ft[6][:], in0=a_s[:], scalar1=pvec[:, 1, hp:hp + 1], scalar2=pvec[:, 2, hp:hp + 1], op0=ALU.mult, op1=ALU.add), reads=[fk[3], 'pvec'], writes=[fk[6]])
                    S.op('pool', lambda e: e.tensor_tensor(out=ft[6][:], in0=ft[6][:], in1=k_s[:], op=ALU.mult), reads=[fk[6], fk[1]], writes=[fk[6]])
                    S.op('pool', lambda e: e.tensor_tensor(out=ft[7][:], in0=ft[4][:], in1=a_s[:], op=ALU.mult), reads=[fk[4], fk[3]], writes=[fk[7]])
                    yield
                    S.op('dve', lambda e: e.tensor_tensor_scan(out=ft[8][:], data0=rmask[:].rearrange("p c j -> p (c j)"), data1=lw_s[:], initial=0.0, op0=ALU.mult, op1=ALU.add), reads=['rmask', fk[2]], writes=[fk[8]])
                    L3 = ft[8][:].rearrange("p (c j) -> p c j", j=CH)
                    if d == 1:
                        S.op('dve', lambda e: e.tensor_tensor(out=ft[9][:], in0=lw_s[:], in1=ft[8][:], op=ALU.subtract), reads=[fk[2], fk[8]], writes=[fk[9]])
                        S.op('dve', lambda e: e.tensor_tensor(out=ft[9][:].rearrange("p (c j) -> p c j", j=CH), in0=ft[9][:].rearrange("p (c j) -> p c j", j=CH), in1=L3[:, :, CH - 1:CH].broadcast_to([128, NCH, CH]), op=ALU.add), reads=[fk[9], fk[8]], writes=[fk[9]])
                        Lt, Lk, Et, Ek = ft[9], fk[9], ft[8], fk[8]
                    else:
                        Lt, Lk, Et, Ek = ft[8], fk[8], ft[9], fk[9]
                    Et3 = Et[:].rearrange("p (c j) -> p c j", j=CH)
                    S.op('act', lambda e: e.activation(out=Et[:], in_=Lt[:], func=AF.Exp), reads=[Lk], writes=[Ek])
                    gi = CH - 1 if d == 0 else 0
                    S.op('dve', lambda e: e.tensor_copy(out=gam[z][d][:], in_=Et3[:, :, gi]), reads=[Ek], writes=[kG])
                    S.op('dve', lambda e: e.tensor_tensor(out=QTt[z][d][:, :, 1, :], in0=r_s[:].rearrange("p (c j) -> p c j", j=CH), in1=Et3, op=ALU.mult), reads=[fk[0], Ek], writes=[kQ])
                    S.op('dve', lambda e: e.tensor_tensor(out=ft[5][:], in0=r_s[:], in1=ft[6][:], op=ALU.mult), reads=[fk[0], fk[6]], writes=[fk[5]])
                    S.op('act', lambda e: e.activation(out=prodT[:], in_=ft[5][:], func=AF.Copy, scale=pvec[:, 3, hp:hp + 1]), reads=[fk[5], 'pvec'], writes=['prodT'])
                    yield
                    S.op('act', lambda e: e.activation(out=Et[:], in_=Lt[:], func=AF.Exp, scale=-1.0), reads=[Lk, kQ, kG], writes=[Ek])
                    S.op('dve', lambda e: e.tensor_tensor(out=KTt[z][d][:, :, 0, :], in0=ft[7][:].rearrange("p (c j) -> p c j", j=CH), in1=Et3, op=ALU.mult), reads=[fk[7], Ek], writes=[kK])
                    S.op('dve', lambda e: e.tensor_tensor(out=KTt[z][d][:, :, 1, :], in0=ft[6][:].rearrange("p (c j) -> p c j", j=CH), in1=Et3, op=ALU.mult), reads=[fk[6], Ek], writes=[kK])
                    yield
                    for hh in range(2):
                        S.op('act', lambda e, hh=hh: e.activation(out=KTm[z][d][:, :, hh, :, :].rearrange("p c a b -> p c (a b)"), in_=KTt[z][d][:].rearrange("p c a b -> p c (a b)"), func=AF.Copy, scale=hself[:, hh:hh + 1]), reads=[kK, 'hself'], writes=[kKm_])
                    S.op('dve', lambda e: e.tensor_tensor(out=Et[:], in0=Lt[:], in1=lw_s[:], op=ALU.subtract), reads=[Lk, fk[2], kK], writes=[Ek])
                    S.op('act', lambda e: e.activation(out=Et[:], in_=Et[:], func=AF.Exp), reads=[Ek], writes=[Ek])
                    S.op('act', lambda e: e.activation(out=ft[7][:], in_=ft[4][:], func=AF.Copy, scale=-1.0), reads=[fk[4], kK], writes=[fk[7]])
                    S.op('dve', lambda e: e.tensor_tensor(out=QTt[z][d][:, :, 0, :], in0=ft[7][:].rearrange("p (c j) -> p c j", j=CH), in1=Et3, op=ALU.mult), reads=[fk[7], Ek], writes=[kQ])
                    yield
                    nb = NS // 128
                    cv = pb_[0][:, 0:nb * 2].rearrange("p (q h) -> p q h", h=2)
                    for q in range(nb):
                        S.op('pe', lambda e, q=q: e.matmul(out=cv[:, q, :], lhsT=prodT[:, q * 128:(q + 1) * 128], rhs=hsel[:], start=True, stop=True), reads=['prodT', 'hsel'], writes=[pk[0]])
                    S.op('act', lambda e: e.activation(out=coefT[:, tok0 // 128:tok0 // 128 + nb, d * 8 + hp * 2:d * 8 + hp * 2 + 2], in_=cv, func=AF.Copy), reads=[pk[0]], writes=['coefT'])
                    yield
                    mk = cstg[:, d, :]
                    combos = [(c, hh) for c in range(NCH) for hh in range(2)]
                    ATf = ATt[z][d][:].rearrange("p c h s -> p (c h) s")
                    Ttf_all = Ttt[z][d][:].rearrange("p c h s -> p (c h) s")
                    for g4 in range(len(combos) // 4):
                        bk = 2 + (g4 % 2)
                        av = pb_[bk][:].rearrange("p (j s) -> p j s", s=128)
                        for j in range(4):
                            c, hh = combos[g4 * 4 + j]
                            S.op('pe', lambda e, av=av, j=j, c=c, hh=hh: e.matmul(out=av[:, j, :], lhsT=KTm[z][d][:, c, hh, :, :].rearrange("p a b -> p (a b)"), rhs=QTt[z][d][:, c, :, :].rearrange("p a b -> p (a b)"), start=True, stop=True),
                                 reads=[kKm_, kQ], writes=[pk[bk]])
                        S.op('dve', lambda e, av=av, g4=g4: e.tensor_tensor(out=ATf[:, g4 * 4:g4 * 4 + 4, :], in0=av, in1=mk.unsqueeze(1).broadcast_to([128, 4, 128]), op=ALU.mult),
                             reads=[pk[bk], 'cstg'], writes=[kAT])
                        yield
                    for g8 in range(NCH // 8):
                        kv = pb_[4][:].bitcast(BF16)[:, 0:1024].rearrange("p (j s) -> p j s", s=128)
                        for j in range(8):
                            c = g8 * 8 + j
                            S.op('pe', lambda e, kv=kv, j=j, c=c: e.transpose(out=kv[:, j, :], in_=KTt[z][d][:, c, :, :].rearrange("p a b -> p (a b)"), identity=ident[:]),
                                 reads=[kK, 'ident'], writes=[pk[4]])
                        S.op('act', lambda e, kv=kv, g8=g8: e.activation(out=Km[z][d][:, g8 * 8:g8 * 8 + 8, :, :].rearrange("p c h s -> p c (h s)"), in_=kv, func=AF.Copy), reads=[pk[4]], writes=[kKmm])
                        yield
                    mnt = cstg[0:64, 3, d * 64:(d + 1) * 64]
                    ngrp = len(combos) // 8
                    for gp in range(0, ngrp, 2):
                        units = [u for u in range(2) if gp + u < ngrp]
                        stt_ = {}
                        for u in units:
                            g8 = gp + u
                            b5, b6, b0 = INV_BANKS[u]
                            v5 = pb_[b5][0:64, :].rearrange("p (j s) -> p j s", s=64)
                            v6 = pb_[b6][0:64, :].rearrange("p (j s) -> p j s", s=64)
                            v0 = pb_[b0][0:64, :].rearrange("p (j s) -> p j s", s=64)
                            gs = slice(g8 * 8, g8 * 8 + 8)
                            X0 = ATf[0:64, gs, 0:64]
                            U_ = str(u)
                            for j in range(8):
                                c, hh = combos[g8 * 8 + j]
                                S.op('pe', lambda e, j=j, c=c, hh=hh, v5=v5: e.matmul(out=v5[:, j, :], lhsT=QTt[z][d][:, c, 0, :], rhs=KTm[z][d][:, c, hh, 0, :], start=True, stop=True),
                                     reads=[kQ, kKm_], writes=[pk[b5]])
                            S.op('dve', lambda e, v5=v5, u=u: e.tensor_tensor(out=XTb[u][0][:], in0=v5, in1=mnt.unsqueeze(1).broadcast_to([64, 8, 64]), op=ALU.mult), reads=[pk[b5], 'cstg'], writes=['XTb' + U_ + '0'])
                            S.op('dve', lambda e, X0=X0, u=u: e.tensor_tensor(out=Ttf[u][:], in0=X0, in1=ident_f[0:64, 0:64].unsqueeze(1).broadcast_to([64, 8, 64]), op=ALU.add), reads=[kAT, 'ident_f'], writes=['Ttf' + U_])
                            S.op('act', lambda e, u=u: e.activation(out=Ttb[u][:], in_=Ttf[u][:], func=AF.Copy), reads=['Ttf' + U_], writes=['Ttb' + U_])
                            stt_[u] = dict(Xc=X0, Xck=kAT, XTc=XTb[u][0], XTck='XTb' + U_ + '0', v5=v5, v6=v6, v0=v0, b5=b5, b6=b6, b0=b0, gs=gs)
                            yield
                        for it in range(1, 6):
                            nx = it % 2
                            for u in units:
                                s_ = stt_[u]
                                U_ = str(u)
                                if it < 5:
                                    for j in range(8):
                                        S.op('pe', lambda e, j=j, s_=s_, Xc=s_['Xc'], XTc=s_['XTc']: e.matmul(out=s_['v6'][:, j, :], lhsT=XTc[:, j, :], rhs=Xc[:, j, :], start=True, stop=True), reads=[s_['Xck'], s_['XTck']], writes=[pk[s_['b6']]])
                                for j in range(8):
                                    S.op('pe', lambda e, j=j, s_=s_, Xc=s_['Xc'], XTc=s_['XTc']: e.matmul(out=s_['v5'][:, j, :], lhsT=Xc[:, j, :], rhs=XTc[:, j, :], start=True, stop=True), reads=[s_['Xck'], s_['XTck']], writes=[pk[s_['b5']]])
                            yield
                            for u in units:
                                s_ = stt_[u]
                                U_ = str(u)
                                if it < 5:
                                    S.op('act', lambda e, nx=nx, s_=s_, u=u: e.activation(out=Xb[u][nx][:], in_=s_['v6'], func=AF.Copy), reads=[pk[s_['b6']]], writes=['Xb%s%d' % (U_, nx)])
                                S.op('dve', lambda e, nx=nx, s_=s_, u=u: e.tensor_copy(out=XTb[u][nx][:], in_=s_['v5']), reads=[pk[s_['b5']]], writes=['XTb%s%d' % (U_, nx)])
                                s_['Xc'], s_['Xck'] = Xb[u][nx], 'Xb%s%d' % (U_, nx)
                                s_['XTc'], s_['XTck'] = XTb[u][nx], 'XTb%s%d' % (U_, nx)
                            yield
                            for u in units:
                                s_ = stt_[u]
                                U_ = str(u)
                                for j in range(8):
                                    S.op('pe', lambda e, j=j, s_=s_, XTc=s_['XTc'], u=u: e.matmul(out=s_['v0'][:, j, :], lhsT=XTc[:, j, :], rhs=Ttb[u][:, j, :], start=True, stop=True), reads=[s_['XTck'], 'Ttb' + U_], writes=[pk[s_['b0']]])
                            yield
                            for u in units:
                                s_ = stt_[u]
                                U_ = str(u)
                                S.op('dve', lambda e, s_=s_, u=u: e.tensor_tensor(out=Ttf[u][:], in0=s_['v0'], in1=Ttf[u][:], op=ALU.add), reads=[pk[s_['b0']], 'Ttf' + U_], writes=['Ttf' + U_])
                                if it < 5:
                                    S.op('act', lambda e, u=u: e.activation(out=Ttb[u][:], in_=Ttf[u][:], func=AF.Copy), reads=['Ttf' + U_], writes=['Ttb' + U_])
                                else:
                                    S.op('act', lambda e, s_=s_, u=u: e.activation(out=Ttf_all[:, s_['gs'], :], in_=Ttf[u][:], func=AF.Copy), reads=['Ttf' + U_], writes=[kTt])
                            yield

                def chain_step(i, z):
                    vx = pb_[1][0:64, 0:256].rearrange("p (h s) -> p h s", s=64)
                    vy = pb_[1][0:64, 256:512].rearrange("p (h s) -> p h s", s=64)
                    vu = pb_[7][0:64, 0:256].rearrange("p (h s) -> p h s", s=64)
                    vh = pb_[7][:, 256:512].rearrange("p (h s) -> p h s", s=64)
                    cs_ = (i, NCH - 1 - i)
                    for d in range(2):
                        c = cs_[d]
                        D_ = str(d) + '_' + str(z)
                        S.op('dve', lambda e, c=c, d=d: e.tensor_scalar(out=Hg[:, d, :], in0=Hf[:, d, :], scalar1=gam[z][d][:, c:c + 1], scalar2=None, op0=ALU.mult), reads=['Hf', 'gam' + D_], writes=['Hg'])
                    for d in range(2):
                        c = cs_[d]
                        D_ = str(d) + '_' + str(z)
                        for hh in range(2):
                            k = d * 2 + hh
                            S.op('pe', lambda e, c=c, hh=hh, d=d, k=k: e.matmul(out=vx[:, k, :], lhsT=QTt[z][d][:, c, 0, :], rhs=Hb[:, k, :], start=True, stop=False), reads=['QTt' + D_, 'Hb'], writes=[pk[1]])
                            S.op('pe', lambda e, c=c, hh=hh, d=d, k=k: e.matmul(out=vx[:, k, :], lhsT=ATt[z][d][:, c, hh, 0:64], rhs=Vm[z][d][:, c, hh, :], start=False, stop=True), reads=['AT' + D_, 'VmV' + D_, 'VmU' + D_], writes=[pk[1]])
                    S.op('act', lambda e: e.activation(out=Xs[:], in_=vx, func=AF.Copy), reads=[pk[1]], writes=['Xs'])
                    yield
                    for d in range(2):
                        c = cs_[d]
                        D_ = str(d) + '_' + str(z)
                        for hh in range(2):
                            k = d * 2 + hh
                            S.op('pe', lambda e, c=c, hh=hh, d=d, k=k: e.matmul(out=vu[:, k, :], lhsT=Ttt[z][d][:, c, hh, :], rhs=Xs[:, k, :], start=True, stop=True), reads=['Tt' + D_, 'Xs'], writes=[pk[7]])
                    yield
                    for d in range(2):
                        c = cs_[d]
                        D_ = str(d) + '_' + str(z)
                        S.op('dve', lambda e, c=c, d=d: e.tensor_copy(out=Vm[z][d][0:64, c, :, :], in_=vu[:, d * 2:d * 2 + 2, :]), reads=[pk[7]], writes=['VmU' + D_])
                    yield
                    for d in range(2):
                        c = cs_[d]
                        D_ = str(d) + '_' + str(z)
                        for hh in range(2):
                            k = d * 2 + hh
                            S.op('pe', lambda e, c=c, hh=hh, d=d, k=k: e.matmul(out=vy[:, k, :], lhsT=QTt[z][d][:, c, 1, :], rhs=Hb[:, k, :], start=True, stop=False), reads=['QTt' + D_, 'Hb'], writes=[pk[1]])
                            S.op('pe', lambda e, c=c, hh=hh, d=d, k=k: e.matmul(out=vy[:, k, :], lhsT=ATt[z][d][:, c, hh, 64:128], rhs=Vm[z][d][:, c, hh, :], start=False, stop=True), reads=['AT' + D_, 'VmV' + D_, 'VmU' + D_], writes=[pk[1]])
                    for d in range(2):
                        c = cs_[d]
                        D_ = str(d) + '_' + str(z)
                        for hh in range(2):
                            k = d * 2 + hh
                            S.op('pe', lambda e, c=c, hh=hh, d=d, k=k: e.matmul(out=vh[:, k, :], lhsT=Km[z][d][:, c, :, :].rearrange("p h s -> p (h s)"), rhs=Vm[z][d][:, c, hh, :], start=True, stop=True), reads=['Km' + D_, 'VmV' + D_, 'VmU' + D_], writes=[pk[7]])
                    yield
                    for d in range(2):
                        c = cs_[d]
                        S.op('act', lambda e, c=c, d=d: e.activation(out=Yt[z][d][:, c, :, :], in_=vy[:, d * 2:d * 2 + 2, :], func=AF.Copy), reads=[pk[1]], writes=['Y' + str(d) + '_' + str(z)])
                    for d in range(2):
                        c = cs_[d]
                        D_ = str(d) + '_' + str(z)
                        for hh in range(2):
                            k = d * 2 + hh
                            p0 = hh * 64
                            S.op('dve', lambda e, c=c, d=d, k=k, p0=p0: e.scalar_tensor_tensor(out=Hb[p0:p0 + 64, k, :], in0=vh[p0:p0 + 64, k, :], scalar=gam[z][d][p0:p0 + 64, c:c + 1], in1=Hg[p0:p0 + 64, d, :], op0=ALU.mult, op1=ALU.add), reads=[pk[7], 'gam' + D_, 'Hg'], writes=['Hb'])
                    for d in range(2):
                        c = cs_[d]
                        D_ = str(d) + '_' + str(z)
                        for hh in range(2):
                            k = d * 2 + hh
                            p0 = hh * 64
                            S.op('dve', lambda e, c=c, d=d, k=k, p0=p0: e.scalar_tensor_tensor(out=Hf[p0:p0 + 64, d, :], in0=vh[p0:p0 + 64, k, :], scalar=gam[z][d][p0:p0 + 64, c:c + 1], in1=Hg[p0:p0 + 64, d, :], op0=ALU.mult, op1=ALU.add), reads=[pk[7], 'gam' + D_, 'Hg'], writes=['Hf'])
                    yield

                tasks = [(hp, s_i) for hp in range(4) for s_i in range(NSC)]

                def pre_task(k):
                    hp, s_i = tasks[k]
                    scs = (s_i, NSC - 1 - s_i)
                    for d in range(2):
                        yield from precompute(hp, d, scs[d], k % 2)

                def chain_task(k):
                    hp, s_i = tasks[k]
                    z = k % 2
                    scs = (s_i, NSC - 1 - s_i)
                    if s_i == 0:
                        S.op('pool', lambda e: e.memset(Hf[:], 0.0), writes=['Hf'])
                        S.op('pool', lambda e: e.memset(Hb[:], 0.0), writes=['Hb'])
                    for i in range(NCH):
                        yield from chain_step(i, z)
                    for d in range(2):
                        tok0 = scs[d] * NS
                        S.dma(YD[d][tok0:tok0 + NS, hp * 128:(hp + 1) * 128].rearrange("(c p) j -> p c j", p=64), Yt[z][d][:].rearrange("p c h s -> p c (h s)"), reads=['Y' + str(d) + '_' + str(z)], writes=['YD'], queue='pool')

                for _ in pre_task(0):
                    pass
                for k in range(len(tasks)):
                    a = chain_task(k)
                    b = pre_task(k + 1) if k + 1 < len(tasks) else iter(())
                    a_alive = b_alive = True
                    while a_alive or b_alive:
                        if a_alive:
                            try:
                                next(a)
                            except StopIteration:
                                a_alive = False
                        if b_alive:
                            try:
                                next(b)
                            except StopIteration:
                                b_alive = False
                    if k % 16 == 15:
                        S.flush()
                S.dma(COEFS16[0:T, :].rearrange("(q p) h -> p q h", p=128), coefT[:], reads=['coefT'], writes=['COEFS'], queue='pool')
                S.flush()
            if debug and 'YD0' in dbg:
                with contextlib.ExitStack() as st:
                    def dump(name, src, shape, dt):
                        t = st.enter_context(SBT("dbgt_" + name, shape, dt))
                        S.dma(t[:], src, writes=[name])
                        S.dma(dbg[name], t[:], reads=[name], writes=[name + 'o'])
                    dump('YD0', YD[0][0:T, :].rearrange("(s p) c -> p s c", p=128), [128, T // 128, RW], F32)
                    dump('YD1', YD[1][0:T, :].rearrange("(s p) c -> p s c", p=128), [128, T // 128, RW], F32)
                    dump('CF', COEFS16[0:T, :].rearrange("(s p) c -> p s c", p=128), [128, T // 128, 16], F32)
                    S.flush()
                continue

            NKT = T // 128
            with contextlib.ExitStack() as st:
                def T_(name, shape, dt):
                    return st.enter_context(SBT(name, list(shape), dt))

                def PS(name, shape, dt):
                    return st.enter_context(PST(name, list(shape), dt))
                Kt = T_("C_Kt", [64, 2, T], BF16)
                Va = T_("C_Va", [128, NKT, 2, 65], BF16)
                Qg = [T_("C_Qg%d" % i, [64, 4, 128], BF16) for i in range(2)]
                Pt = [T_("C_Pt%d" % i, [128, 512], BF16) for i in range(2)]
                ob = [T_("C_ob%d" % i, [128, 4, 64], BF16) for i in range(2)]
                rec = T_("C_rec", [128, 4], F32)
                psS = [PS("C_psS%d" % i, [128, 512], F32) for i in range(2)]
                psO = [PS("C_psO%d" % i, [128, 512], F32) for i in range(4)]
                if QUART:
                    for hp in range(4):
                        S.dma(QTO[hp][:, 0:NQ], (lambda eng, hp=hp: QT[hp][:, bass.ds(qoff(eng), NQ)]), writes=['QTO'])
                        if hp % 2 == 1:
                            S.flush()
                    QTv, YBv = QTO, YBO
                else:
                    QTv, YBv = QT, YB
                for kv in range(2):
                    S.dma(Kt[:, kv, :], KTA[kv * 64:(kv + 1) * 64, 0:T], writes=['Kt'])
                S.op('pool', lambda e: e.memset(Va[:, :, :, 64:65], 1.0), writes=['Va1'])
                for kv in range(2):
                    S.dma(Va[:, :, kv, 0:64], VA[0:T, kv * 64:(kv + 1) * 64].rearrange("(k p) j -> p k j", p=128), writes=['Va'])
                it = 0
                for kv in range(2):
                    for qt in range(NQ // 128):
                        qb = it % 2
                        it += 1
                        qcols = slice(qt * 128, (qt + 1) * 128)
                        for g in range(4):
                            h = kv * 4 + g
                            S.dma(Qg[qb][:, g, :], QTv[h // 2][(h % 2) * 64:(h % 2 + 1) * 64, qt * 128:(qt + 1) * 128], reads=['QTO'], writes=['Qg%d' % qb])
                        def qk(kt):
                            sb = kt % 2
                            S.op('pe', lambda e, sb=sb, kt=kt, kv=kv, qb=qb: e.matmul(out=psS[sb][:], lhsT=Kt[:, kv, kt * 128:(kt + 1) * 128], rhs=Qg[qb][:].rearrange("p g q -> p (g q)"), start=True, stop=True),
                                 reads=['Kt', 'Qg%d' % qb], writes=['psS%d' % sb])
                        qk(0)
                        for kt in range(NKT):
                            sb = kt % 2
                            if kt + 1 < NKT:
                                qk(kt + 1)
                            S.op('act', lambda e, sb=sb: e.activation(out=Pt[sb][:], in_=psS[sb][:], func=AF.Exp, scale=0.125), reads=['psS%d' % sb], writes=['Pt%d' % sb])
                            for g in range(4):
                                S.op('pe', lambda e, sb=sb, g=g, kt=kt, kv=kv: e.matmul(out=psO[g][:, 0:65], lhsT=Pt[sb][:, g * 128:(g + 1) * 128], rhs=Va[:, kt, kv, :], start=(kt == 0), stop=(kt == NKT - 1)),
                                     reads=['Pt%d' % sb, 'Va', 'Va1'], writes=['psO%d' % g])
                        for g in range(4):
                            S.op('dve', lambda e, g=g: e.reciprocal(out=rec[:, g:g + 1], in_=psO[g][:, 64:65]), reads=['psO%d' % g], writes=['rec%d' % g])
                            S.op('act', lambda e, g=g, qb=qb: e.activation(out=ob[qb][:, g, :], in_=psO[g][:, 0:64], func=AF.Copy, scale=rec[:, g:g + 1]), reads=['psO%d' % g, 'rec%d' % g], writes=['ob%d' % qb])
                        S.dma(YBv[qt * 128:(qt + 1) * 128, kv * 256:(kv + 1) * 256], ob[qb][:].rearrange("p g j -> p (g j)"), reads=['ob%d' % qb], writes=['YB'], queue='pool')
                    S.flush()

            with contextlib.ExitStack() as st:
                def T_(name, shape, dt):
                    return st.enter_context(SBT(name, list(shape), dt))
                P = alloc_ffn(st, "D_")
                xt = T_("D_xt", [128, 4, D], F32)
                h2T = T_("D_h2T", [128, KC, 512], BF16)
                wba = T_("D_wba", [128, 4, D], BF16)
                wbb = T_("D_wbb", [128, 4, D], BF16)
                wout = T_("D_wout", [128, KC, D], BF16)
                S.dma(wba[:], WBA.rearrange("f p c -> p f c"), writes=['wba'])
                S.dma(wbb[:], WBB.rearrange("f p c -> p f c"), writes=['wbb'])
                S.dma(wout[:], WOUT.rearrange("f p c -> p f c"), writes=['wout'])
                lnw = T_("D_lnw", [128, 2, RW], F32)
                S.dma(lnw[:, 0, :], lnx_w.partition_broadcast(128), writes=['lnw'])
                S.dma(lnw[:, 1, :], lnx_b.partition_broadcast(128), writes=['lnw'])
                bgT = T_("D_bgT", [128, 16], F32)
                S.dma(bgT[:], b_gate.rearrange("(o p) -> p o", p=128), writes=['bgT'], allow_slow_non_contiguous=True)
                yd = [T_("D_yd%d" % i, [128, 8, 64], F32) for i in range(2)]
                gt = T_("D_gt", [128, 8, 64], F32)
                vt = T_("D_vt", [128, 8, 64], BF16)
                ybt = T_("D_ybt", [128, RW], BF16)
                cf = T_("D_cf", [128, 16], F32)
                st8 = T_("D_st8", [128, 6, 8], F32)
                yab = T_("D_yab", [128, RW], BF16)
                sga = [T_("D_sga%d" % i, [128, 512], F32) for i in range(2)]
                mtmp = [T_("D_mtmp%d" % i, [128, 512], F32) for i in range(2)]
                tA = P['junk'][:, 0:512].rearrange("p (h j) -> p h j", j=64)
                tB = P['junk'][:, 512:1024].rearrange("p (h j) -> p h j", j=64)
                tC = P['tmpn'][:, 0:512].rearrange("p (h j) -> p h j", j=64)
                yaT = P['actT'][:, 8:12, :]
                ybT = P['actT'][:, 12:16, :]
                mT = P['actT'][:, 0:8, :]

                def bc8(ap):
                    return ap.unsqueeze(2).broadcast_to([128, 8, 64])
                if QUART:
                    jobs = [(X1O[0:NQ, :], lambda eng: X1[bass.ds(qoff(eng), NQ), :]),
                            (H2O[:, :, 0:NQ], lambda eng: H2[:, :, bass.ds(qoff(eng) + 1, NQ)]),
                            (YDO[0][0:NQ, :], lambda eng: YD[0][bass.ds(qoff(eng), NQ), :]),
                            (YDO[1][0:NQ, :], lambda eng: YD[1][bass.ds(qoff(eng), NQ), :]),
                            (GO[0:NQ, :], lambda eng: G_TM[bass.ds(qoff(eng), NQ), :]),
                            (VO[0:NQ, :], lambda eng: V_TM[bass.ds(qoff(eng), NQ), :]),
                            (CFO[0:NQ, :], lambda eng: COEFS16[bass.ds(qoff(eng), NQ), :])]
                    for ji, (dst, src) in enumerate(jobs):
                        S.dma(dst, src, writes=['slab'])
                        if ji % 2 == 1:
                            S.flush()
                    S.flush()
                    X1v, H2v, YDv, Gv, Vv, CFv, YBv = X1O, H2O, YDO, GO, VO, CFO, YBO
                else:
                    X1v, H2v, YDv, Gv, Vv, CFv, YBv = X1, H2[:, :, 1:T + 1], YD, G_TM, V_TM, COEFS16, YB
                for ti in range(NQ // 512):
                    S.dma(xt[:], X1v[ti * 512:(ti + 1) * 512, :].rearrange("(s p) d -> p s d", p=128), writes=['xt'])
                    S.dma(h2T[:], H2v[:, :, ti * 512:(ti + 1) * 512], writes=['h2T'])
                    for sub in range(4):
                        rows = slice(ti * 512 + sub * 128, ti * 512 + (sub + 1) * 128)
                        S.dma(yd[0][:].rearrange("p h j -> p (h j)"), YDv[0][rows, :], writes=['yd0'])
                        S.dma(yd[1][:].rearrange("p h j -> p (h j)"), YDv[1][rows, :], writes=['yd1'])
                        S.dma(gt[:].rearrange("p h j -> p (h j)"), Gv[rows, :], writes=['gt'])
                        S.dma(vt[:].rearrange("p h j -> p (h j)"), Vv[rows, :], writes=['vt'])
                        S.dma(ybt[:], YBv[rows, :], writes=['ybt'])
                        S.dma(cf[:], CFv[rows, :], writes=['cf'])
                        S.op('dve', lambda e: e.tensor_tensor(out=yd[0][:], in0=yd[0][:], in1=yd[1][:], op=ALU.add), reads=['yd0', 'yd1'], writes=['yd0'])
                        S.op('dve', lambda e: e.tensor_reduce(out=st8[:, 0, :], in_=yd[0][:], axis=AX.X, op=ALU.add), reads=['yd0'], writes=['st8_0'])
                        S.op('dve', lambda e: e.tensor_scalar(out=st8[:, 1, :], in0=st8[:, 0, :], scalar1=1.0 / 64, scalar2=None, op0=ALU.mult), reads=['st8_0'], writes=['st8_1'])
                        S.op('dve', lambda e: e.tensor_tensor(out=tA, in0=yd[0][:], in1=bc8(st8[:, 1, :]), op=ALU.subtract), reads=['yd0', 'st8_1'], writes=['junkA'])
                        S.op('act', lambda e: e.activation(out=tB, in_=tA, func=AF.Square), reads=['junkA'], writes=['junkB'])
                        S.op('dve', lambda e: e.tensor_reduce(out=st8[:, 2, :], in_=tB, axis=AX.X, op=ALU.add), reads=['junkB'], writes=['st8_2'])
                        S.op('dve', lambda e: e.tensor_scalar(out=st8[:, 3, :], in0=st8[:, 2, :], scalar1=1.0 / 64, scalar2=LNX_EPS, op0=ALU.mult, op1=ALU.add), reads=['st8_2'], writes=['st8_3'])
                        S.op('pool', lambda e: e.tensor_tensor(out=st8[:, 4, :], in0=st8[:, 3, :], in1=mhalf[:, 0:8], op=ALU.pow), reads=['st8_3', 'mhalf'], writes=['st8_4'])
                        S.op('dve', lambda e: e.tensor_tensor(out=tB, in0=tA, in1=bc8(st8[:, 4, :]), op=ALU.mult), reads=['junkA', 'st8_4'], writes=['junkB'])
                        S.op('dve', lambda e: e.tensor_tensor(out=tA, in0=tB, in1=lnw[:, 0, :].rearrange("p (h j) -> p h j", j=64), op=ALU.mult), reads=['junkB', 'lnw'], writes=['junkA'])
                        S.op('dve', lambda e: e.tensor_tensor(out=tA, in0=tA, in1=lnw[:, 1, :].rearrange("p (h j) -> p h j", j=64), op=ALU.add), reads=['junkA', 'lnw'], writes=['junkA'])
                        S.op('dve', lambda e: e.tensor_tensor(out=st8[:, 5, :], in0=cf[:, 0:8], in1=cf[:, 8:16], op=ALU.add), reads=['cf'], writes=['st8_5'])
                        S.op('dve', lambda e: e.tensor_tensor(out=tC, in0=vt[:], in1=bc8(st8[:, 5, :]), op=ALU.mult), reads=['vt', 'st8_5'], writes=['tmpnC'])
                        S.op('dve', lambda e: e.tensor_tensor(out=tA, in0=tA, in1=tC, op=ALU.add), reads=['junkA', 'tmpnC'], writes=['junkA'])
                        S.op('dve', lambda e: e.tensor_tensor(out=yab[:].rearrange("p (h j) -> p h j", j=64), in0=tA, in1=gt[:], op=ALU.mult), reads=['junkA', 'gt'], writes=['yab'])
                        for kc in range(4):
                            S.op('pe', lambda e, kc=kc: e.transpose(out=P['pst'][:, kc, :], in_=yab[:, kc * 128:(kc + 1) * 128], identity=ident[:]), reads=['yab', 'ident'], writes=['pst'])
                            S.op('pe', lambda e, kc=kc: e.transpose(out=P['pst'][:, 4 + kc, :], in_=ybt[:, kc * 128:(kc + 1) * 128], identity=ident[:]), reads=['ybt', 'ident'], writes=['pst'])
                        S.op('dve', lambda e, sub=sub: e.tensor_copy(out=yaT[:, :, sub * 128:(sub + 1) * 128], in_=P['pst'][:, 0:4, :]), reads=['pst'], writes=['actT'])
                        S.op('act', lambda e, sub=sub: e.activation(out=ybT[:, :, sub * 128:(sub + 1) * 128], in_=P['pst'][:, 4:8, :], func=AF.Copy), reads=['pst'], writes=['actT'])
                    for oc in range(8):
                        sl = oc % 3
                        S.dma(P['wg'][sl][:], WIN1[:, :, 2688 + oc * 128:2688 + (oc + 1) * 128].rearrange("k p c -> p k c"), writes=['wg%d' % sl])
                        S.dma(P['wu'][sl][:], WIN1[:, :, 2688 + 1024 + oc * 128:2688 + 1024 + (oc + 1) * 128].rearrange("k p c -> p k c"), writes=['wu%d' % sl])
                        ocs = slice(oc * 128, (oc + 1) * 128)
                        for kc in range(KC):
                            S.op('pe', lambda e, kc=kc, sl=sl: e.matmul(out=P['psg'][0][:], lhsT=P['wg'][sl][:, kc, :], rhs=h2T[:, kc, :], start=(kc == 0), stop=(kc == KC - 1)), reads=['wg%d' % sl, 'h2T'], writes=['psg0'])
                        for kc in range(KC):
                            S.op('pe', lambda e, kc=kc, sl=sl: e.matmul(out=P['psg'][1][:], lhsT=P['wu'][sl][:, kc, :], rhs=h2T[:, kc, :], start=(kc == 0), stop=(kc == KC - 1)), reads=['wu%d' % sl, 'h2T'], writes=['psg1'])
                        for kc in range(4):
                            S.op('pe', lambda e, kc=kc, ocs=ocs: e.matmul(out=P['psu'][0][:], lhsT=wba[:, kc, ocs], rhs=yaT[:, kc, :], start=(kc == 0), stop=(kc == 3)), reads=['wba', 'actT'], writes=['psu0'])
                        for kc in range(4):
                            S.op('pe', lambda e, kc=kc, ocs=ocs: e.matmul(out=P['psu'][1][:], lhsT=wbb[:, kc, ocs], rhs=ybT[:, kc, :], start=(kc == 0), stop=(kc == 3)), reads=['wbb', 'actT'], writes=['psu1'])
                        S.op('act', lambda e, oc=oc: e.activation(out=sga[0][:], in_=P['psg'][0][:], func=AF.Sigmoid, bias=bgT[:, oc:oc + 1]), reads=['psg0', 'bgT'], writes=['sga0'])
                        S.op('act', lambda e, oc=oc: e.activation(out=sga[1][:], in_=P['psg'][1][:], func=AF.Sigmoid, bias=bgT[:, 8 + oc:9 + oc]), reads=['psg1', 'bgT'], writes=['sga1'])
                        S.op('dve', lambda e: e.tensor_tensor(out=mtmp[0][:], in0=P['psu'][0][:], in1=sga[0][:], op=ALU.mult), reads=['psu0', 'sga0'], writes=['mtmp0'])
                        S.op('dve', lambda e: e.tensor_tensor(out=mtmp[1][:], in0=P['psu'][1][:], in1=sga[1][:], op=ALU.mult), reads=['psu1', 'sga1'], writes=['mtmp1'])
                        S.op('pool', lambda e, oc=oc: e.tensor_tensor(out=mT[:, oc, :], in0=mtmp[0][:], in1=mtmp[1][:], op=ALU.add), reads=['mtmp0', 'mtmp1'], writes=['actT'])
                    for sub in range(4):
                        for hf in range(2):
                            for kc in range(KC):
                                S.op('pe', lambda e, kc=kc, sub=sub, hf=hf: e.matmul(out=P['psd'][hf][:], lhsT=mT[:, kc, sub * 128:(sub + 1) * 128], rhs=wout[:, kc, hf * 512:(hf + 1) * 512], start=(kc == 0), stop=(kc == KC - 1)),
                                     reads=['actT', 'wout'], writes=['psd%d' % hf])
                            S.op('act', lambda e, hf=hf: e.activation(out=P['junk'][:, 0:512], in_=P['psd'][hf][:], func=AF.Square, accum_out=P['ssd'][:, hf:hf + 1]),
                                 reads=['psd%d' % hf], writes=['junk', 'junkA', 'ssd'])
                        post_norm_res(S, P, [P['psd'][0], P['psd'][1]], ['psd0', 'psd1'], gpost[1], xt, 'xt', xt, 'xt', sub)
                    for sub in range(4):
                        norm_T(S, P, xt, 'xt', sub, h2T, 'h2T')
                    ffn(S, P, 1, h2T, 'h2T', xt, 'xt', gpost[2], xt, 'xt')
                    S.dma(yout[ti * 512:(ti + 1) * 512, :].rearrange("(s p) d -> p s d", p=128), xt[:], reads=['xt'], writes=['yout'], queue='pool')
                    if ti % 4 == 3 and ti + 1 < NQ // 512:
                        S.flush()
                S.flush()
    return nc


_NC_CACHE = {}


def kernel(x_prompt, x_sample, norm_g, ffn_w_gate, ffn_w_up, ffn_w_down, w_in, mu_shift,
           w0, w_lora_up, a0, a_lora_up, g_lora_up, k_k, k_a, r_k, lnx_w, lnx_b,
           qk_norm_g, w_branch_a, w_branch_b, b_gate, w_out):
    f = lambda a: np.ascontiguousarray(np.asarray(a, dtype=np.float32))
    x_prompt = f(x_prompt)
    x_sample = f(x_sample)
    NP, TP, _ = x_prompt.shape
    NSMP, TS, _ = x_sample.shape
    n = 8
    per = NSMP // n
    seq_lens = [TP] + [TS] * per
    key = tuple(seq_lens)
    if key not in _NC_CACHE:
        _NC_CACHE[key] = build_nc(seq_lens, quarter=(0,))
    nc = _NC_CACHE[key]
    shared = {
        "norm_g": f(norm_g)[0], "ffn_w_gate": f(ffn_w_gate)[0], "ffn_w_up": f(ffn_w_up)[0],
        "ffn_w_down": f(ffn_w_down)[0], "w_in": f(w_in)[0], "mu_shift": f(mu_shift)[0],
        "w0": f(w0)[0], "w_lora_up": f(w_lora_up)[0], "a0": f(a0)[0], "a_lora_up": f(a_lora_up)[0],
        "g_lora_up": f(g_lora_up)[0], "k_k": f(k_k)[0], "k_a": f(k_a)[0], "r_k": f(r_k)[0].reshape(RW),
        "lnx_w": f(lnx_w)[0], "lnx_b": f(lnx_b)[0], "qk_norm_g": f(qk_norm_g)[0],
        "w_branch_a": f(w_branch_a)[0], "w_branch_b": f(w_branch_b)[0],
        "b_gate": f(b_gate)[0].reshape(2 * D), "w_out": f(w_out)[0],
    }
    for k_, v_ in _consts().items():
        shared["c_" + k_] = v_
    for T_ in sorted(set(seq_lens)):
        c_, s_ = _rope_tables(T_)
        shared["cos%d" % T_] = c_
        shared["sin%d" % T_] = s_
    in_maps = []
    cores_per_prompt = n // NP
    for c in range(n):
        m = dict(shared)
        m["x0"] = x_prompt[c // cores_per_prompt]
        for j in range(per):
            m["x%d" % (j + 1)] = x_sample[c * per + j]
        in_maps.append(m)
    res = run_bass_kernel_spmd(nc, in_maps, core_ids=list(range(n)))
    y_prompt = np.stack([np.concatenate([res.results[p * cores_per_prompt + q]["y0"] for q in range(cores_per_prompt)], axis=0)
                         for p in range(NP)], axis=0).astype(np.float32)
    y_sample = np.stack([res.results[c]["y%d" % (j + 1)] for c in range(n) for j in range(per)], axis=0).astype(np.float32)
    return (y_prompt, y_sample)
```

```python
import contextlib
import numpy as np
import ml_dtypes
import concourse.bass as bass
import concourse.mybir as mybir
from concourse.bass_utils import run_bass_kernel_spmd

F32 = mybir.dt.float32
BF16 = mybir.dt.bfloat16
AF = mybir.ActivationFunctionType
ALU = mybir.AluOpType
AX = mybir.AxisListType

D = 1024
KC = 8
FF = 2816
FC = 22
NIN = 4736
RW = 512
NRW = 1920
EPS = 1e-6
LNX_EPS = 64e-5
CH = 64
NS_CAP = 1024
NS_CAP_B = 512


class Sched:
    ENGS = ('pe', 'act', 'dve', 'pool', 'sp')
    NLANES = 16

    def __init__(self, nc, st):
        self.nc = nc
        self.sems = {}
        self.st = st
        self.cnt = {e: 0 for e in self.ENGS}
        self.lane_cnt = {}
        self.lane_rr = {e: 0 for e in self.ENGS}
        self.nflush = 0
        self._reset()

    def _reset(self):
        self.q = {e: [] for e in self.ENGS}
        self.lw = {}
        self.rd = {}
        self.seen = {e: {} for e in self.ENGS}
        self.signaled = {e: set() for e in self.ENGS}

    def _sem(self, name):
        if name not in self.sems:
            self.sems[name] = self.st.enter_context(self.nc.semaphore(name))
        return self.sems[name]

    def _deps(self, reads, writes):
        toks = set()
        for k in reads:
            t = self.lw.get(k)
            if t is not None:
                toks.add(t)
        for k in writes:
            t = self.lw.get(k)
            if t is not None:
                toks.add(t)
            for r in self.rd.get(k, {}).values():
                toks.add(r)
        return toks

    def _commit(self, tok, reads, writes):
        src = tok[1]
        for k in reads:
            self.rd.setdefault(k, {})[src] = tok
        for k in writes:
            self.lw[k] = tok
            self.rd[k] = {}

    def _filter(self, eng, toks):
        best = {}
        for t in toks:
            kind, src, n = t
            if kind == 'c' and src == 'pe' and eng == 'pe':
                continue
            if self.seen[eng].get(src, 0) >= n:
                continue
            if src not in best or best[src][2] < n:
                best[src] = t
        for t in best.values():
            self.seen[eng][t[1]] = t[2]
            if t[0] == 'c':
                self.signaled[t[1]].add(t[2])
        return list(best.values())

    def op(self, eng, fn, reads=(), writes=()):
        toks = self._deps(reads, writes)
        waits = self._filter(eng, toks)
        idx = len(self.q[eng]) + 1
        self.q[eng].append(dict(fn=fn, waits=waits, idx=idx, dma=None))
        self._commit(('c', eng, idx), reads, writes)

    def dma(self, out, in_, reads=(), writes=(), queue='sp', **kw):
        toks = self._deps(reads, writes)
        li = self.lane_rr[queue]
        self.lane_rr[queue] = (li + 1) % self.NLANES
        lane = 'L_%s_%d' % (queue, li)
        c = self.lane_cnt.get(lane, 0)
        if c > 0:
            toks.add(('d', lane, 16 * c))
        self.lane_cnt[lane] = c + 1
        waits = self._filter(queue, toks)
        idx = len(self.q[queue]) + 1
        self.q[queue].append(dict(fn=None, waits=waits, idx=idx, dma=(out, in_, lane, kw)))
        self._commit(('d', lane, 16 * (c + 1)), reads, writes)

    def coll(self, kind, ins, outs, groups, reads=(), writes=()):
        queue = 'pool'
        toks = self._deps(reads, writes)
        lane = 'L_coll'
        c = self.lane_cnt.get(lane, 0)
        if c > 0:
            toks.add(('d', lane, 16 * c))
        self.lane_cnt[lane] = c + 1
        waits = self._filter(queue, toks)
        idx = len(self.q[queue]) + 1
        fn = lambda e: e.collective_compute(kind, ALU.bypass, replica_groups=groups, ins=ins, outs=outs)
        self.q[queue].append(dict(fn=None, waits=waits, idx=idx, dma=(fn, None, lane, None)))
        self._commit(('d', lane, 16 * (c + 1)), reads, writes)

    def flush(self):
        nc = self.nc
        toks = set(('d', lane, 16 * c) for lane, c in self.lane_cnt.items())
        waits = self._filter('sp', toks)
        self.q['sp'].append(dict(fn=None, waits=waits, idx=len(self.q['sp']) + 1, dma=None))
        cmap = {}
        for e in self.ENGS:
            m = {}
            c = self.cnt[e]
            for i in sorted(self.signaled[e]):
                c += 1
                m[i] = c
            cmap[e] = m
            self.cnt[e] = c
            self._sem('s_' + e)
        for lane in self.lane_cnt:
            self._sem(lane)
        sems = self.sems
        engobj = {'pe': 'tensor', 'act': 'scalar', 'dve': 'vector', 'pool': 'gpsimd', 'sp': 'sync'}
        q = self.q

        def make(e):
            def body(eng):
                for o in q[e]:
                    for (kind, src, n) in o['waits']:
                        if kind == 'c':
                            eng.wait_ge(sems['s_' + src], cmap[src][n])
                        else:
                            eng.wait_ge(sems[src], n)
                    if o['dma'] is not None:
                        out, in_, lane, kw = o['dma']
                        if kw is None:
                            out(eng).then_inc(sems[lane], 16)
                        else:
                            if callable(out):
                                out = out(eng)
                            if callable(in_):
                                in_ = in_(eng)
                            try:
                                eng.dma_start(out=out, in_=in_, **kw).then_inc(sems[lane], 16)
                            except Exception:
                                print("DMA FAIL", e, lane, getattr(out, 'shape', None), getattr(in_, 'shape', None), out, in_)
                                raise
                    elif o['fn'] is not None:
                        ins = o['fn'](eng)
                        if o['idx'] in cmap[e]:
                            ins.then_inc(sems['s_' + e], 1)
            return body

        with nc.Block() as block:
            for e in self.ENGS:
                if q[e]:
                    getattr(block, engobj[e])(make(e))
        self.nflush += 1
        self._reset()


def _consts():
    c = {}
    c['ident'] = np.eye(128, dtype=np.float32)
    s = np.arange(64)[:, None]
    t = np.arange(64)[None, :]
    su = (s < t).astype(np.float32)
    iu = (s <= t).astype(np.float32)
    mf = np.block([[su, iu], [su, iu]])
    sl = (s > t).astype(np.float32)
    il = (s >= t).astype(np.float32)
    mb = np.block([[sl, il], [sl, il]])
    c['maskf'] = mf
    c['maskb'] = mb
    nt = np.zeros((2, 128, 64), np.float32)
    nt[0, :64] = (t.T > s.T).astype(np.float32)
    nt[1, :64] = (t.T < s.T).astype(np.float32)
    c['masknt'] = nt
    bo = np.zeros((128, 128), np.float32)
    bo[:64, :64] = 1.0
    bo[64:, 64:] = 1.0
    c['blockones'] = bo
    hs = np.zeros((128, 2), np.float32)
    hs[:64, 0] = 1.0
    hs[64:, 1] = 1.0
    c['headsel'] = hs
    return c


def _rope_tables(T):
    rows = T // 64
    r_idx = np.repeat(np.arange(rows, dtype=np.float32), 64)
    c_idx = np.tile(np.arange(64, dtype=np.float32), rows)
    inv = (10000.0 ** (-np.arange(0, 32, 2, dtype=np.float32) / 32)).astype(np.float32)
    ang = np.concatenate([r_idx[:, None] * inv, c_idx[:, None] * inv], axis=-1)
    return np.cos(ang).astype(np.float32), np.sin(ang).astype(np.float32)


def build_nc(seq_lens, debug=None, quarter=()):
    nc = bass.Bass("TRN2", target_bir_lowering=False)
    TMAX = max(seq_lens)
    NSEQ = len(seq_lens)
    TT = sum(seq_lens)

    def din(name, shape, dt=F32):
        return nc.dram_tensor(name, list(shape), dt, kind="ExternalInput").ap()

    def dscr(name, shape, dt):
        return nc.dram_tensor(name, list(shape), dt, kind="Internal").ap()

    xs_in = [din("x%d" % i, [T, D]) for i, T in enumerate(seq_lens)]
    ys_out = [nc.dram_tensor("y%d" % i, [T // 4 if i in quarter else T, D], F32, kind="ExternalOutput").ap()
              for i, T in enumerate(seq_lens)]
    cos_in = {T: din("cos%d" % T, [T, 32]) for T in sorted(set(seq_lens))}
    sin_in = {T: din("sin%d" % T, [T, 32]) for T in sorted(set(seq_lens))}
    norm_g = din("norm_g", [6, D])
    w_gate = din("ffn_w_gate", [2, D, FF])
    w_up = din("ffn_w_up", [2, D, FF])
    w_down = din("ffn_w_down", [2, FF, D])
    w_in = din("w_in", [D, NIN])
    mu = din("mu_shift", [NRW])
    w0 = din("w0", [2, RW])
    wl_up = din("w_lora_up", [2, 64, RW])
    a0 = din("a0", [2, RW])
    al_up = din("a_lora_up", [2, 64, RW])
    gl_up = din("g_lora_up", [128, RW])
    k_k = din("k_k", [RW])
    k_a = din("k_a", [RW])
    r_k = din("r_k", [RW])
    lnx_w = din("lnx_w", [RW])
    lnx_b = din("lnx_b", [RW])
    qk_g = din("qk_norm_g", [2, 64])
    w_ba = din("w_branch_a", [RW, D])
    w_bb = din("w_branch_b", [RW, D])
    b_gate = din("b_gate", [2 * D])
    w_out = din("w_out", [D, D])
    C = _consts()
    cin = {k: din("c_" + k, v.shape) for k, v in C.items()}

    dbg = {}
    if debug:
        for name, shape, dt in debug:
            dbg[name] = nc.dram_tensor("dbg_" + name, list(shape), dt, kind="ExternalOutput").ap()

    WG = [dscr("WG%d" % l, [FC, 128, KC, 128], BF16) for l in range(2)]
    WU = [dscr("WU%d" % l, [FC, 128, KC, 128], BF16) for l in range(2)]
    WD = [dscr("WD%d" % l, [FC, 128, D], BF16) for l in range(2)]
    WIN1 = dscr("WIN1", [KC, 128, NIN], BF16)
    WIN2 = dscr("WIN2", [KC, 128, NRW], BF16)
    WBA = dscr("WBA", [4, 128, D], BF16)
    WBB = dscr("WBB", [4, 128, D], BF16)
    WOUT = dscr("WOUT", [KC, 128, D], BF16)
    X1 = dscr("X1", [TMAX, D], F32)
    H2 = dscr("H2", [128, KC, TMAX + 2], BF16)
    R_T = dscr("R_T", [4, 128, TMAX], F32)
    K_T = dscr("K_T", [4, 128, TMAX], F32)
    LW = dscr("LW", [2, 4, 128, TMAX], F32)
    AA = dscr("AA", [2, 4, 128, TMAX], F32)
    V_TM = dscr("V_TM", [TMAX, RW], BF16)
    VF_TM = dscr("VF_TM", [TMAX, RW], F32)
    G_TM = dscr("G_TM", [TMAX, RW], F32)
    QT = dscr("QT", [4, 128, TMAX], BF16)
    KTA = dscr("KTA", [128, TMAX], BF16)
    VA = dscr("VA", [TMAX, 128], BF16)
    YD = [dscr("YD%d" % d, [TMAX, RW], F32) for d in range(2)]
    COEF = [dscr("COEF%d" % d, [TMAX, 8], F32) for d in range(2)]
    YB = dscr("YB", [TMAX, RW], BF16)
    COEFS16 = dscr("COEFS16", [TMAX, 16], F32)
    TQ = max([seq_lens[i] // 4 for i in quarter] + [128])
    QTO = dscr("QTO", [4, 128, TQ], BF16)
    YBO = dscr("YBO", [TQ, RW], BF16)
    X1O = dscr("X1O", [TQ, D], F32)
    H2O = dscr("H2O", [128, KC, TQ], BF16)
    YDO = [dscr("YDO%d" % d, [TQ, RW], F32) for d in range(2)]
    GO = dscr("GO", [TQ, RW], F32)
    VO = dscr("VO", [TQ, RW], BF16)
    CFO = dscr("CFO", [TQ, 16], F32)

    SFX = [""]
    PIDC = {}

    def SBT(name, shape, dt):
        return nc.sbuf_tensor(name + SFX[0], shape, dt)

    def PST(name, shape, dt):
        return nc.psum_tensor(name + SFX[0], shape, dt)

    gst = contextlib.ExitStack()
    with gst:
        def GT(name, shape, dt):
            return gst.enter_context(SBT(name, list(shape), dt))

        S = Sched(nc, gst)
        ident_f = GT("ident_f", [128, 128], F32)
        ident = GT("ident", [128, 128], BF16)
        mhalf = GT("mhalf", [128, 16], F32)
        S.dma(ident_f[:], cin['ident'], writes=['ident_f'])
        S.op('dve', lambda e: e.tensor_copy(out=ident[:], in_=ident_f[:]), reads=['ident_f'], writes=['ident'])
        S.op('pool', lambda e: e.memset(mhalf[:], -0.5), writes=['mhalf'])
        gpost = [GT("gpost%d" % i, [128, D], F32) for i in range(3)]
        for i, (gi, sc) in enumerate([(1, 0.5), (3, 1.0), (5, 0.5)]):
            S.dma(gpost[i][:], norm_g[gi].partition_broadcast(128), writes=['gpost%d' % i])
            S.op('dve', lambda e, i=i, sc=sc: e.tensor_scalar(out=gpost[i][:], in0=gpost[i][:], scalar1=sc, scalar2=None, op0=ALU.mult),
                 reads=['gpost%d' % i], writes=['gpost%d' % i])
        S.flush()

        with contextlib.ExitStack() as st:
            def T_(name, shape, dt):
                return st.enter_context(SBT(name, list(shape), dt))
            gT = T_("gT", [128, 3, KC], F32)
            for i, gi in enumerate([0, 2, 4]):
                S.dma(gT[:, i, :], norm_g[gi].rearrange("(kc p) -> p kc", p=128), writes=['gT'],
                      allow_slow_non_contiguous=True)
            omm = T_("omm", [128, NRW], F32)
            hmu = T_("hmu", [128, NRW], F32)
            S.dma(omm[:], mu.partition_broadcast(128), writes=['omm'])
            S.op('dve', lambda e: e.tensor_scalar(out=hmu[:], in0=omm[:], scalar1=0.5, scalar2=None, op0=ALU.mult), reads=['omm'], writes=['hmu'])
            S.op('dve', lambda e: e.tensor_scalar(out=omm[:], in0=omm[:], scalar1=-1.0, scalar2=1.0, op0=ALU.mult, op1=ALU.add), reads=['omm', 'hmu'], writes=['omm'])
            stg = [T_("stg%d" % i, [128, NIN], F32) for i in range(2)]
            stb = [T_("stb%d" % i, [128, NIN], BF16) for i in range(2)]
            stb2 = [T_("stb2%d" % i, [128, NRW], BF16) for i in range(2)]
            cnt = [0]

            def cast_job(src_ap, ncols, dst_ap, gcol=None, rw=None, dst2_ap=None, fin=None, fout=None):
                i = cnt[0] % 2
                cnt[0] += 1
                sk, bk, b2k = 'stg%d' % i, 'stb%d' % i, 'stb2%d' % i
                sv = stg[i][:, 0:ncols]
                if fin:
                    sv = sv.rearrange("p (f c) -> p f c", c=fin)
                S.dma(sv, src_ap, writes=[sk])
                eng = 'act' if (cnt[0] % 2) else 'dve'
                if rw is None:
                    if gcol is None:
                        if eng == 'act':
                            S.op('act', lambda e: e.activation(out=stb[i][:, 0:ncols], in_=stg[i][:, 0:ncols], func=AF.Copy), reads=[sk], writes=[bk])
                        else:
                            S.op('dve', lambda e: e.tensor_copy(out=stb[i][:, 0:ncols], in_=stg[i][:, 0:ncols]), reads=[sk], writes=[bk])
                    else:
                        if eng == 'act':
                            S.op('act', lambda e: e.activation(out=stb[i][:, 0:ncols], in_=stg[i][:, 0:ncols], func=AF.Copy, scale=gcol), reads=[sk, 'gT'], writes=[bk])
                        else:
                            S.op('dve', lambda e: e.tensor_scalar(out=stb[i][:, 0:ncols], in0=stg[i][:, 0:ncols], scalar1=gcol, scalar2=None, op0=ALU.mult), reads=[sk, 'gT'], writes=[bk])
                else:
                    S.op('act', lambda e: e.activation(out=stb[i][:, NRW:ncols], in_=stg[i][:, NRW:ncols], func=AF.Copy, scale=gcol), reads=[sk, 'gT'], writes=[bk])
                    S.op('act', lambda e: e.activation(out=stg[i][:, 0:NRW], in_=stg[i][:, 0:NRW], func=AF.Copy, scale=gcol), reads=[sk, 'gT'], writes=[sk])
                    S.op('dve', lambda e: e.tensor_tensor(out=stb2[i][:], in0=stg[i][:, 0:NRW], in1=hmu[:], op=ALU.mult), reads=[sk, 'hmu'], writes=[b2k])
                    S.op('dve', lambda e: e.tensor_tensor(out=stb[i][:, 0:NRW], in0=stg[i][:, 0:NRW], in1=omm[:], op=ALU.mult), reads=[sk, 'omm'], writes=[bk])
                    S.dma(dst2_ap, stb2[i][:], reads=[b2k], writes=['wscr'], queue='pool')
                bv = stb[i][:, 0:ncols]
                if fout:
                    bv = bv.rearrange("p (f c) -> p f c", c=fout)
                S.dma(dst_ap, bv, reads=[bk], writes=['wscr'], queue='pool')

            for l in range(2):
                gi = 0 if l == 0 else 2
                for kc in range(KC):
                    rows = slice(kc * 128, (kc + 1) * 128)
                    for (Wsrc, Wdst) in ((w_gate, WG), (w_up, WU)):
                        cast_job(Wsrc[l, rows, :], FF,
                                 Wdst[l][:, :, kc, :].rearrange("f p c -> p f c"),
                                 gcol=gT[:, gi, kc:kc + 1], fout=128)
                for fc in range(0, FC, 4):
                    n = min(4, FC - fc)
                    cast_job(w_down[l, fc * 128:(fc + n) * 128, :].rearrange("(f p) c -> p f c", p=128), n * D,
                             WD[l][fc:fc + n].rearrange("f p c -> p f c"), fin=D, fout=D)
            for kc in range(KC):
                rows = slice(kc * 128, (kc + 1) * 128)
                cast_job(w_in[rows, :], NIN, WIN1[kc], gcol=gT[:, 1, kc:kc + 1], rw=True, dst2_ap=WIN2[kc])
            cast_job(w_ba.rearrange("(f p) c -> p f c", p=128), 4 * D, WBA.rearrange("f p c -> p f c"), fin=D, fout=D)
            cast_job(w_bb.rearrange("(f p) c -> p f c", p=128), 4 * D, WBB.rearrange("f p c -> p f c"), fin=D, fout=D)
            for h in range(2):
                cast_job(w_out[h * 512:(h + 1) * 512, :].rearrange("(f p) c -> p f c", p=128), 4 * D,
                         WOUT[h * 4:(h + 1) * 4].rearrange("f p c -> p f c"), fin=D, fout=D)
            S.flush()

        def norm_T(S, P, xin, xin_key, sub, hT, hT_key):
            S.op('act', lambda e: e.activation(out=P['junk'][:], in_=xin[:, sub, :], func=AF.Square, accum_out=P['ss'][:, 0:1]),
                 reads=[xin_key], writes=['junk', 'ss'])
            S.op('dve', lambda e: e.tensor_scalar(out=P['ss'][:, 1:2], in0=P['ss'][:, 0:1], scalar1=1.0 / D, scalar2=EPS, op0=ALU.mult, op1=ALU.add),
                 reads=['ss'], writes=['ss1'])
            S.op('pool', lambda e: e.tensor_tensor(out=P['ss'][:, 2:3], in0=P['ss'][:, 1:2], in1=mhalf[:, 0:1], op=ALU.pow),
                 reads=['ss1', 'mhalf'], writes=['ss2'])
            S.op('act', lambda e: e.activation(out=P['xnb'][:], in_=xin[:, sub, :], func=AF.Copy, scale=P['ss'][:, 2:3]),
                 reads=[xin_key, 'ss2'], writes=['xnb'])
            for kc in range(KC):
                S.op('pe', lambda e, kc=kc: e.transpose(out=P['pst'][:, kc, :], in_=P['xnb'][:, kc * 128:(kc + 1) * 128], identity=ident[:]),
                     reads=['xnb', 'ident'], writes=['pst'])
            S.op('dve', lambda e: e.tensor_copy(out=hT[:, :, sub * 128:(sub + 1) * 128], in_=P['pst'][:]),
                 reads=['pst'], writes=[hT_key])

        def ffn(S, P, l, hT, hT_key, xres, xres_key, gp, xout, xout_key):
            S.dma(P['wd'][:], WD[l].rearrange("f p c -> p f c"), writes=['wd'])
            for fc in range(FC):
                sl = fc % 3
                S.dma(P['wg'][sl][:], WG[l][fc], writes=['wg%d' % sl])
                S.dma(P['wu'][sl][:], WU[l][fc], writes=['wu%d' % sl])
                b = fc % 2
                for kc in range(KC):
                    S.op('pe', lambda e, kc=kc, sl=sl, b=b: e.matmul(out=P['psg'][b][:], lhsT=P['wg'][sl][:, kc, :], rhs=hT[:, kc, :], start=(kc == 0), stop=(kc == KC - 1)),
                         reads=['wg%d' % sl, hT_key], writes=['psg%d' % b])
                for kc in range(KC):
                    S.op('pe', lambda e, kc=kc, sl=sl, b=b: e.matmul(out=P['psu'][b][:], lhsT=P['wu'][sl][:, kc, :], rhs=hT[:, kc, :], start=(kc == 0), stop=(kc == KC - 1)),
                         reads=['wu%d' % sl, hT_key], writes=['psu%d' % b])
                S.op('act', lambda e, b=b: e.activation(out=P['sg'][b][:], in_=P['psg'][b][:], func=AF.Silu),
                     reads=['psg%d' % b], writes=['sg%d' % b])
                S.op('dve', lambda e, b=b, fc=fc: e.tensor_tensor(out=P['actT'][:, fc, :], in0=P['psu'][b][:], in1=P['sg'][b][:], op=ALU.mult),
                     reads=['psu%d' % b, 'sg%d' % b], writes=['actT'])
            for sub in range(4):
                bks = [(2 * sub + hf) % 3 for hf in range(2)]
                for hf in range(2):
                    bk = bks[hf]
                    for fc in range(FC):
                        S.op('pe', lambda e, fc=fc, sub=sub, hf=hf, bk=bk: e.matmul(out=P['psd'][bk][:], lhsT=P['actT'][:, fc, sub * 128:(sub + 1) * 128], rhs=P['wd'][:, fc, hf * 512:(hf + 1) * 512], start=(fc == 0), stop=(fc == FC - 1)),
                             reads=['actT', 'wd'], writes=['psd%d' % bk])
                    S.op('act', lambda e, hf=hf, bk=bk: e.activation(out=P['junk'][:, 0:512], in_=P['psd'][bk][:], func=AF.Square, accum_out=P['ssd'][:, hf:hf + 1]),
                         reads=['psd%d' % bk], writes=['junk', 'ssd'])
                post_norm_res(S, P, [P['psd'][bks[0]], P['psd'][bks[1]]], ['psd%d' % bks[0], 'psd%d' % bks[1]], gp, xres, xres_key, xout, xout_key, sub)

        def post_norm_res(S, P, ps, ps_keys, gp, xres, xres_key, xout, xout_key, sub):
            S.op('dve', lambda e: e.tensor_tensor(out=P['ssd'][:, 2:3], in0=P['ssd'][:, 0:1], in1=P['ssd'][:, 1:2], op=ALU.add),
                 reads=['ssd'], writes=['ssd2'])
            S.op('dve', lambda e: e.tensor_scalar(out=P['ssd'][:, 3:4], in0=P['ssd'][:, 2:3], scalar1=1.0 / D, scalar2=EPS, op0=ALU.mult, op1=ALU.add),
                 reads=['ssd2'], writes=['ssd3'])
            S.op('pool', lambda e: e.tensor_tensor(out=P['ssd'][:, 4:5], in0=P['ssd'][:, 3:4], in1=mhalf[:, 0:1], op=ALU.pow),
                 reads=['ssd3', 'mhalf'], writes=['ssd4'])
            for hf in range(2):
                cs = slice(hf * 512, (hf + 1) * 512)
                S.op('act', lambda e, hf=hf, cs=cs: e.activation(out=P['tmpn'][:, cs], in_=ps[hf][:], func=AF.Copy, scale=P['ssd'][:, 4:5]),
                     reads=[ps_keys[hf], 'ssd4'], writes=['tmpn%d' % hf])
                S.op('dve', lambda e, hf=hf, cs=cs: e.tensor_tensor(out=P['tmpn'][:, cs], in0=P['tmpn'][:, cs], in1=gp[:, cs], op=ALU.mult),
                     reads=['tmpn%d' % hf], writes=['tmpn%d' % hf])
                S.op('pool', lambda e, hf=hf, cs=cs: e.tensor_tensor(out=xout[:, sub, cs], in0=P['tmpn'][:, cs], in1=xres[:, sub, cs], op=ALU.add),
                     reads=['tmpn%d' % hf, xres_key], writes=[xout_key])

        def alloc_ffn(st, pfx):
            def T_(name, shape, dt):
                return st.enter_context(SBT(pfx + name, list(shape), dt))

            def PS(name, shape, dt):
                return st.enter_context(PST(pfx + name, list(shape), dt))
            P = {}
            P['junk'] = T_("junk", [128, D], F32)
            P['ss'] = T_("ss", [128, 4], F32)
            P['ssd'] = T_("ssd", [128, 8], F32)
            P['xnb'] = T_("xnb", [128, D], BF16)
            P['tmpn'] = T_("tmpn", [128, D], F32)
            P['wd'] = T_("wd", [128, FC, D], BF16)
            P['wg'] = [T_("wg%d" % i, [128, KC, 128], BF16) for i in range(3)]
            P['wu'] = [T_("wu%d" % i, [128, KC, 128], BF16) for i in range(3)]
            P['sg'] = [T_("sg%d" % i, [128, 512], F32) for i in range(2)]
            P['actT'] = T_("actT", [128, FC, 512], BF16)
            P['pst'] = PS("pst", [128, KC, 128], BF16)
            P['psg'] = [PS("psg%d" % i, [128, 512], F32) for i in range(2)]
            P['psu'] = [PS("psu%d" % i, [128, 512], F32) for i in range(2)]
            P['psd'] = [PS("psd%d" % i, [128, 512], F32) for i in range(3)]
            return P

        for si, T in enumerate(seq_lens):
            NT = T // 512
            SFX[0] = "_q%d" % si
            QUART = si in quarter
            NQ = T // 4 if QUART else T

            def qoff(eng, NQ=NQ):
                key = (str(eng.engine), S.nflush)
                if key not in PIDC:
                    PIDC[key] = eng.snap((eng.partition_id() % 4) * NQ)
                return PIDC[key]

            def rowsl(base, size):
                if not QUART:
                    return lambda eng: slice(base, base + size)
                return lambda eng: bass.ds(qoff(eng) + base, size)
            xin = xs_in[si]
            yout = ys_out[si]
            with contextlib.ExitStack() as st:
                def T_(name, shape, dt):
                    return st.enter_context(SBT(name, list(shape), dt))
                P = alloc_ffn(st, "A_")
                xres = [T_("A_xres%d" % i, [128, 4, D], F32) for i in range(2)]
                x1 = [T_("A_x1%d" % i, [128, 4, D], F32) for i in range(2)]
                hT = T_("A_hT", [128, KC, 512], BF16)
                h2T = [T_("A_h2T%d" % i, [128, KC, 512], BF16) for i in range(2)]
                zc = T_("A_zc", [128, KC, 1], BF16)
                S.op('pool', lambda e: e.memset(zc[:], 0.0), writes=['zc'])
                S.dma(H2[:, :, 0:1], zc[:], reads=['zc'], writes=['H2pad'], allow_slow_non_contiguous=True)
                S.dma(H2[:, :, T + 1:T + 2], zc[:], reads=['zc'], writes=['H2pad'], allow_slow_non_contiguous=True)
                for ti in range(NT):
                    b = ti % 2
                    xk, x1k, h2k = 'xres%d' % b, 'x1%d' % b, 'h2T%d' % b
                    S.dma(xres[b][:], xin[ti * 512:(ti + 1) * 512, :].rearrange("(s p) d -> p s d", p=128), writes=[xk])
                    for sub in range(4):
                        norm_T(S, P, xres[b], xk, sub, hT, 'hT')
                    ffn(S, P, 0, hT, 'hT', xres[b], xk, gpost[0], x1[b], x1k)
                    S.dma(X1[ti * 512:(ti + 1) * 512, :].rearrange("(s p) d -> p s d", p=128), x1[b][:], reads=[x1k], writes=['X1'], queue='pool')
                    for sub in range(4):
                        norm_T(S, P, x1[b], x1k, sub, h2T[b], h2k)
                    S.dma(H2[:, :, 1 + ti * 512:1 + (ti + 1) * 512], h2T[b][:], reads=[h2k], writes=['H2'], queue='pool')
                S.flush()
            if debug and 'X1' in dbg:
                with contextlib.ExitStack() as st:
                    t1 = st.enter_context(SBT("dbg_t1", [128, T // 128, D], F32))
                    S.dma(t1[:], X1[0:T, :].rearrange("(s p) d -> p s d", p=128), writes=['t1'])
                    S.dma(dbg['X1'].rearrange("(s p) d -> p s d", p=128), t1[:], reads=['t1'], writes=['o'])
                    t2 = st.enter_context(SBT("dbg_t2", [128, KC, T + 2], BF16))
                    S.dma(t2[:], H2[:, :, 0:T + 2], writes=['t2'])
                    S.dma(dbg['H2'], t2[:], reads=['t2'], writes=['o2'])
                    S.flush()
                continue

            with contextlib.ExitStack() as st:
                def T_(name, shape, dt):
                    return st.enter_context(SBT(name, list(shape), dt))

                def PS(name, shape, dt):
                    return st.enter_context(PST(name, list(shape), dt))
                win1 = T_("B_win1", [128, KC, NIN], BF16)
                win2 = T_("B_win2", [128, KC, NRW], BF16)
                for kc in range(KC):
                    S.dma(win1[:, kc, :], WIN1[kc], writes=['win1'])
                    S.dma(win2[:, kc, :], WIN2[kc], writes=['win2'])
                lstg = T_("B_lstg", [128, 3, RW], F32)
                lw_b = T_("B_lw_b", [128, 3, RW], BF16)
                S.dma(lstg[:, 0, :], wl_up.rearrange("d r c -> (d r) c"), writes=['lstg'])
                S.dma(lstg[:, 1, :], al_up.rearrange("d r c -> (d r) c"), writes=['lstg'])
                S.dma(lstg[:, 2, :], gl_up, writes=['lstg'])
                S.op('dve', lambda e: e.tensor_copy(out=lw_b[:], in_=lstg[:]), reads=['lstg'], writes=['lw_b'])
                w0T = T_("B_w0T", [128, 2, 4], F32)
                a0T = T_("B_a0T", [128, 2, 4], F32)
                S.dma(w0T[:], w0.rearrange("d (h p) -> p d h", p=128), writes=['w0T'], allow_slow_non_contiguous=True)
                S.dma(a0T[:], a0.rearrange("d (h p) -> p d h", p=128), writes=['a0T'], allow_slow_non_contiguous=True)
                g64 = T_("B_g64", [128, 2, 64], F32)
                S.dma(g64[:, 0, :], qk_g[0].partition_broadcast(128), writes=['g64'])
                S.dma(g64[:, 1, :], qk_g[1].partition_broadcast(128), writes=['g64'])
                gq = T_("B_gq", [128, 8, 64], F32)
                gk = T_("B_gk", [128, 2, 64], F32)
                S.op('dve', lambda e: e.tensor_copy(out=gq[:], in_=g64[:, 0:1, :].broadcast_to([128, 8, 64])), reads=['g64'], writes=['gq'])
                S.op('dve', lambda e: e.tensor_copy(out=gk[:], in_=g64[:, 1:2, :].broadcast_to([128, 2, 64])), reads=['g64'], writes=['gk'])
                hext = [T_("B_hext%d" % i, [128, KC, 514], BF16) for i in range(2)]
                hs = T_("B_hs", [128, KC, 512], BF16)
                fst = [T_("B_fst%d" % i, [128, 512], F32) for i in range(3)]
                tw = T_("B_tw", [128, 512], BF16)
                ta = T_("B_ta", [128, 512], BF16)
                tg = T_("B_tg", [128, 512], BF16)
                vst = [T_("B_vst%d" % i, [128, 512], BF16) for i in range(2)]
                cs = [T_("B_cs%d" % i, [128, 2, 32], F32) for i in range(2)]
                nq = T_("B_nq", [128, 8, 64], F32)
                nq2 = T_("B_nq2", [128, 8, 64], F32)
                rt = [T_("B_rt%d" % i, [128, 8, 32], F32) for i in range(4)]
                qrb = [T_("B_qr%d" % i, [128, 8, 64], BF16) for i in range(2)]
                krb = [T_("B_kr%d" % i, [128, 2, 64], BF16) for i in range(2)]
                pending = []
                nss = T_("B_nss", [128, 3, 8], F32)
                qTs = T_("B_qTs", [128, 4, 128], BF16)
                kTs = T_("B_kTs", [128, 128], BF16)
                vas = T_("B_vas", [128, 128], BF16)
                psF = [PS("B_psF%d" % i, [128, 512], F32) for i in range(2)]
                psT = [PS("B_psT%d" % i, [128, 512], F32) for i in range(2)]
                psl = [PS("B_psl%d" % i, [128, 512], F32) for i in range(2)]
                pstr = PS("B_pstr", [128, 8, 128], BF16)
                fcnt = [0]
                lcnt = [0]
                tcnt = [0]

                def fm_proj(hb, co):
                    b = fcnt[0] % 2
                    fcnt[0] += 1
                    for kc in range(KC):
                        S.op('pe', lambda e, kc=kc, b=b: e.matmul(out=psF[b][:], lhsT=win1[:, kc, co:co + 128], rhs=hext[hb][:, kc, 1:513], start=(kc == 0), stop=False),
                             reads=['win1', 'hext%d' % hb], writes=['psF%d' % b])
                    for kc in range(KC):
                        S.op('pe', lambda e, kc=kc, b=b: e.matmul(out=psF[b][:], lhsT=win2[:, kc, co:co + 128], rhs=hs[:, kc, :], start=False, stop=(kc == KC - 1)),
                             reads=['win2', 'hs'], writes=['psF%d' % b])
                    return b

                def norm_rope(ps, ps_key, nh, gain, gain_key, cb, outb, out_key):
                    pv = ps.rearrange("p (h j) -> p h j", j=64)
                    S.op('act', lambda e: e.activation(out=nq[:, 0:nh, :], in_=pv, func=AF.Square), reads=[ps_key], writes=['nq'])
                    S.op('dve', lambda e: e.tensor_reduce(out=nss[:, 0, 0:nh], in_=nq[:, 0:nh, :], axis=AX.X, op=ALU.add), reads=['nq'], writes=['nss0'])
                    S.op('dve', lambda e: e.tensor_scalar(out=nss[:, 1, 0:nh], in0=nss[:, 0, 0:nh], scalar1=1.0 / 64, scalar2=EPS, op0=ALU.mult, op1=ALU.add), reads=['nss0'], writes=['nss1'])
                    S.op('pool', lambda e: e.tensor_tensor(out=nss[:, 2, 0:nh], in0=nss[:, 1, 0:nh], in1=mhalf[:, 0:nh], op=ALU.pow), reads=['nss1', 'mhalf'], writes=['nss2'])
                    S.op('dve', lambda e: e.tensor_tensor(out=nq2[:, 0:nh, :], in0=pv, in1=nss[:, 2, 0:nh].unsqueeze(2).broadcast_to([128, nh, 64]), op=ALU.mult), reads=[ps_key, 'nss2'], writes=['nq2'])
                    S.op('dve', lambda e: e.tensor_tensor(out=nq[:, 0:nh, :], in0=nq2[:, 0:nh, :], in1=gain[:, 0:nh, :], op=ALU.mult), reads=['nq2', gain_key], writes=['nq'])
                    x0 = nq[:, 0:nh, 0:64:2]
                    x1 = nq[:, 0:nh, 1:64:2]
                    cc = cs[cb][:, 0:1, :].broadcast_to([128, nh, 32])
                    sn = cs[cb][:, 1:2, :].broadcast_to([128, nh, 32])
                    ck = 'cs%d' % cb
                    S.op('dve', lambda e: e.tensor_tensor(out=rt[0][:, 0:nh, :], in0=x0, in1=cc, op=ALU.mult), reads=['nq', ck], writes=['rt0'])
                    S.op('dve', lambda e: e.tensor_tensor(out=rt[1][:, 0:nh, :], in0=x1, in1=sn, op=ALU.mult), reads=['nq', ck], writes=['rt1'])
                    S.op('dve', lambda e: e.tensor_tensor(out=rt[2][:, 0:nh, :], in0=x0, in1=sn, op=ALU.mult), reads=['nq', ck], writes=['rt2'])
                    S.op('dve', lambda e: e.tensor_tensor(out=rt[3][:, 0:nh, :], in0=x1, in1=cc, op=ALU.mult), reads=['nq', ck], writes=['rt3'])
                    S.op('dve', lambda e: e.tensor_tensor(out=outb[:, 0:nh, 0:32], in0=rt[0][:, 0:nh, :], in1=rt[1][:, 0:nh, :], op=ALU.subtract), reads=['rt0', 'rt1'], writes=[out_key])
                    S.op('dve', lambda e: e.tensor_tensor(out=outb[:, 0:nh, 32:64], in0=rt[2][:, 0:nh, :], in1=rt[3][:, 0:nh, :], op=ALU.add), reads=['rt2', 'rt3'], writes=[out_key])

                for ti in range(NT):
                    hb = ti % 2
                    tok = slice(ti * 512, (ti + 1) * 512)
                    S.dma(hext[hb][:], H2[:, :, ti * 512:ti * 512 + 514], writes=['hext%d' % hb])
                    S.op('pool', lambda e, hb=hb: e.tensor_tensor(out=hs[:], in0=hext[hb][:, :, 0:512], in1=hext[hb][:, :, 2:514], op=ALU.add),
                         reads=['hext%d' % hb], writes=['hs'])
                    for (co0, DST) in ((0, R_T), (512, K_T)):
                        for hp in range(4):
                            b = fm_proj(hb, co0 + hp * 128)
                            if co0 == 0 and hp == 1 and pending:
                                pending.pop(0)()
                            f = fcnt[0] % 3
                            S.op('act', lambda e, b=b, f=f: e.activation(out=fst[f][:], in_=psF[b][:], func=AF.Copy), reads=['psF%d' % b], writes=['fst%d' % f])
                            S.dma(DST[hp][:, tok], fst[f][:], reads=['fst%d' % f], writes=['rk_scr'], queue='pool')
                    b = fm_proj(hb, 1536)
                    S.op('act', lambda e, b=b: e.activation(out=tw[:], in_=psF[b][:], func=AF.Tanh), reads=['psF%d' % b], writes=['tw'])
                    b = fm_proj(hb, 1664)
                    S.op('act', lambda e, b=b: e.activation(out=ta[:], in_=psF[b][:], func=AF.Copy), reads=['psF%d' % b], writes=['ta'])
                    b = fm_proj(hb, 1792)
                    S.op('act', lambda e, b=b: e.activation(out=tg[:], in_=psF[b][:], func=AF.Sigmoid), reads=['psF%d' % b], writes=['tg'])
                    for (wi, src, src_key, biasT, bias_key, DST, scl) in ((0, tw, 'tw', w0T, 'w0T', LW, -0.6065306597126334), (1, ta, 'ta', a0T, 'a0T', AA, None)):
                        for d in range(2):
                            for hp in range(4):
                                lb = lcnt[0] % 2
                                lcnt[0] += 1
                                S.op('pe', lambda e, lb=lb, wi=wi, d=d, hp=hp, src=src: e.matmul(out=psl[lb][:], lhsT=lw_b[d * 64:(d + 1) * 64, wi, hp * 128:(hp + 1) * 128], rhs=src[d * 64:(d + 1) * 64, :], start=True, stop=True),
                                     reads=['lw_b', src_key], writes=['psl%d' % lb])
                                f = lcnt[0] % 3
                                S.op('act', lambda e, lb=lb, f=f, d=d, hp=hp, biasT=biasT: e.activation(out=fst[f][:], in_=psl[lb][:], func=AF.Sigmoid, bias=biasT[:, d, hp:hp + 1]),
                                     reads=['psl%d' % lb, bias_key], writes=['fst%d' % f])
                                if scl is not None:
                                    S.op('dve', lambda e, f=f, scl=scl: e.tensor_scalar(out=fst[f][:], in0=fst[f][:], scalar1=scl, scalar2=None, op0=ALU.mult), reads=['fst%d' % f], writes=['fst%d' % f])
                                S.dma(DST[d, hp][:, tok], fst[f][:], reads=['fst%d' % f], writes=['la_scr'], queue='pool')
                    for sub in range(4):
                        rows = slice(ti * 512 + sub * 128, ti * 512 + (sub + 1) * 128)
                        scol = slice(sub * 128, (sub + 1) * 128)
                        tb = tcnt[0] % 2
                        tcnt[0] += 1
                        S.op('pe', lambda e, tb=tb, scol=scol: e.matmul(out=psT[tb][:], lhsT=tg[:, scol], rhs=lw_b[:, 2, :], start=True, stop=True),
                             reads=['tg', 'lw_b'], writes=['psT%d' % tb])
                        f = tcnt[0] % 3
                        S.op('act', lambda e, tb=tb, f=f: e.activation(out=fst[f][:], in_=psT[tb][:], func=AF.Copy), reads=['psT%d' % tb], writes=['fst%d' % f])
                        S.dma(G_TM[rows, :], fst[f][:], reads=['fst%d' % f], writes=['g_scr'], queue='pool')
                        tb = tcnt[0] % 2
                        tcnt[0] += 1
                        for kc in range(KC):
                            S.op('pe', lambda e, kc=kc, tb=tb, sub=sub, hb=hb: e.matmul(out=psT[tb][:], lhsT=hext[hb][:, kc, 1 + sub * 128:1 + (sub + 1) * 128], rhs=win1[:, kc, 1024:1536], start=(kc == 0), stop=False),
                                 reads=['win1', 'hext%d' % hb], writes=['psT%d' % tb])
                        for kc in range(KC):
                            S.op('pe', lambda e, kc=kc, tb=tb, scol=scol: e.matmul(out=psT[tb][:], lhsT=hs[:, kc, scol], rhs=win2[:, kc, 1024:1536], start=False, stop=(kc == KC - 1)),
                                 reads=['win2', 'hs'], writes=['psT%d' % tb])
                        vb = tcnt[0] % 2
                        S.op('act', lambda e, tb=tb, vb=vb: e.activation(out=vst[vb][:], in_=psT[tb][:], func=AF.Copy), reads=['psT%d' % tb], writes=['vst%d' % vb])
                        S.dma(V_TM[rows, :], vst[vb][:], reads=['vst%d' % vb], writes=['v_scr'], queue='pool')
                        cb = sub % 2
                        S.dma(cs[cb][:, 0, :], cos_in[T][rows, :], writes=['cs%d' % cb])
                        S.dma(cs[cb][:, 1, :], sin_in[T][rows, :], writes=['cs%d' % cb])
                        tb = tcnt[0] % 2
                        tcnt[0] += 1
                        for kc in range(KC):
                            S.op('pe', lambda e, kc=kc, tb=tb, sub=sub, hb=hb: e.matmul(out=psT[tb][:], lhsT=hext[hb][:, kc, 1 + sub * 128:1 + (sub + 1) * 128], rhs=win1[:, kc, 1920:2432], start=(kc == 0), stop=(kc == KC - 1)),
                                 reads=['win1', 'hext%d' % hb], writes=['psT%d' % tb])
                        qr, qrk = qrb[sub % 2], 'qr%d' % (sub % 2)
                        kr, krk = krb[sub % 2], 'kr%d' % (sub % 2)
                        norm_rope(psT[tb][:], 'psT%d' % tb, 8, gq, 'gq', cb, qr, qrk)
                        tb = tcnt[0] % 2
                        tcnt[0] += 1
                        for kc in range(KC):
                            S.op('pe', lambda e, kc=kc, tb=tb, sub=sub, hb=hb: e.matmul(out=psT[tb][:, 0:256], lhsT=hext[hb][:, kc, 1 + sub * 128:1 + (sub + 1) * 128], rhs=win1[:, kc, 2432:2688], start=(kc == 0), stop=(kc == KC - 1)),
                                 reads=['win1', 'hext%d' % hb], writes=['psT%d' % tb])
                        S.op('act', lambda e, tb=tb: e.activation(out=vas[:], in_=psT[tb][:, 128:256], func=AF.Copy), reads=['psT%d' % tb], writes=['vas'])
                        S.dma(VA[rows, :], vas[:], reads=['vas'], writes=['va_scr'], queue='pool')
                        norm_rope(psT[tb][:, 0:128], 'psT%d' % tb, 2, gk, 'gk', cb, kr, krk)

                        def part2(qr=qr, qrk=qrk, kr=kr, krk=krk, rows=rows):
                            for hp in range(4):
                                S.op('pe', lambda e, hp=hp: e.transpose(out=pstr[:, hp, :], in_=qr[:, 2 * hp:2 * hp + 2, :].rearrange("p a b -> p (a b)"), identity=ident[:]),
                                     reads=[qrk, 'ident'], writes=['pstr'])
                            S.op('dve', lambda e: e.tensor_copy(out=qTs[:], in_=pstr[:, 0:4, :]), reads=['pstr'], writes=['qTs'])
                            S.dma(QT[:, :, rows].rearrange("h p t -> p h t"), qTs[:], reads=['qTs'], writes=['q_scr'], queue='pool')
                            S.op('pe', lambda e: e.transpose(out=pstr[:, 4, :], in_=kr[:].rearrange("p a b -> p (a b)"), identity=ident[:]),
                                 reads=[krk, 'ident'], writes=['pstr'])
                            S.op('dve', lambda e: e.tensor_copy(out=kTs[:], in_=pstr[:, 4, :]), reads=['pstr'], writes=['kTs'])
                            S.dma(KTA[:, rows], kTs[:], reads=['kTs'], writes=['k_scr'], queue='pool')
                        if pending:
                            pending.pop(0)()
                        pending.append(part2)
                while pending:
                    pending.pop(0)()
                S.flush()
            if debug and 'RT' in dbg:
                with contextlib.ExitStack() as st:
                    def dump(name, src, shape, dt):
                        t = st.enter_context(SBT("dbgt_" + name, shape, dt))
                        S.dma(t[:], src, writes=[name])
                        S.dma(dbg[name], t[:], reads=[name], writes=[name + 'o'])
                    dump('RT', R_T[0][:, 0:T], [128, T], F32)
                    dump('KT', K_T[1][:, 0:T], [128, T], F32)
                    dump('LW', LW[1, 2][:, 0:T], [128, T], F32)
                    dump('AA', AA[0, 3][:, 0:T], [128, T], F32)
                    dump('V', V_TM[0:128, :], [128, RW], BF16)
                    dump('G', G_TM[128:256, :], [128, RW], F32)
                    dump('QT', QT[1][:, 0:T], [128, T], BF16)
                    dump('KTA', KTA[:, 0:T], [128, T], BF16)
                    dump('VA', VA[0:128, :], [128, 128], BF16)
                    S.flush()
                continue

            NS = min(T, NS_CAP_B)
            NSC = T // NS
            NCH = NS // CH
            with contextlib.ExitStack() as st:
                def T_(name, shape, dt):
                    return st.enter_context(SBT(name, list(shape), dt))
                pb_ = [st.enter_context(PST("R_pb%d" % i, [128, 512], F32)) for i in range(8)]
                pk = ['pb%d' % i for i in range(8)]
                cstg = T_("R_cstg", [128, 5, 128], F32)
                S.dma(cstg[:, 0, :], cin['maskf'], writes=['cstg'])
                S.dma(cstg[:, 1, :], cin['maskb'], writes=['cstg'])
                S.dma(cstg[:, 2, :], cin['blockones'], writes=['cstg'])
                S.dma(cstg[:, 3, 0:64], cin['masknt'][0], writes=['cstg'])
                S.dma(cstg[:, 3, 64:128], cin['masknt'][1], writes=['cstg'])
                S.dma(cstg[:, 4, 0:2], cin['headsel'], writes=['cstg'])
                bones = T_("R_bones", [128, 128], BF16)
                hsel = T_("R_hsel", [128, 2], BF16)
                hself = T_("R_hself", [128, 2], F32)
                S.op('dve', lambda e: e.tensor_copy(out=bones[:], in_=cstg[:, 2, :]), reads=['cstg'], writes=['bones'])
                S.op('dve', lambda e: e.tensor_copy(out=hsel[:], in_=cstg[:, 4, 0:2]), reads=['cstg'], writes=['hsel'])
                S.op('dve', lambda e: e.tensor_copy(out=hself[:], in_=cstg[:, 4, 0:2]), reads=['cstg'], writes=['hself'])
                pvec = T_("R_pvec", [128, 4, 4], F32)
                for i, v in enumerate((k_k, k_a, k_a, r_k)):
                    S.dma(pvec[:, i, :], v.rearrange("(h p) -> p h", p=128), writes=['pvec'], allow_slow_non_contiguous=True)
                S.op('dve', lambda e: e.tensor_scalar(out=pvec[:, 2, :], in0=pvec[:, 2, :], scalar1=-1.0, scalar2=1.0, op0=ALU.mult, op1=ALU.add), reads=['pvec'], writes=['pvec'])
                rmask = T_("R_rmask", [128, NCH, CH], F32)
                S.op('pool', lambda e: e.memset(rmask[:], 1.0), writes=['rmask'])
                S.op('pool', lambda e: e.memset(rmask[:, :, 0:1], 0.0), reads=[], writes=['rmask'])
                ft = [T_("R_ft%d" % i, [128, NS], F32) for i in range(10)]
                fk = ['ft%d' % i for i in range(10)]
                sqb = T_("R_sqb", [128, NS], BF16)
                prodT = T_("R_prodT", [128, NS], BF16)
                QTt = [[T_("R_QTt%d_%d" % (z, d), [128, NCH, 2, CH], BF16) for d in range(2)] for z in range(2)]
                KTt = [[T_("R_KTt%d_%d" % (z, d), [128, NCH, 2, CH], BF16) for d in range(2)] for z in range(2)]
                Vm = [[T_("R_Vm%d_%d" % (z, d), [128, NCH, 2, CH], BF16) for d in range(2)] for z in range(2)]
                ATt = [[T_("R_AT%d_%d" % (z, d), [128, NCH, 2, 128], BF16) for d in range(2)] for z in range(2)]
                Km = [[T_("R_Km%d_%d" % (z, d), [128, NCH, 2, CH], BF16) for d in range(2)] for z in range(2)]
                KTm = [[T_("R_KTm%d_%d" % (z, d), [128, NCH, 2, 2, CH], BF16) for d in range(2)] for z in range(2)]
                Ttt = [[T_("R_Tt%d_%d" % (z, d), [64, NCH, 2, CH], BF16) for d in range(2)] for z in range(2)]
                Yt = [[T_("R_Y%d_%d" % (z, d), [64, NCH, 2, CH], F32) for d in range(2)] for z in range(2)]
                gam = [[T_("R_gam%d_%d" % (z, d), [128, NCH], F32) for d in range(2)] for z in range(2)]
                Xb = [[T_("R_Xb%d_%d" % (u, i), [64, 8, CH], BF16) for i in range(2)] for u in range(2)]
                XTb = [[T_("R_XTb%d_%d" % (u, i), [64, 8, CH], BF16) for i in range(2)] for u in range(2)]
                Ttf = [T_("R_Ttf%d" % u, [64, 8, CH], F32) for u in range(2)]
                Ttb = [T_("R_Ttb%d" % u, [64, 8, CH], BF16) for u in range(2)]
                Hf = T_("R_Hf", [128, 2, CH], F32)
                Hg = T_("R_Hg", [128, 2, CH], F32)
                Hb = T_("R_Hb", [128, 4, CH], BF16)
                Xs = T_("R_Xs", [64, 4, CH], BF16)
                coefT = T_("R_coefT", [128, T // 128, 16], F32)
                S.flush()
                INV_BANKS = ((5, 6, 0), (2, 3, 4))

                def precompute(hp, d, sc, z):
                    tok0 = sc * NS
                    cols = slice(tok0, tok0 + NS)
                    D_ = str(d) + '_' + str(z)
                    kQ, kK, kKm_, kAT, kKmm, kTt, kVV, kVU, kG = 'QTt' + D_, 'KTt' + D_, 'KTm' + D_, 'AT' + D_, 'Km' + D_, 'Tt' + D_, 'VmV' + D_, 'VmU' + D_, 'gam' + D_
                    r_s, k_s, lw_s, a_s = ft[0], ft[1], ft[2], ft[3]
                    S.dma(r_s[:], R_T[hp][:, cols], writes=[fk[0]])
                    S.dma(k_s[:], K_T[hp][:, cols], writes=[fk[1]])
                    S.dma(lw_s[:], LW[d, hp][:, cols], writes=[fk[2]])
                    S.dma(a_s[:], AA[d, hp][:, cols], writes=[fk[3]])
                    for hh in range(2):
                        S.dma(Vm[z][d][64:128, :, hh, :], V_TM[tok0:tok0 + NS, (hp * 2 + hh) * 64:(hp * 2 + hh + 1) * 64].rearrange("(c p) j -> p c j", p=64), writes=[kVV])
                    S.op('pool', lambda e: e.memset(Vm[z][d][0:64, :, :, :], 0.0), writes=[kVU])
                    yield
                    S.op('dve', lambda e: e.tensor_scalar(out=ft[4][:], in0=k_s[:], scalar1=pvec[:, 0, hp:hp + 1], scalar2=None, op0=ALU.mult), reads=[fk[1], 'pvec'], writes=[fk[4]])
                    S.op('act', lambda e: e.activation(out=sqb[:], in_=ft[4][:], func=AF.Square), reads=[fk[4]], writes=['sqb'])
                    for q in range(NS // 512):
                        qs = slice(q * 512, (q + 1) * 512)
                        S.op('pe', lambda e, q=q, qs=qs: e.matmul(out=pb_[0][:], lhsT=bones[:], rhs=sqb[:, qs], start=True, stop=True), reads=['bones', 'sqb'], writes=[pk[0]])
                        S.op('dve', lambda e, q=q, qs=qs: e.tensor_scalar(out=ft[5][:, qs], in0=pb_[0][:], scalar1=1e-24, scalar2=None, op0=ALU.max), reads=[pk[0]], writes=[fk[5]])
                    S.op('act', lambda e: e.activation(out=ft[5][:], in_=ft[5][:], func=AF.Ln), reads=[fk[5]], writes=[fk[5]])
                    S.op('act', lambda e: e.activation(out=ft[5][:], in_=ft[5][:], func=AF.Exp, scale=-0.5), reads=[fk[5]], writes=[fk[5]])
                    S.op('dve', lambda e: e.tensor_tensor(out=ft[4][:], in0=ft[4][:], in1=ft[5][:], op=ALU.mult), reads=[fk[4], fk[5]], writes=[fk[4]])
                    yield
                    S.op('dve', lambda e: e.tensor_scalar(out=ft[6][:], in0=a_s[:], scalar1=pvec[:, 1, hp:hp + 1], scalar2=pvec[:, 2, hp:hp + 1], op0=ALU.mult, op1=ALU.add), reads=[fk[3], 'pvec'], writes=[fk[6]])
                    S.op('pool', lambda e: e.tensor_tensor(out=ft[6][:], in0=ft[6][:], in1=k_s[:], op=ALU.mult), reads=[fk[6], fk[1]], writes=[fk[6]])
                    S.op('pool', lambda e: e.tensor_tensor(out=ft[7][:], in0=ft[4][:], in1=a_s[:], op=ALU.mult), reads=[fk[4], fk[3]], writes=[fk[7]])
                    yield
                    S.op('dve', lambda e: e.tensor_tensor_scan(out=ft[8][:], data0=rmask[:].rearrange("p c j -> p (c j)"), data1=lw_s[:], initial=0.0, op0=ALU.mult, op1=ALU.add), reads=['rmask', fk[2]], writes=[fk[8]])
                    L3 = ft[8][:].rearrange("p (c j) -> p c j", j=CH)
                    if d == 1:
                        S.op('dve', lambda e: e.tensor_tensor(out=ft[9][:], in0=lw_s[:], in1=ft[8][:], op=ALU.subtract), reads=[fk[2], fk[8]], writes=[fk[9]])
                        S.op('dve', lambda e: e.tensor_tensor(out=ft[9][:].rearrange("p (c j) -> p c j", j=CH), in0=ft[9][:].rearrange("p (c j) -> p c j", j=CH), in1=L3[:, :, CH - 1:CH].broadcast_to([128, NCH, CH]), op=ALU.add), reads=[fk[9], fk[8]], writes=[fk[9]])
                        Lt, Lk, Et, Ek = ft[9], fk[9], ft[8], fk[8]
                    else:
                        Lt, Lk, Et, Ek = ft[8], fk[8], ft[9], fk[9]
                    Et3 = Et[:].rearrange("p (c j) -> p c j", j=CH)
                    S.op('act', lambda e: e.activation(out=Et[:], in_=Lt[:], func=AF.Exp), reads=[Lk], writes=[Ek])
                    gi = CH - 1 if d == 0 else 0
                    S.op('dve', lambda e: e.tensor_copy(out=gam[z][d][:], in_=Et3[:, :, gi]), reads=[Ek], writes=[kG])
                    S.op('dve', lambda e: e.tensor_tensor(out=QTt[z][d][:, :, 1, :], in0=r_s[:].rearrange("p (c j) -> p c j", j=CH), in1=Et3, op=ALU.mult), reads=[fk[0], Ek], writes=[kQ])
                    S.op('dve', lambda e: e.tensor_tensor(out=ft[5][:], in0=r_s[:], in1=ft[6][:], op=ALU.mult), reads=[fk[0], fk[6]], writes=[fk[5]])
                    S.op('act', lambda e: e.activation(out=prodT[:], in_=ft[5][:], func=AF.Copy, scale=pvec[:, 3, hp:hp + 1]), reads=[fk[5], 'pvec'], writes=['prodT'])
                    yield
                    S.op('act', lambda e: e.activation(out=Et[:], in_=Lt[:], func=AF.Exp, scale=-1.0), reads=[Lk, kQ, kG], writes=[Ek])
                    S.op('dve', lambda e: e.tensor_tensor(out=KTt[z][d][:, :, 0, :], in0=ft[7][:].rearrange("p (c j) -> p c j", j=CH), in1=Et3, op=ALU.mult), reads=[fk[7], Ek], writes=[kK])
                    S.op('dve', lambda e: e.tensor_tensor(out=KTt[z][d][:, :, 1, :], in0=ft[6][:].rearrange("p (c j) -> p c j", j=CH), in1=Et3, op=ALU.mult), reads=[fk[6], Ek], writes=[kK])
                    yield
                    for hh in range(2):
                        S.op('act', lambda e, hh=hh: e.activation(out=KTm[z][d][:, :, hh, :, :].rearrange("p c a b -> p c (a b)"), in_=KTt[z][d][:].rearrange("p c a b -> p c (a b)"), func=AF.Copy, scale=hself[:, hh:hh + 1]), reads=[kK, 'hself'], writes=[kKm_])
                    S.op('dve', lambda e: e.tensor_tensor(out=Et[:], in0=Lt[:], in1=lw_s[:], op=ALU.subtract), reads=[Lk, fk[2], kK], writes=[Ek])
                    S.op('act', lambda e: e.activation(out=Et[:], in_=Et[:], func=AF.Exp), reads=[Ek], writes=[Ek])
                    S.op('act', lambda e: e.activation(out=ft[7][:], in_=ft[4][:], func=AF.Copy, scale=-1.0), reads=[fk[4], kK], writes=[fk[7]])
                    S.op('dve', lambda e: e.tensor_tensor(out=QTt[z][d][:, :, 0, :], in0=ft[7][:].rearrange("p (c j) -> p c j", j=CH), in1=Et3, op=ALU.mult), reads=[fk[7], Ek], writes=[kQ])
                    yield
                    nb = NS // 128
                    cv = pb_[0][:, 0:nb * 2].rearrange("p (q h) -> p q h", h=2)
                    for q in range(nb):
                        S.op('pe', lambda e, q=q: e.matmul(out=cv[:, q, :], lhsT=prodT[:, q * 128:(q + 1) * 128], rhs=hsel[:], start=True, stop=True), reads=['prodT', 'hsel'], writes=[pk[0]])
                    S.op('act', lambda e: e.activation(out=coefT[:, tok0 // 128:tok0 // 128 + nb, d * 8 + hp * 2:d * 8 + hp * 2 + 2], in_=cv, func=AF.Copy), reads=[pk[0]], writes=['coefT'])
                    yield
                    mk = cstg[:, d, :]
                    combos = [(c, hh) for c in range(NCH) for hh in range(2)]
                    ATf = ATt[z][d][:].rearrange("p c h s -> p (c h) s")
                    Ttf_all = Ttt[z][d][:].rearrange("p c h s -> p (c h) s")
                    for g4 in range(len(combos) // 4):
                        bk = 2 + (g4 % 2)
                        av = pb_[bk][:].rearrange("p (j s) -> p j s", s=128)
                        for j in range(4):
                            c, hh = combos[g4 * 4 + j]
                            S.op('pe', lambda e, av=av, j=j, c=c, hh=hh: e.matmul(out=av[:, j, :], lhsT=KTm[z][d][:, c, hh, :, :].rearrange("p a b -> p (a b)"), rhs=QTt[z][d][:, c, :, :].rearrange("p a b -> p (a b)"), start=True, stop=True),
                                 reads=[kKm_, kQ], writes=[pk[bk]])
                        S.op('dve', lambda e, av=av, g4=g4: e.tensor_tensor(out=ATf[:, g4 * 4:g4 * 4 + 4, :], in0=av, in1=mk.unsqueeze(1).broadcast_to([128, 4, 128]), op=ALU.mult),
                             reads=[pk[bk], 'cstg'], writes=[kAT])
                        yield
                    for g8 in range(NCH // 8):
                        kv = pb_[4][:].bitcast(BF16)[:, 0:1024].rearrange("p (j s) -> p j s", s=128)
                        for j in range(8):
                            c = g8 * 8 + j
                            S.op('pe', lambda e, kv=kv, j=j, c=c: e.transpose(out=kv[:, j, :], in_=KTt[z][d][:, c, :, :].rearrange("p a b -> p (a b)"), identity=ident[:]),
                                 reads=[kK, 'ident'], writes=[pk[4]])
                        S.op('act', lambda e, kv=kv, g8=g8: e.activation(out=Km[z][d][:, g8 * 8:g8 * 8 + 8, :, :].rearrange("p c h s -> p c (h s)"), in_=kv, func=AF.Copy), reads=[pk[4]], writes=[kKmm])
                        yield
                    mnt = cstg[0:64, 3, d * 64:(d + 1) * 64]
                    ngrp = len(combos) // 8
                    for gp in range(0, ngrp, 2):
                        units = [u for u in range(2) if gp + u < ngrp]
                        stt_ = {}
                        for u in units:
                            g8 = gp + u
                            b5, b6, b0 = INV_BANKS[u]
                            v5 = pb_[b5][0:64, :].rearrange("p (j s) -> p j s", s=64)
                            v6 = pb_[b6][0:64, :].rearrange("p (j s) -> p j s", s=64)
                            v0 = pb_[b0][0:64, :].rearrange("p (j s) -> p j s", s=64)
                            gs = slice(g8 * 8, g8 * 8 + 8)
                            X0 = ATf[0:64, gs, 0:64]
                            U_ = str(u)
                            for j in range(8):
                                c, hh = combos[g8 * 8 + j]
                                S.op('pe', lambda e, j=j, c=c, hh=hh, v5=v5: e.matmul(out=v5[:, j, :], lhsT=QTt[z][d][:, c, 0, :], rhs=KTm[z][d][:, c, hh, 0, :], start=True, stop=True),
                                     reads=[kQ, kKm_], writes=[pk[b5]])
                            S.op('dve', lambda e, v5=v5, u=u: e.tensor_tensor(out=XTb[u][0][:], in0=v5, in1=mnt.unsqueeze(1).broadcast_to([64, 8, 64]), op=ALU.mult), reads=[pk[b5], 'cstg'], writes=['XTb' + U_ + '0'])
                            S.op('dve', lambda e, X0=X0, u=u: e.tensor_tensor(out=Ttf[u][:], in0=X0, in1=ident_f[0:64, 0:64].unsqueeze(1).broadcast_to([64, 8, 64]), op=ALU.add), reads=[kAT, 'ident_f'], writes=['Ttf' + U_])
                            S.op('act', lambda e, u=u: e.activation(out=Ttb[u][:], in_=Ttf[u][:], func=AF.Copy), reads=['Ttf' + U_], writes=['Ttb' + U_])
                            stt_[u] = dict(Xc=X0, Xck=kAT, XTc=XTb[u][0], XTck='XTb' + U_ + '0', v5=v5, v6=v6, v0=v0, b5=b5, b6=b6, b0=b0, gs=gs)
                            yield
                        for it in range(1, 6):
                            nx = it % 2
                            for u in units:
                                s_ = stt_[u]
                                U_ = str(u)
                                if it < 5:
                                    for j in range(8):
                                        S.op('pe', lambda e, j=j, s_=s_, Xc=s_['Xc'], XTc=s_['XTc']: e.matmul(out=s_['v6'][:, j, :], lhsT=XTc[:, j, :], rhs=Xc[:, j, :], start=True, stop=True), reads=[s_['Xck'], s_['XTck']], writes=[pk[s_['b6']]])
                                for j in range(8):
                                    S.op('pe', lambda e, j=j, s_=s_, Xc=s_['Xc'], XTc=s_['XTc']: e.matmul(out=s_['v5'][:, j, :], lhsT=Xc[:, j, :], rhs=XTc[:, j, :], start=True, stop=True), reads=[s_['Xck'], s_['XTck']], writes=[pk[s_['b5']]])
                            yield
                            for u in units:
                                s_ = stt_[u]
                                U_ = str(u)
                                if it < 5:
                                    S.op('act', lambda e, nx=nx, s_=s_, u=u: e.activation(out=Xb[u][nx][:], in_=s_['v6'], func=AF.Copy), reads=[pk[s_['b6']]], writes=['Xb%s%d' % (U_, nx)])
                                S.op('dve', lambda e, nx=nx, s_=s_, u=u: e.tensor_copy(out=XTb[u][nx][:], in_=s_['v5']), reads=[pk[s_['b5']]], writes=['XTb%s%d' % (U_, nx)])
                                s_['Xc'], s_['Xck'] = Xb[u][nx], 'Xb%s%d' % (U_, nx)
                                s_['XTc'], s_['XTck'] = XTb[u][nx], 'XTb%s%d' % (U_, nx)
                            yield
                            for u in units:
                                s_ = stt_[u]
                                U_ = str(u)
                                for j in range(8):
                                    S.op('pe', lambda e, j=j, s_=s_, XTc=s_['XTc'], u=u: e.matmul(out=s_['v0'][:, j, :], lhsT=XTc[:, j, :], rhs=Ttb[u][:, j, :], start=True, stop=True), reads=[s_['XTck'], 'Ttb' + U_], writes=[pk[s_['b0']]])
                            yield
                            for u in units:
                                s_ = stt_[u]
                                U_ = str(u)
                                S.op('dve', lambda e, s_=s_, u=u: e.tensor_tensor(out=Ttf[u][:], in0=s_['v0'], in1=Ttf[u][:], op=ALU.add), reads=[pk[s_['b0']], 'Ttf' + U_], writes=['Ttf' + U_])
                                if it < 5:
                                    S.op('act', lambda e, u=u: e.activation(out=Ttb[u][:], in_=Ttf[u][:], func=AF.Copy), reads=['Ttf' + U_], writes=['Ttb' + U_])
                                else:
                                    S.op('act', lambda e, s_=s_, u=u: e.activation(out=Ttf_all[:, s_['gs'], :], in_=Ttf[u][:], func=AF.Copy), reads=['Ttf' + U_], writes=[kTt])
                            yield

                def chain_step(i, z):
                    vx = pb_[1][0:64, 0:256].rearrange("p (h s) -> p h s", s=64)
                    vy = pb_[1][0:64, 256:512].rearrange("p (h s) -> p h s", s=64)
                    vu = pb_[7][0:64, 0:256].rearrange("p (h s) -> p h s", s=64)
                    vh = pb_[7][:, 256:512].rearrange("p (h s) -> p h s", s=64)
                    cs_ = (i, NCH - 1 - i)
                    for d in range(2):
                        c = cs_[d]
                        D_ = str(d) + '_' + str(z)
                        S.op('dve', lambda e, c=c, d=d: e.tensor_scalar(out=Hg[:, d, :], in0=Hf[:, d, :], scalar1=gam[z][d][:, c:c + 1], scalar2=None, op0=ALU.mult), reads=['Hf', 'gam' + D_], writes=['Hg'])
                    for d in range(2):
                        c = cs_[d]
                        D_ = str(d) + '_' + str(z)
                        for hh in range(2):
                            k = d * 2 + hh
                            S.op('pe', lambda e, c=c, hh=hh, d=d, k=k: e.matmul(out=vx[:, k, :], lhsT=QTt[z][d][:, c, 0, :], rhs=Hb[:, k, :], start=True, stop=False), reads=['QTt' + D_, 'Hb'], writes=[pk[1]])
                            S.op('pe', lambda e, c=c, hh=hh, d=d, k=k: e.matmul(out=vx[:, k, :], lhsT=ATt[z][d][:, c, hh, 0:64], rhs=Vm[z][d][:, c, hh, :], start=False, stop=True), reads=['AT' + D_, 'VmV' + D_, 'VmU' + D_], writes=[pk[1]])
                    S.op('act', lambda e: e.activation(out=Xs[:], in_=vx, func=AF.Copy), reads=[pk[1]], writes=['Xs'])
                    yield
                    for d in range(2):
                        c = cs_[d]
                        D_ = str(d) + '_' + str(z)
                        for hh in range(2):
                            k = d * 2 + hh
                            S.op('pe', lambda e, c=c, hh=hh, d=d, k=k: e.matmul(out=vu[:, k, :], lhsT=Ttt[z][d][:, c, hh, :], rhs=Xs[:, k, :], start=True, stop=True), reads=['Tt' + D_, 'Xs'], writes=[pk[7]])
                    yield
                    for d in range(2):
                        c = cs_[d]
                        D_ = str(d) + '_' + str(z)
                        S.op('dve', lambda e, c=c, d=d: e.tensor_copy(out=Vm[z][d][0:64, c, :, :], in_=vu[:, d * 2:d * 2 + 2, :]), reads=[pk[7]], writes=['VmU' + D_])
                    yield
                    for d in range(2):
                        c = cs_[d]
                        D_ = str(d) + '_' + str(z)
                        for hh in range(2):
                            k = d * 2 + hh
                            S.op('pe', lambda e, c=c, hh=hh, d=d, k=k: e.matmul(out=vy[:, k, :], lhsT=QTt[z][d][:, c, 1, :], rhs=Hb[:, k, :], start=True, stop=False), reads=['QTt' + D_, 'Hb'], writes=[pk[1]])
                            S.op('pe', lambda e, c=c, hh=hh, d=d, k=k: e.matmul(out=vy[:, k, :], lhsT=ATt[z][d][:, c, hh, 64:128], rhs=Vm[z][d][:, c, hh, :], start=False, stop=True), reads=['AT' + D_, 'VmV' + D_, 'VmU' + D_], writes=[pk[1]])
                    for d in range(2):
                        c = cs_[d]
                        D_ = str(d) + '_' + str(z)
                        for hh in range(2):
                            k = d * 2 + hh
                            S.op('pe', lambda e, c=c, hh=hh, d=d, k=k: e.matmul(out=vh[:, k, :], lhsT=Km[z][d][:, c, :, :].rearrange("p h s -> p (h s)"), rhs=Vm[z][d][:, c, hh, :], start=True, stop=True), reads=['Km' + D_, 'VmV' + D_, 'VmU' + D_], writes=[pk[7]])
                    yield
                    for d in range(2):
                        c = cs_[d]
                        S.op('act', lambda e, c=c, d=d: e.activation(out=Yt[z][d][:, c, :, :], in_=vy[:, d * 2:d * 2 + 2, :], func=AF.Copy), reads=[pk[1]], writes=['Y' + str(d) + '_' + str(z)])
                    for d in range(2):
                        c = cs_[d]
                        D_ = str(d) + '_' + str(z)
                        for hh in range(2):
                            k = d * 2 + hh
                            p0 = hh * 64
                            S.op('dve', lambda e, c=c, d=d, k=k, p0=p0: e.scalar_tensor_tensor(out=Hb[p0:p0 + 64, k, :], in0=vh[p0:p0 + 64, k, :], scalar=gam[z][d][p0:p0 + 64, c:c + 1], in1=Hg[p0:p0 + 64, d, :], op0=ALU.mult, op1=ALU.add), reads=[pk[7], 'gam' + D_, 'Hg'], writes=['Hb'])
                    for d in range(2):
                        c = cs_[d]
                        D_ = str(d) + '_' + str(z)
                        for hh in range(2):
                            k = d * 2 + hh
                            p0 = hh * 64
                            S.op('dve', lambda e, c=c, d=d, k=k, p0=p0: e.scalar_tensor_tensor(out=Hf[p0:p0 + 64, d, :], in0=vh[p0:p0 + 64, k, :], scalar=gam[z][d][p0:p0 + 64, c:c + 1], in1=Hg[p0:p0 + 64, d, :], op0=ALU.mult, op1=ALU.add), reads=[pk[7], 'gam' + D_, 'Hg'], writes=['Hf'])
                    yield

                tasks = [(hp, s_i) for hp in range(4) for s_i in range(NSC)]

                def pre_task(k):
                    hp, s_i = tasks[k]
                    scs = (s_i, NSC - 1 - s_i)
                    for d in range(2):
                        yield from precompute(hp, d, scs[d], k % 2)

                def chain_task(k):
                    hp, s_i = tasks[k]
                    z = k % 2
                    scs = (s_i, NSC - 1 - s_i)
                    if s_i == 0:
                        S.op('pool', lambda e: e.memset(Hf[:], 0.0), writes=['Hf'])
                        S.op('pool', lambda e: e.memset(Hb[:], 0.0), writes=['Hb'])
                    for i in range(NCH):
                        yield from chain_step(i, z)
                    for d in range(2):
                        tok0 = scs[d] * NS
                        S.dma(YD[d][tok0:tok0 + NS, hp * 128:(hp + 1) * 128].rearrange("(c p) j -> p c j", p=64), Yt[z][d][:].rearrange("p c h s -> p c (h s)"), reads=['Y' + str(d) + '_' + str(z)], writes=['YD'], queue='pool')

                for _ in pre_task(0):
                    pass
                for k in range(len(tasks)):
                    a = chain_task(k)
                    b = pre_task(k + 1) if k + 1 < len(tasks) else iter(())
                    a_alive = b_alive = True
                    while a_alive or b_alive:
                        if a_alive:
                            try:
                                next(a)
                            except StopIteration:
                                a_alive = False
                        if b_alive:
                            try:
                                next(b)
                            except StopIteration:
                                b_alive = False
                    if k % 16 == 15:
                        S.flush()
                S.dma(COEFS16[0:T, :].rearrange("(q p) h -> p q h", p=128), coefT[:], reads=['coefT'], writes=['COEFS'], queue='pool')
                S.flush()
            if debug and 'YD0' in dbg:
                with contextlib.ExitStack() as st:
                    def dump(name, src, shape, dt):
                        t = st.enter_context(SBT("dbgt_" + name, shape, dt))
                        S.dma(t[:], src, writes=[name])
                        S.dma(dbg[name], t[:], reads=[name], writes=[name + 'o'])
                    dump('YD0', YD[0][0:T, :].rearrange("(s p) c -> p s c", p=128), [128, T // 128, RW], F32)
                    dump('YD1', YD[1][0:T, :].rearrange("(s p) c -> p s c", p=128), [128, T // 128, RW], F32)
                    dump('CF', COEFS16[0:T, :].rearrange("(s p) c -> p s c", p=128), [128, T // 128, 16], F32)
                    S.flush()
                continue

            NKT = T // 128
            with contextlib.ExitStack() as st:
                def T_(name, shape, dt):
                    return st.enter_context(SBT(name, list(shape), dt))

                def PS(name, shape, dt):
                    return st.enter_context(PST(name, list(shape), dt))
                Kt = T_("C_Kt", [64, 2, T], BF16)
                Va = T_("C_Va", [128, NKT, 2, 65], BF16)
                Qg = [T_("C_Qg%d" % i, [64, 4, 128], BF16) for i in range(2)]
                Pt = [T_("C_Pt%d" % i, [128, 512], BF16) for i in range(2)]
                ob = [T_("C_ob%d" % i, [128, 4, 64], BF16) for i in range(2)]
                rec = T_("C_rec", [128, 4], F32)
                psS = [PS("C_psS%d" % i, [128, 512], F32) for i in range(2)]
                psO = [PS("C_psO%d" % i, [128, 512], F32) for i in range(4)]
                if QUART:
                    for hp in range(4):
                        S.dma(QTO[hp][:, 0:NQ], (lambda eng, hp=hp: QT[hp][:, bass.ds(qoff(eng), NQ)]), writes=['QTO'])
                        if hp % 2 == 1:
                            S.flush()
                    QTv, YBv = QTO, YBO
                else:
                    QTv, YBv = QT, YB
                for kv in range(2):
                    S.dma(Kt[:, kv, :], KTA[kv * 64:(kv + 1) * 64, 0:T], writes=['Kt'])
                S.op('pool', lambda e: e.memset(Va[:, :, :, 64:65], 1.0), writes=['Va1'])
                for kv in range(2):
                    S.dma(Va[:, :, kv, 0:64], VA[0:T, kv * 64:(kv + 1) * 64].rearrange("(k p) j -> p k j", p=128), writes=['Va'])
                it = 0
                for kv in range(2):
                    for qt in range(NQ // 128):
                        qb = it % 2
                        it += 1
                        qcols = slice(qt * 128, (qt + 1) * 128)
                        for g in range(4):
                            h = kv * 4 + g
                            S.dma(Qg[qb][:, g, :], QTv[h // 2][(h % 2) * 64:(h % 2 + 1) * 64, qt * 128:(qt + 1) * 128], reads=['QTO'], writes=['Qg%d' % qb])
                        def qk(kt):
                            sb = kt % 2
                            S.op('pe', lambda e, sb=sb, kt=kt, kv=kv, qb=qb: e.matmul(out=psS[sb][:], lhsT=Kt[:, kv, kt * 128:(kt + 1) * 128], rhs=Qg[qb][:].rearrange("p g q -> p (g q)"), start=True, stop=True),
                                 reads=['Kt', 'Qg%d' % qb], writes=['psS%d' % sb])
                        qk(0)
                        for kt in range(NKT):
                            sb = kt % 2
                            if kt + 1 < NKT:
                                qk(kt + 1)
                            S.op('act', lambda e, sb=sb: e.activation(out=Pt[sb][:], in_=psS[sb][:], func=AF.Exp, scale=0.125), reads=['psS%d' % sb], writes=['Pt%d' % sb])
                            for g in range(4):
                                S.op('pe', lambda e, sb=sb, g=g, kt=kt, kv=kv: e.matmul(out=psO[g][:, 0:65], lhsT=Pt[sb][:, g * 128:(g + 1) * 128], rhs=Va[:, kt, kv, :], start=(kt == 0), stop=(kt == NKT - 1)),
                                     reads=['Pt%d' % sb, 'Va', 'Va1'], writes=['psO%d' % g])
                        for g in range(4):
                            S.op('dve', lambda e, g=g: e.reciprocal(out=rec[:, g:g + 1], in_=psO[g][:, 64:65]), reads=['psO%d' % g], writes=['rec%d' % g])
                            S.op('act', lambda e, g=g, qb=qb: e.activation(out=ob[qb][:, g, :], in_=psO[g][:, 0:64], func=AF.Copy, scale=rec[:, g:g + 1]), reads=['psO%d' % g, 'rec%d' % g], writes=['ob%d' % qb])
                        S.dma(YBv[qt * 128:(qt + 1) * 128, kv * 256:(kv + 1) * 256], ob[qb][:].rearrange("p g j -> p (g j)"), reads=['ob%d' % qb], writes=['YB'], queue='pool')
                    S.flush()

            with contextlib.ExitStack() as st:
                def T_(name, shape, dt):
                    return st.enter_context(SBT(name, list(shape), dt))
                P = alloc_ffn(st, "D_")
                xt = T_("D_xt", [128, 4, D], F32)
                h2T = T_("D_h2T", [128, KC, 512], BF16)
                wba = T_("D_wba", [128, 4, D], BF16)
                wbb = T_("D_wbb", [128, 4, D], BF16)
                wout = T_("D_wout", [128, KC, D], BF16)
                S.dma(wba[:], WBA.rearrange("f p c -> p f c"), writes=['wba'])
                S.dma(wbb[:], WBB.rearrange("f p c -> p f c"), writes=['wbb'])
                S.dma(wout[:], WOUT.rearrange("f p c -> p f c"), writes=['wout'])
                lnw = T_("D_lnw", [128, 2, RW], F32)
                S.dma(lnw[:, 0, :], lnx_w.partition_broadcast(128), writes=['lnw'])
                S.dma(lnw[:, 1, :], lnx_b.partition_broadcast(128), writes=['lnw'])
                bgT = T_("D_bgT", [128, 16], F32)
                S.dma(bgT[:], b_gate.rearrange("(o p) -> p o", p=128), writes=['bgT'], allow_slow_non_contiguous=True)
                yd = [T_("D_yd%d" % i, [128, 8, 64], F32) for i in range(2)]
                gt = T_("D_gt", [128, 8, 64], F32)
                vt = T_("D_vt", [128, 8, 64], BF16)
                ybt = T_("D_ybt", [128, RW], BF16)
                cf = T_("D_cf", [128, 16], F32)
                st8 = T_("D_st8", [128, 6, 8], F32)
                yab = T_("D_yab", [128, RW], BF16)
                sga = [T_("D_sga%d" % i, [128, 512], F32) for i in range(2)]
                mtmp = [T_("D_mtmp%d" % i, [128, 512], F32) for i in range(2)]
                tA = P['junk'][:, 0:512].rearrange("p (h j) -> p h j", j=64)
                tB = P['junk'][:, 512:1024].rearrange("p (h j) -> p h j", j=64)
                tC = P['tmpn'][:, 0:512].rearrange("p (h j) -> p h j", j=64)
                yaT = P['actT'][:, 8:12, :]
                ybT = P['actT'][:, 12:16, :]
                mT = P['actT'][:, 0:8, :]

                def bc8(ap):
                    return ap.unsqueeze(2).broadcast_to([128, 8, 64])
                if QUART:
                    jobs = [(X1O[0:NQ, :], lambda eng: X1[bass.ds(qoff(eng), NQ), :]),
                            (H2O[:, :, 0:NQ], lambda eng: H2[:, :, bass.ds(qoff(eng) + 1, NQ)]),
                            (YDO[0][0:NQ, :], lambda eng: YD[0][bass.ds(qoff(eng), NQ), :]),
                            (YDO[1][0:NQ, :], lambda eng: YD[1][bass.ds(qoff(eng), NQ), :]),
                            (GO[0:NQ, :], lambda eng: G_TM[bass.ds(qoff(eng), NQ), :]),
                            (VO[0:NQ, :], lambda eng: V_TM[bass.ds(qoff(eng), NQ), :]),
                            (CFO[0:NQ, :], lambda eng: COEFS16[bass.ds(qoff(eng), NQ), :])]
                    for ji, (dst, src) in enumerate(jobs):
                        S.dma(dst, src, writes=['slab'])
                        if ji % 2 == 1:
                            S.flush()
                    S.flush()
                    X1v, H2v, YDv, Gv, Vv, CFv, YBv = X1O, H2O, YDO, GO, VO, CFO, YBO
                else:
                    X1v, H2v, YDv, Gv, Vv, CFv, YBv = X1, H2[:, :, 1:T + 1], YD, G_TM, V_TM, COEFS16, YB
                for ti in range(NQ // 512):
                    S.dma(xt[:], X1v[ti * 512:(ti + 1) * 512, :].rearrange("(s p) d -> p s d", p=128), writes=['xt'])
                    S.dma(h2T[:], H2v[:, :, ti * 512:(ti + 1) * 512], writes=['h2T'])
                    for sub in range(4):
                        rows = slice(ti * 512 + sub * 128, ti * 512 + (sub + 1) * 128)
                        S.dma(yd[0][:].rearrange("p h j -> p (h j)"), YDv[0][rows, :], writes=['yd0'])
                        S.dma(yd[1][:].rearrange("p h j -> p (h j)"), YDv[1][rows, :], writes=['yd1'])
                        S.dma(gt[:].rearrange("p h j -> p (h j)"), Gv[rows, :], writes=['gt'])
                        S.dma(vt[:].rearrange("p h j -> p (h j)"), Vv[rows, :], writes=['vt'])
                        S.dma(ybt[:], YBv[rows, :], writes=['ybt'])
                        S.dma(cf[:], CFv[rows, :], writes=['cf'])
                        S.op('dve', lambda e: e.tensor_tensor(out=yd[0][:], in0=yd[0][:], in1=yd[1][:], op=ALU.add), reads=['yd0', 'yd1'], writes=['yd0'])
                        S.op('dve', lambda e: e.tensor_reduce(out=st8[:, 0, :], in_=yd[0][:], axis=AX.X, op=ALU.add), reads=['yd0'], writes=['st8_0'])
                        S.op('dve', lambda e: e.tensor_scalar(out=st8[:, 1, :], in0=st8[:, 0, :], scalar1=1.0 / 64, scalar2=None, op0=ALU.mult), reads=['st8_0'], writes=['st8_1'])
                        S.op('dve', lambda e: e.tensor_tensor(out=tA, in0=yd[0][:], in1=bc8(st8[:, 1, :]), op=ALU.subtract), reads=['yd0', 'st8_1'], writes=['junkA'])
                        S.op('act', lambda e: e.activation(out=tB, in_=tA, func=AF.Square), reads=['junkA'], writes=['junkB'])
                        S.op('dve', lambda e: e.tensor_reduce(out=st8[:, 2, :], in_=tB, axis=AX.X, op=ALU.add), reads=['junkB'], writes=['st8_2'])
                        S.op('dve', lambda e: e.tensor_scalar(out=st8[:, 3, :], in0=st8[:, 2, :], scalar1=1.0 / 64, scalar2=LNX_EPS, op0=ALU.mult, op1=ALU.add), reads=['st8_2'], writes=['st8_3'])
                        S.op('pool', lambda e: e.tensor_tensor(out=st8[:, 4, :], in0=st8[:, 3, :], in1=mhalf[:, 0:8], op=ALU.pow), reads=['st8_3', 'mhalf'], writes=['st8_4'])
                        S.op('dve', lambda e: e.tensor_tensor(out=tB, in0=tA, in1=bc8(st8[:, 4, :]), op=ALU.mult), reads=['junkA', 'st8_4'], writes=['junkB'])
                        S.op('dve', lambda e: e.tensor_tensor(out=tA, in0=tB, in1=lnw[:, 0, :].rearrange("p (h j) -> p h j", j=64), op=ALU.mult), reads=['junkB', 'lnw'], writes=['junkA'])
                        S.op('dve', lambda e: e.tensor_tensor(out=tA, in0=tA, in1=lnw[:, 1, :].rearrange("p (h j) -> p h j", j=64), op=ALU.add), reads=['junkA', 'lnw'], writes=['junkA'])
                        S.op('dve', lambda e: e.tensor_tensor(out=st8[:, 5, :], in0=cf[:, 0:8], in1=cf[:, 8:16], op=ALU.add), reads=['cf'], writes=['st8_5'])
                        S.op('dve', lambda e: e.tensor_tensor(out=tC, in0=vt[:], in1=bc8(st8[:, 5, :]), op=ALU.mult), reads=['vt', 'st8_5'], writes=['tmpnC'])
                        S.op('dve', lambda e: e.tensor_tensor(out=tA, in0=tA, in1=tC, op=ALU.add), reads=['junkA', 'tmpnC'], writes=['junkA'])
                        S.op('dve', lambda e: e.tensor_tensor(out=yab[:].rearrange("p (h j) -> p h j", j=64), in0=tA, in1=gt[:], op=ALU.mult), reads=['junkA', 'gt'], writes=['yab'])
                        for kc in range(4):
                            S.op('pe', lambda e, kc=kc: e.transpose(out=P['pst'][:, kc, :], in_=yab[:, kc * 128:(kc + 1) * 128], identity=ident[:]), reads=['yab', 'ident'], writes=['pst'])
                            S.op('pe', lambda e, kc=kc: e.transpose(out=P['pst'][:, 4 + kc, :], in_=ybt[:, kc * 128:(kc + 1) * 128], identity=ident[:]), reads=['ybt', 'ident'], writes=['pst'])
                        S.op('dve', lambda e, sub=sub: e.tensor_copy(out=yaT[:, :, sub * 128:(sub + 1) * 128], in_=P['pst'][:, 0:4, :]), reads=['pst'], writes=['actT'])
                        S.op('act', lambda e, sub=sub: e.activation(out=ybT[:, :, sub * 128:(sub + 1) * 128], in_=P['pst'][:, 4:8, :], func=AF.Copy), reads=['pst'], writes=['actT'])
                    for oc in range(8):
                        sl = oc % 3
                        S.dma(P['wg'][sl][:], WIN1[:, :, 2688 + oc * 128:2688 + (oc + 1) * 128].rearrange("k p c -> p k c"), writes=['wg%d' % sl])
                        S.dma(P['wu'][sl][:], WIN1[:, :, 2688 + 1024 + oc * 128:2688 + 1024 + (oc + 1) * 128].rearrange("k p c -> p k c"), writes=['wu%d' % sl])
                        ocs = slice(oc * 128, (oc + 1) * 128)
                        for kc in range(KC):
                            S.op('pe', lambda e, kc=kc, sl=sl: e.matmul(out=P['psg'][0][:], lhsT=P['wg'][sl][:, kc, :], rhs=h2T[:, kc, :], start=(kc == 0), stop=(kc == KC - 1)), reads=['wg%d' % sl, 'h2T'], writes=['psg0'])
                        for kc in range(KC):
                            S.op('pe', lambda e, kc=kc, sl=sl: e.matmul(out=P['psg'][1][:], lhsT=P['wu'][sl][:, kc, :], rhs=h2T[:, kc, :], start=(kc == 0), stop=(kc == KC - 1)), reads=['wu%d' % sl, 'h2T'], writes=['psg1'])
                        for kc in range(4):
                            S.op('pe', lambda e, kc=kc, ocs=ocs: e.matmul(out=P['psu'][0][:], lhsT=wba[:, kc, ocs], rhs=yaT[:, kc, :], start=(kc == 0), stop=(kc == 3)), reads=['wba', 'actT'], writes=['psu0'])
                        for kc in range(4):
                            S.op('pe', lambda e, kc=kc, ocs=ocs: e.matmul(out=P['psu'][1][:], lhsT=wbb[:, kc, ocs], rhs=ybT[:, kc, :], start=(kc == 0), stop=(kc == 3)), reads=['wbb', 'actT'], writes=['psu1'])
                        S.op('act', lambda e, oc=oc: e.activation(out=sga[0][:], in_=P['psg'][0][:], func=AF.Sigmoid, bias=bgT[:, oc:oc + 1]), reads=['psg0', 'bgT'], writes=['sga0'])
                        S.op('act', lambda e, oc=oc: e.activation(out=sga[1][:], in_=P['psg'][1][:], func=AF.Sigmoid, bias=bgT[:, 8 + oc:9 + oc]), reads=['psg1', 'bgT'], writes=['sga1'])
                        S.op('dve', lambda e: e.tensor_tensor(out=mtmp[0][:], in0=P['psu'][0][:], in1=sga[0][:], op=ALU.mult), reads=['psu0', 'sga0'], writes=['mtmp0'])
                        S.op('dve', lambda e: e.tensor_tensor(out=mtmp[1][:], in0=P['psu'][1][:], in1=sga[1][:], op=ALU.mult), reads=['psu1', 'sga1'], writes=['mtmp1'])
                        S.op('pool', lambda e, oc=oc: e.tensor_tensor(out=mT[:, oc, :], in0=mtmp[0][:], in1=mtmp[1][:], op=ALU.add), reads=['mtmp0', 'mtmp1'], writes=['actT'])
                    for sub in range(4):
                        for hf in range(2):
                            for kc in range(KC):
                                S.op('pe', lambda e, kc=kc, sub=sub, hf=hf: e.matmul(out=P['psd'][hf][:], lhsT=mT[:, kc, sub * 128:(sub + 1) * 128], rhs=wout[:, kc, hf * 512:(hf + 1) * 512], start=(kc == 0), stop=(kc == KC - 1)),
                                     reads=['actT', 'wout'], writes=['psd%d' % hf])
                            S.op('act', lambda e, hf=hf: e.activation(out=P['junk'][:, 0:512], in_=P['psd'][hf][:], func=AF.Square, accum_out=P['ssd'][:, hf:hf + 1]),
                                 reads=['psd%d' % hf], writes=['junk', 'junkA', 'ssd'])
                        post_norm_res(S, P, [P['psd'][0], P['psd'][1]], ['psd0', 'psd1'], gpost[1], xt, 'xt', xt, 'xt', sub)
                    for sub in range(4):
                        norm_T(S, P, xt, 'xt', sub, h2T, 'h2T')
                    ffn(S, P, 1, h2T, 'h2T', xt, 'xt', gpost[2], xt, 'xt')
                    S.dma(yout[ti * 512:(ti + 1) * 512, :].rearrange("(s p) d -> p s d", p=128), xt[:], reads=['xt'], writes=['yout'], queue='pool')
                    if ti % 4 == 3 and ti + 1 < NQ // 512:
                        S.flush()
                S.flush()
    return nc


_NC_CACHE = {}


def kernel(x_prompt, x_sample, norm_g, ffn_w_gate, ffn_w_up, ffn_w_down, w_in, mu_shift,
           w0, w_lora_up, a0, a_lora_up, g_lora_up, k_k, k_a, r_k, lnx_w, lnx_b,
           qk_norm_g, w_branch_a, w_branch_b, b_gate, w_out):
    f = lambda a: np.ascontiguousarray(np.asarray(a, dtype=np.float32))
    x_prompt = f(x_prompt)
    x_sample = f(x_sample)
    NP, TP, _ = x_prompt.shape
    NSMP, TS, _ = x_sample.shape
    n = 8
    per = NSMP // n
    seq_lens = [TP] + [TS] * per
    key = tuple(seq_lens)
    if key not in _NC_CACHE:
        _NC_CACHE[key] = build_nc(seq_lens, quarter=(0,))
    nc = _NC_CACHE[key]
    shared = {
        "norm_g": f(norm_g)[0], "ffn_w_gate": f(ffn_w_gate)[0], "ffn_w_up": f(ffn_w_up)[0],
        "ffn_w_down": f(ffn_w_down)[0], "w_in": f(w_in)[0], "mu_shift": f(mu_shift)[0],
        "w0": f(w0)[0], "w_lora_up": f(w_lora_up)[0], "a0": f(a0)[0], "a_lora_up": f(a_lora_up)[0],
        "g_lora_up": f(g_lora_up)[0], "k_k": f(k_k)[0], "k_a": f(k_a)[0], "r_k": f(r_k)[0].reshape(RW),
        "lnx_w": f(lnx_w)[0], "lnx_b": f(lnx_b)[0], "qk_norm_g": f(qk_norm_g)[0],
        "w_branch_a": f(w_branch_a)[0], "w_branch_b": f(w_branch_b)[0],
        "b_gate": f(b_gate)[0].reshape(2 * D), "w_out": f(w_out)[0],
    }
    for k_, v_ in _consts().items():
        shared["c_" + k_] = v_
    for T_ in sorted(set(seq_lens)):
        c_, s_ = _rope_tables(T_)
        shared["cos%d" % T_] = c_
        shared["sin%d" % T_] = s_
    in_maps = []
    cores_per_prompt = n // NP
    for c in range(n):
        m = dict(shared)
        m["x0"] = x_prompt[c // cores_per_prompt]
        for j in range(per):
            m["x%d" % (j + 1)] = x_sample[c * per + j]
        in_maps.append(m)
    res = run_bass_kernel_spmd(nc, in_maps, core_ids=list(range(n)))
    y_prompt = np.stack([np.concatenate([res.results[p * cores_per_prompt + q]["y0"] for q in range(cores_per_prompt)], axis=0)
                         for p in range(NP)], axis=0).astype(np.float32)
    y_sample = np.stack([res.results[c]["y%d" % (j + 1)] for c in range(n) for j in range(per)], axis=0).astype(np.float32)
    return (y_prompt, y_sample)
```
